# Optimizing a Trainium2 kernel written in Bass

```python
import math
import jax, jax.numpy as jnp
from jax import lax
import numpy as np

D_MODEL = 1024
BATCH = 16
SEQ = 2048
DEPTH = 1

M_HEADS = 4
M_HEAD_DIM = 128
M_WIDTH = M_HEADS * M_HEAD_DIM
M_CHUNK = 128
CONV_WIDTH = 4
N_HEADS = 8
N_KV_GROUPS = 2
N_REP = N_HEADS // N_KV_GROUPS
N_HEAD_DIM = 64
N_WIDTH = N_HEADS * N_HEAD_DIM
N_KV_WIDTH = N_KV_GROUPS * N_HEAD_DIM
CMP_BLOCK = 32
CMP_STRIDE = 16
CMP_HIDDEN = 2 * N_HEAD_DIM
SEL_BLOCK = 64
SEL_TOPK = 16
WINDOW = 512
NSA_Q_BLOCK = 64
REL_BUCKETS = 32
REL_MAX_DIST = 1024
N_BRANCH = 2
BRANCH_WIDTH = 512
D_FF = -(-8 * D_MODEL // (3 * 256)) * 256
RMS_EPS = 1e-6
BIG = 1e9

IN_SIZES = (M_WIDTH, M_WIDTH, M_WIDTH, M_WIDTH, M_HEADS, M_HEADS,
            N_WIDTH, 6 * N_KV_WIDTH, 3 * N_HEADS, N_BRANCH * D_MODEL)
D_IN_TOTAL = sum(IN_SIZES)

kernel_name = 'hybrid_mlstm_nsa_block'


def rms_norm(x, g):
    x32 = x.astype(jnp.float32)
    y = x32 * lax.rsqrt(jnp.mean(x32 * x32, axis=-1, keepdims=True) + RMS_EPS)
    return (y * g.astype(jnp.float32)).astype(x.dtype)


def split_columns(p, sizes):
    out = []
    start = 0
    for s in sizes:
        out.append(p[..., start:start + s])
        start += s
    return out


def causal_depthwise_conv(x, w):
    T = x.shape[1]
    xp = jnp.pad(x, ((0, 0), (CONV_WIDTH - 1, 0), (0, 0)))
    return sum(w[j] * xp[:, j:j + T] for j in range(CONV_WIDTH))


def rel_bucket(dist):
    n = jnp.maximum(dist, 0)
    max_exact = REL_BUCKETS // 2
    nf = jnp.maximum(n, 1).astype(jnp.float32)
    large = max_exact + (jnp.log(nf / max_exact) / math.log(REL_MAX_DIST / max_exact)
                         * (REL_BUCKETS - max_exact)).astype(jnp.int32)
    large = jnp.minimum(large, REL_BUCKETS - 1)
    return jnp.where(n < max_exact, n, large)


def masked_softmax(logits, mask):
    logits = logits.astype(jnp.float32)
    z = jnp.where(mask, logits, -1e30)
    z = z - jnp.max(z, axis=-1, keepdims=True)
    e = jnp.where(mask, jnp.exp(z), 0.0)
    return e / jnp.maximum(jnp.sum(e, axis=-1, keepdims=True), 1e-30)


def mlstm_chunkwise(q, k, v, i_pre, f_pre):
    B, H, T, dk = q.shape
    dv = v.shape[-1]
    L = M_CHUNK
    nc = T // L
    qc = q.reshape(B, H, nc, L, dk)
    kc = k.reshape(B, H, nc, L, dk)
    vc = v.reshape(B, H, nc, L, dv)
    ic = i_pre.reshape(B, H, nc, L)
    bc = jnp.cumsum(jax.nn.log_sigmoid(f_pre).reshape(B, H, nc, L), axis=-1)
    b_last = bc[..., -1]
    a = b_last[..., None] - bc + ic

    def step(carry, inp):
        C, n, m = carry
        k_j, v_j, a_j, bl = inp
        m_new = jnp.maximum(bl + m, jnp.max(a_j, axis=-1))
        decay = jnp.exp(bl + m - m_new)
        w = jnp.exp(a_j - m_new[..., None])
        C_new = decay[..., None, None] * C + jnp.einsum('bhl,bhlk,bhlv->bhkv', w, k_j, v_j)
        n_new = decay[..., None] * n + jnp.einsum('bhl,bhlk->bhk', w, k_j)
        return (C_new, n_new, m_new), (C, n, m)

    init = (jnp.zeros((B, H, dk, dv), jnp.float32), jnp.zeros((B, H, dk), jnp.float32),
            jnp.zeros((B, H), jnp.float32))
    xs = (jnp.moveaxis(kc, 2, 0), jnp.moveaxis(vc, 2, 0), jnp.moveaxis(a, 2, 0), jnp.moveaxis(b_last, 2, 0))
    _, (C_prev, n_prev, m_prev) = lax.scan(step, init, xs)
    C_prev = jnp.moveaxis(C_prev, 0, 2)
    n_prev = jnp.moveaxis(n_prev, 0, 2)
    m_prev = jnp.moveaxis(m_prev, 0, 2)

    causal = jnp.tril(jnp.ones((L, L), dtype=bool))
    log_d = jnp.where(causal, bc[..., :, None] - bc[..., None, :] + ic[..., None, :], -jnp.inf)
    log_inter = bc + m_prev[..., None]
    m_i = jnp.maximum(jnp.max(log_d, axis=-1), log_inter)
    P = jnp.exp(log_d - m_i[..., None]) * jnp.einsum('bhcid,bhcjd->bhcij', qc, kc)
    s_inter = jnp.exp(log_inter - m_i)
    num = (jnp.einsum('bhcij,bhcjv->bhciv', P, vc)
           + s_inter[..., None] * jnp.einsum('bhcid,bhcdv->bhciv', qc, C_prev))
    den = jnp.sum(P, axis=-1) + s_inter * jnp.einsum('bhcid,bhcd->bhci', qc, n_prev)
    h = num / jnp.maximum(jnp.abs(den), jnp.exp(-m_i))[..., None]
    return h.reshape(B, H, T, dv)


def mlstm_mixer(q_pre, k_pre, v, o_pre, i_pre, f_pre, b_fgate, conv_w, g_head):
    B, T, _ = q_pre.shape
    qk = jax.nn.silu(causal_depthwise_conv(jnp.concatenate([q_pre, k_pre], axis=-1), conv_w))
    to_heads = lambda t: t.reshape(B, T, M_HEADS, M_HEAD_DIM).transpose(0, 2, 1, 3).astype(jnp.float32)
    q = to_heads(qk[..., :M_WIDTH])
    k = to_heads(qk[..., M_WIDTH:]) * (M_HEAD_DIM ** -0.5)
    vh = to_heads(v)
    ig = i_pre.astype(jnp.float32).transpose(0, 2, 1)
    fg = (f_pre.astype(jnp.float32) + b_fgate.astype(jnp.float32)).transpose(0, 2, 1)
    hh = mlstm_chunkwise(q, k, vh, ig, fg)
    hh = (hh * lax.rsqrt(jnp.mean(hh * hh, axis=-1, keepdims=True) + RMS_EPS)
          * g_head.astype(jnp.float32).reshape(M_HEADS, 1, M_HEAD_DIM))
    hh = hh.transpose(0, 2, 1, 3).reshape(B, T, M_WIDTH)
    return (jax.nn.sigmoid(o_pre.astype(jnp.float32)) * hh).astype(q_pre.dtype)


def nsa_mixer(q_in, kv_in, gate_in, pe_cmp, w_cmp1, w_cmp2, rel_bias):
    B, T, _ = q_in.shape
    G, R, dh, Q = N_KV_GROUPS, N_REP, N_HEAD_DIM, NSA_Q_BLOCK
    q = q_in.reshape(B, T, G, R, dh).transpose(0, 2, 3, 1, 4) * (dh ** -0.5)
    kv = kv_in.reshape(B, T, 6, G, dh).transpose(2, 0, 3, 1, 4)
    k_c, v_c, k_s, v_s, k_w, v_w = kv[0], kv[1], kv[2], kv[3], kv[4], kv[5]
    gates = jax.nn.sigmoid(gate_in.astype(jnp.float32).reshape(B, T, G, R, 3).transpose(0, 2, 3, 1, 4))

    n_cmp = (T - CMP_BLOCK) // CMP_STRIDE + 1
    cmp_start = jnp.arange(n_cmp) * CMP_STRIDE
    cmp_idx = cmp_start[:, None] + jnp.arange(CMP_BLOCK)[None, :]
    cmp_end = cmp_start + CMP_BLOCK - 1

    def compress(kx, j):
        blk = kx[:, :, cmp_idx] + pe_cmp[j]
        flat = blk.reshape(B, G, n_cmp, CMP_BLOCK * dh)
        return jax.nn.silu(flat @ w_cmp1[j]) @ w_cmp2[j]

    kc_ = compress(k_c, 0)
    vc_ = compress(v_c, 1)

    n_slc = T // SEL_BLOCK
    n_top = min(SEL_TOPK, n_slc)
    k_sb = k_s.reshape(B, G, n_slc, SEL_BLOCK, dh)
    v_sb = v_s.reshape(B, G, n_slc, SEL_BLOCK, dh)
    slc_start = jnp.arange(n_slc) * SEL_BLOCK
    overlap = (jnp.clip(jnp.minimum(cmp_start[:, None] + CMP_BLOCK, slc_start[None, :] + SEL_BLOCK)
                        - jnp.maximum(cmp_start[:, None], slc_start[None, :]), 0)
               / CMP_STRIDE).astype(jnp.float32)

    k_wp = jnp.pad(k_w, ((0, 0), (0, 0), (WINDOW, 0), (0, 0)))
    v_wp = jnp.pad(v_w, ((0, 0), (0, 0), (WINDOW, 0), (0, 0)))

    table = rel_bias.astype(jnp.float32).T.reshape(G, R, REL_BUCKETS)
    g_ix = jnp.arange(G)[None, :, None, None, None]
    r_ix = jnp.arange(R)[None, None, :, None, None]
    gather_blocks = jax.vmap(jax.vmap(lambda blk, ix: blk[ix]))

    def block(i):
        qs = i * Q
        t = qs + jnp.arange(Q)
        qb = lax.dynamic_slice_in_dim(q, qs, Q, axis=3)
        lc = (jnp.einsum('bgrqd,bgcd->bgrqc', qb, kc_).astype(jnp.float32)
              + table[:, :, rel_bucket(t[:, None] - cmp_end[None, :])])
        p_c = masked_softmax(lc, cmp_end[None, :] <= t[:, None])
        o_c = jnp.einsum('bgrqc,bgcd->bgrqd', p_c, vc_.astype(jnp.float32))
        imp = jnp.einsum('bgrqc,cj->bgqj', p_c, overlap)
        cur = t // SEL_BLOCK
        jb = jnp.arange(n_slc)[None, :]
        forced = (jb == 0) | (jb == cur[:, None]) | (jb == cur[:, None] - 1)
        eligible = jb * SEL_BLOCK <= t[:, None]
        score = jnp.where(eligible, jnp.where(forced, BIG, imp), -BIG)
        top_val, top_idx = lax.top_k(score, n_top)
        N = n_top * SEL_BLOCK
        k_g = gather_blocks(k_sb, top_idx).reshape(B, G, Q, N, dh)
        v_g = gather_blocks(v_sb, top_idx).reshape(B, G, Q, N, dh)
        pos_u = top_idx[..., None] * SEL_BLOCK + jnp.arange(SEL_BLOCK)
        mask_u = (top_val > -BIG / 2)[..., None] & (pos_u <= t[:, None, None])
        pos = pos_u.reshape(B, G, Q, N)
        mask_s = mask_u.reshape(B, G, Q, N)
        bias_s = table[g_ix, r_ix, rel_bucket(t[:, None] - pos)[:, :, None]]
        ls = jnp.einsum('bgrqd,bgqnd->bgrqn', qb, k_g).astype(jnp.float32) + bias_s
        p_s = masked_softmax(ls, mask_s[:, :, None])
        o_s = jnp.einsum('bgrqn,bgqnd->bgrqd', p_s, v_g.astype(jnp.float32))
        span = Q + WINDOW
        k_wb = lax.dynamic_slice_in_dim(k_wp, qs, span, axis=2)
        v_wb = lax.dynamic_slice_in_dim(v_wp, qs, span, axis=2)
        pos_w = qs - WINDOW + jnp.arange(span)
        dist = t[:, None] - pos_w[None, :]
        mask_w = (pos_w[None, :] >= 0) & (dist >= 0) & (dist < WINDOW)
        lw = jnp.einsum('bgrqd,bgkd->bgrqk', qb, k_wb).astype(jnp.float32) + table[:, :, rel_bucket(dist)]
        p_w = masked_softmax(lw, mask_w)
        o_w = jnp.einsum('bgrqk,bgkd->bgrqd', p_w, v_wb.astype(jnp.float32))
        gb = lax.dynamic_slice_in_dim(gates, qs, Q, axis=3)
        return gb[..., 0:1] * o_c + gb[..., 1:2] * o_s + gb[..., 2:3] * o_w

    out = lax.map(block, jnp.arange(T // Q))
    out = out.transpose(1, 0, 4, 2, 3, 5).reshape(B, T, N_WIDTH)
    return out.astype(q_in.dtype)


def setup_inputs(seed: int = 0) -> dict:
    key = jax.random.key(seed)
    ks = jax.random.split(key, 20)
    nrm = lambda k, shape, scale: jax.random.normal(k, shape, jnp.float32) * scale
    x = nrm(ks[0], (BATCH, SEQ, D_MODEL), 1.0)
    g_norm_mix = 1.0 + nrm(ks[1], (DEPTH, D_MODEL), 0.02)
    w_in = nrm(ks[2], (DEPTH, D_MODEL, D_IN_TOTAL), D_MODEL ** -0.5)
    b_in = nrm(ks[3], (DEPTH, D_IN_TOTAL), 0.02)
    b_fgate = jnp.linspace(3.0, 6.0, M_HEADS, dtype=jnp.float32)[None, :] + nrm(ks[4], (DEPTH, M_HEADS), 0.1)
    conv_qk = nrm(ks[5], (DEPTH, CONV_WIDTH, 2 * M_WIDTH), CONV_WIDTH ** -0.5)
    g_mlstm_head = 1.0 + nrm(ks[6], (DEPTH, M_WIDTH), 0.02)
    pe_cmp = nrm(ks[7], (DEPTH, 2, CMP_BLOCK, N_HEAD_DIM), 0.1)
    w_cmp1 = nrm(ks[8], (DEPTH, 2, CMP_BLOCK * N_HEAD_DIM, CMP_HIDDEN), (CMP_BLOCK * N_HEAD_DIM) ** -0.5)
    w_cmp2 = nrm(ks[9], (DEPTH, 2, CMP_HIDDEN, N_HEAD_DIM), CMP_HIDDEN ** -0.5)
    rel_bias = nrm(ks[10], (REL_BUCKETS, N_HEADS), 0.5)
    w_branch = nrm(ks[11], (DEPTH, N_BRANCH, BRANCH_WIDTH, D_MODEL), BRANCH_WIDTH ** -0.5)
    w_out = nrm(ks[12], (DEPTH, D_MODEL, D_MODEL), D_MODEL ** -0.5)
    g_norm_ffn = 1.0 + nrm(ks[13], (DEPTH, D_MODEL), 0.02)
    w_gate = nrm(ks[14], (DEPTH, D_MODEL, D_FF), D_MODEL ** -0.5)
    w_up = nrm(ks[15], (DEPTH, D_MODEL, D_FF), D_MODEL ** -0.5)
    w_down = nrm(ks[16], (DEPTH, D_FF, D_MODEL), D_FF ** -0.5)
    g_final = 1.0 + nrm(ks[17], (D_MODEL,), 0.02)
    return {'x': x, 'g_norm_mix': g_norm_mix, 'w_in': w_in, 'b_in': b_in, 'b_fgate': b_fgate,
            'conv_qk': conv_qk, 'g_mlstm_head': g_mlstm_head, 'pe_cmp': pe_cmp, 'w_cmp1': w_cmp1,
            'w_cmp2': w_cmp2, 'rel_bias': rel_bias, 'w_branch': w_branch, 'w_out': w_out,
            'g_norm_ffn': g_norm_ffn, 'w_gate': w_gate, 'w_up': w_up, 'w_down': w_down, 'g_final': g_final}


def reference(x, g_norm_mix, w_in, b_in, b_fgate, conv_qk, g_mlstm_head, pe_cmp, w_cmp1, w_cmp2,
              rel_bias, w_branch, w_out, g_norm_ffn, w_gate, w_up, w_down, g_final):
    B, T, _ = x.shape
    h = x
    for l in range(DEPTH):
        u = rms_norm(h, g_norm_mix[l])
        proj = jnp.einsum('btd,de->bte', u, w_in[l]) + b_in[l]
        mq, mk, mv, mo, mi, mf, nq, nkv, ngate, merge = split_columns(proj, IN_SIZES)
        y_m = mlstm_mixer(mq, mk, mv, mo, mi, mf, b_fgate[l], conv_qk[l], g_mlstm_head[l])
        y_n = nsa_mixer(nq, nkv, ngate, pe_cmp[l], w_cmp1[l], w_cmp2[l], rel_bias)
        ys = jnp.stack([y_m, y_n], axis=2)
        branch = jnp.einsum('btnw,nwd->btnd', ys, w_branch[l])
        gates = jax.nn.sigmoid(merge.reshape(B, T, N_BRANCH, D_MODEL))
        mixed = jnp.sum(gates * branch, axis=2)
        h = h + mixed @ w_out[l]
        f = rms_norm(h, g_norm_ffn[l])
        h = h + (jax.nn.silu(f @ w_gate[l]) * (f @ w_up[l])) @ w_down[l]
    return rms_norm(h, g_final)
```

```python
import math
from contextlib import ExitStack
import numpy as np
import concourse.bass as bass
import concourse.mybir as mybir
from concourse.bass_utils import run_bass_kernel_spmd

F32 = mybir.dt.float32
BF16 = mybir.dt.bfloat16
AF = mybir.ActivationFunctionType
ALU = mybir.AluOpType
AX = mybir.AxisListType

T = 2048
D = 1024
NT = 16
DFF = 2816
NEG = -30000.0
LN_C = math.log(128 ** -0.5)

_DTSZ = {"dt.float32": 4, "dt.bfloat16": 2, "dt.int32": 4, "dt.uint32": 4, "dt.float16": 2,
         "dt.uint16": 2, "dt.int16": 2, "dt.uint8": 1, "dt.int8": 1}


def _box(ap):
    t = ap.tensor
    name = t.name
    tn = type(t).__name__
    if tn == "PSumTensorHandle":
        return (name, 0, 128, 0, 2048)
    a = ap.ap
    off = int(ap.offset)
    isz = _DTSZ[str(ap.dtype)]
    if tn == "DRamTensorHandle":
        ext = 1
        for st, cnt in a:
            ext += (cnt - 1) * abs(st)
        return (name, 0, 1, off * isz, (off + ext) * isz)
    pstep, pcnt = a[0]
    if pstep == 0:
        pstep = 1 << 40
    p0 = off // pstep
    f0 = off % pstep
    ext = 1
    for st, cnt in a[1:]:
        ext += (cnt - 1) * abs(st)
    return (name, p0, p0 + pcnt, f0 * isz, (f0 + ext) * isz)


def _ovl(a, b):
    return a[1] < b[2] and b[1] < a[2] and a[3] < b[4] and b[3] < a[4]


def _contains(a, b):
    return a[1] <= b[1] and b[2] <= a[2] and a[3] <= b[3] and b[4] <= a[4]


class Op:
    __slots__ = ("eng", "fn", "rb", "wb", "dma", "deps", "sig", "clock", "idx", "waits")


class Sched:
    ENGS = ("pe", "act", "dve", "pool", "sp")

    def __init__(self, nc, n_dma_sems=32):
        self.nc = nc
        self.ops = []
        self.n_dma_sems = n_dma_sems
        self.out_ops = []

    def op(self, eng, fn, reads=(), writes=(), dma=False, is_out=False):
        o = Op()
        o.eng = eng
        o.fn = fn
        o.rb = [_box(r) for r in reads if r is not None and not isinstance(r, (int, float))]
        o.wb = [_box(w) for w in writes]
        o.dma = dma
        o.idx = len(self.ops)
        self.ops.append(o)
        if is_out:
            self.out_ops.append(o)
        return o

    def analyze(self, skip=()):
        hist = {}
        dma_ops = []
        ops = self.ops
        self.dead = []
        for o in ops:
            deps = set()
            for r in o.rb:
                if r[0] in skip:
                    continue
                psum = r[0].startswith("ps")
                for rec in hist.get(r[0], ()):
                    if _ovl(rec[0], r) and (rec[2] or (psum and rec[1].eng != o.eng)):
                        deps.add(rec[1].idx)
                        if rec[2]:
                            rec[3] += 1
            for w in o.wb:
                if w[0] in skip:
                    continue
                for rec in hist.get(w[0], ()):
                    if _ovl(rec[0], w):
                        deps.add(rec[1].idx)
            for w in o.wb:
                if w[0] in skip:
                    continue
                lst = hist.setdefault(w[0], [])
                keep = []
                for rec in lst:
                    if _contains(w, rec[0]):
                        if rec[2] and rec[3] == 0 and not (rec[1].eng == "pe" and o.eng == "pe" and not o.dma):
                            self.dead.append((w[0], rec[1].idx, o.idx))
                    else:
                        keep.append(rec)
                lst[:] = keep
                lst.append([w, o, True, 0])
            for r in o.rb:
                if r[0] in skip:
                    continue
                lst = hist.setdefault(r[0], [])
                if not o.dma:
                    lst[:] = [rec for rec in lst if not ((not rec[2]) and rec[0] == r
                                                         and rec[1].eng == o.eng and not rec[1].dma)]
                lst.append([r, o, False, 0])
            if o.dma:
                k = len(dma_ops)
                if k >= self.n_dma_sems:
                    deps.add(dma_ops[k - self.n_dma_sems].idx)
                dma_ops.append(o)
            deps.discard(o.idx)
            if o.eng == "pe" and not o.dma:
                deps = {d for d in deps if ops[d].eng != "pe" or ops[d].dma}
            o.deps = sorted(deps)
        need = set()
        for o in ops:
            need.update(o.deps)
        for o in self.out_ops:
            need.add(o.idx)
        cnt = {e: 0 for e in self.ENGS}
        ndma = 0
        for o in ops:
            o.sig = None
            if o.dma:
                s = ndma % self.n_dma_sems
                o.sig = (("dma", s), 16 * (ndma // self.n_dma_sems + 1))
                ndma += 1
            elif o.idx in need:
                cnt[o.eng] += 1
                o.sig = ((o.eng,), cnt[o.eng])
        seen = {e: {} for e in self.ENGS}
        nw = 0
        for o in ops:
            sn = seen[o.eng]
            wm = {}
            for d in o.deps:
                dop = ops[d]
                s, v = dop.sig
                if sn.get(s, 0) >= v:
                    continue
                if wm.get(s, 0) < v:
                    wm[s] = v
                for k2, v2 in dop.clock.items():
                    if sn.get(k2, 0) < v2:
                        sn[k2] = v2
            o.waits = list(wm.items())
            nw += len(o.waits)
            if o.sig is not None:
                c = dict(sn)
                c[o.sig[0]] = o.sig[1]
                o.clock = c
            else:
                o.clock = None
        self.stats = dict(n_ops=len(ops), n_waits=nw, sig=dict(cnt), n_dma=ndma, dead=self.dead[:8], n_dead=len(self.dead))
        return self.stats

    def emit(self):
        nc = self.nc
        with ExitStack() as es:
            sems = {}
            for e in self.ENGS:
                sems[(e,)] = es.enter_context(nc.semaphore("s_" + e))
            for i in range(self.n_dma_sems):
                sems[("dma", i)] = es.enter_context(nc.semaphore("s_dma%d" % i))
            block = es.enter_context(nc.Block())
            per = {e: [o for o in self.ops if o.eng == e] for e in self.ENGS}
            finals = [o.sig for o in self.out_ops]

            def run(engh, lst, final=False):
                for o in lst:
                    for s, v in o.waits:
                        engh.wait_ge(sems[s], v)
                    if o.fn is None:
                        continue
                    ins = o.fn(engh)
                    if o.sig is not None:
                        ins.then_inc(sems[o.sig[0]], 16 if o.dma else 1)
                if final:
                    fm = {}
                    for s, v in finals:
                        if fm.get(s, 0) < v:
                            fm[s] = v
                    for s, v in fm.items():
                        engh.wait_ge(sems[s], v)

            @block.tensor
            def _(e):
                run(e, per["pe"])

            @block.scalar
            def _(e):
                run(e, per["act"])

            @block.vector
            def _(e):
                run(e, per["dve"])

            @block.gpsimd
            def _(e):
                run(e, per["pool"])

            @block.sync
            def _(e):
                run(e, per["sp"], final=True)


def _rel_bucket_np():
    n = np.arange(0, T + 1)
    nf = np.maximum(n, 1).astype(np.float32)
    large = 16 + (np.log(nf / np.float32(16)) / np.float32(math.log(1024 / 16)) * np.float32(16)).astype(np.int32)
    large = np.minimum(large, 31)
    return np.where(n < 16, n, large).astype(np.int64)


_OFF = {}
_o = 0
for _n, _s in (("mq", 512), ("mk", 512), ("mv", 512), ("mo", 512), ("mi", 4), ("mf", 4),
               ("nq", 512), ("nkv", 768), ("ngate", 24), ("merge", 2048)):
    _OFF[_n] = _o
    _o += _s


def prep_shared(inp):
    w_in = np.asarray(inp["w_in"][0], np.float32)
    b_in = np.asarray(inp["b_in"][0], np.float32)
    cols = []
    cols += list(range(_OFF["mq"], _OFF["mq"] + 512))
    cols += list(range(_OFF["mk"], _OFF["mk"] + 512))
    for j in range(4):
        for g in range(2):
            h = g * 4 + j
            cols += list(range(_OFF["nq"] + h * 64, _OFF["nq"] + h * 64 + 64))
    for s in (0, 1, 2, 4):
        cols += list(range(_OFF["nkv"] + s * 128, _OFF["nkv"] + s * 128 + 128))
    cols = np.array(cols)
    tm1 = np.array(list(range(_OFF["mv"], _OFF["mv"] + 512)) + list(range(_OFF["mo"], _OFF["mo"] + 512))
                   + list(range(_OFF["mi"], _OFF["mi"] + 4)) + list(range(_OFF["mf"], _OFF["mf"] + 4)))
    tm2 = np.array(list(range(_OFF["nkv"] + 3 * 128, _OFF["nkv"] + 4 * 128))
                   + list(range(_OFF["nkv"] + 5 * 128, _OFF["nkv"] + 6 * 128))
                   + list(range(_OFF["ngate"], _OFF["ngate"] + 24)))
    mg = np.arange(_OFF["merge"], _OFF["merge"] + 2048)
    sh = {}
    def chunked(w, nch):
        return np.ascontiguousarray(w.reshape(8, 128, nch, 128).transpose(2, 1, 0, 3).reshape(nch, 128, 1024))
    sh["wfm"] = chunked(w_in[:, cols], 16)
    sh["wtm1"] = np.ascontiguousarray(w_in[:, tm1])
    sh["wtm2"] = np.ascontiguousarray(w_in[:, tm2])
    sh["wmg"] = chunked(w_in[:, mg], 16)
    sh["bfm"] = np.ascontiguousarray(b_in[cols].reshape(16, 128).T)
    sh["btm1"] = np.ascontiguousarray(b_in[tm1][None, :])
    sh["btm2"] = np.ascontiguousarray(b_in[tm2][None, :])
    sh["bmg"] = np.ascontiguousarray(b_in[mg].reshape(16, 128).T)
    bc = lambda v: np.ascontiguousarray(np.broadcast_to(np.asarray(v, np.float32)[None, :], (128, len(v))))
    sh["gmix"] = bc(inp["g_norm_mix"][0])
    sh["gffn"] = bc(inp["g_norm_ffn"][0])
    sh["gfin"] = bc(inp["g_final"])
    sh["bfg"] = bc(inp["b_fgate"][0])
    sh["ghead"] = bc(inp["g_mlstm_head"][0])
    cw = np.asarray(inp["conv_qk"][0], np.float32)
    sh["convw"] = np.ascontiguousarray(cw.T.reshape(8, 128, 4).transpose(1, 0, 2))
    sh["ident"] = np.eye(128, dtype=np.float32)
    jj, ii = np.meshgrid(np.arange(128), np.arange(128), indexing="ij")
    sh["negtri"] = np.where(jj <= ii, -1.0, 0.0).astype(np.float32)
    sh["negones"] = -np.ones((128, 128), np.float32)
    sh["maskT"] = np.where(ii >= jj, 1.0, 0.0).astype(np.float32)
    pe = np.asarray(inp["pe_cmp"][0], np.float32)
    peT = pe.transpose(2, 0, 1)
    sh["peT"] = np.ascontiguousarray(np.concatenate([peT, peT], 0))
    w1 = np.asarray(inp["w_cmp1"][0], np.float32).reshape(2, 32, 64, 128).transpose(2, 0, 1, 3)
    sh["w1"] = np.ascontiguousarray(np.concatenate([w1, w1], 0))
    w2 = np.asarray(inp["w_cmp2"][0], np.float32).transpose(1, 0, 2)
    sh["w2"] = np.ascontiguousarray(np.concatenate([w2, w2], 2))
    cs = np.arange(127) * 16
    ss = np.arange(32) * 64
    ov = np.clip(np.minimum(cs[:, None] + 32, ss[None, :] + 64) - np.maximum(cs[:, None], ss[None, :]), 0, None) / 16.0
    ovp = np.zeros((128, 32), np.float32)
    ovp[:127] = ov
    sh["overlap"] = ovp
    rb = np.asarray(inp["rel_bias"], np.float32)
    bucket = _rel_bucket_np()
    tbl = rb.T
    dist = np.arange(T)[None, :] - np.arange(128)[:, None]
    BS = np.full((8, 128, T), NEG, np.float32)
    ok = dist >= 0
    BS[:, ok] = tbl[:, bucket[dist[ok]]]
    sh["BS"] = BS
    d640 = dist[:, :640]
    BW = np.full((8, 128, 640), NEG, np.float32)
    okw = (d640 >= 0) & (d640 < 512)
    BW[:, okw] = tbl[:, bucket[d640[okw]]]
    sh["BW"] = BW
    dc = np.arange(T)[None, :] - (np.arange(128) * 16 + 31)[:, None]
    BC = np.full((8, 128, T), NEG, np.float32)
    okc = dc >= 0
    okc[127, :] = False
    BC[:, okc] = tbl[:, bucket[dc[okc]]]
    sh["BC"] = BC
    ex = np.zeros((32, T), np.float32)
    ex[np.arange(T) // 64, np.arange(T)] = 1.0
    sh["expand"] = ex
    cand = np.zeros((8, 128, 32), np.float32)
    forced = np.zeros((8, 128, 32), np.float32)
    for qi in range(8):
        t = (8 + qi) * 128 + np.arange(128)
        cur = t // 64
        jb = np.arange(32)[None, :]
        f = (jb == 0) | (jb == cur[:, None]) | (jb == cur[:, None] - 1)
        el = jb * 64 <= t[:, None]
        forced[qi] = f
        cand[qi] = el & ~f
    sh["cand"] = np.ascontiguousarray(cand.transpose(1, 0, 2))
    sh["negm"] = np.ascontiguousarray((cand - 1.0).transpose(1, 0, 2))
    sh["forced"] = np.ascontiguousarray(forced.transpose(1, 0, 2))
    wbr = np.asarray(inp["w_branch"][0], np.float32).reshape(2, 4, 128, 8, 128)
    sh["wbr"] = np.ascontiguousarray(wbr.transpose(0, 3, 2, 1, 4).reshape(16, 128, 512))
    sh["wout"] = np.ascontiguousarray(np.asarray(inp["w_out"][0], np.float32).reshape(8, 128, 1024).transpose(1, 0, 2).reshape(128, 8192))
    sh["wgate"] = chunked(np.asarray(inp["w_gate"][0], np.float32), 22)
    sh["wup"] = chunked(np.asarray(inp["w_up"][0], np.float32), 22)
    sh["wdown"] = np.ascontiguousarray(np.asarray(inp["w_down"][0], np.float32).reshape(22, 128, 2, 512).transpose(2, 1, 0, 3).reshape(2 * 128 * 22, 512))
    return sh


ARENA_BYTES = 124 * 1024
NQ_OFF = 3 * 2048 + 4480
import os as _os
B2STOP = int(_os.environ.get('B2STOP', '0'))
B2SUB = int(_os.environ.get('B2SUB', '0'))
B2TM = int(_os.environ.get('B2TM', '0'))


class Builder:
    def __init__(self, NS, dbg=None, stop_after=None):
        self.NS = NS
        self.dbg = dbg or ()
        self.stop_after = stop_after
        self.nc = bass.Bass("TRN2", target_bir_lowering=False)
        self.S = Sched(self.nc)
        self.es = ExitStack()
        self.dr = {}
        self.rr = 0
        self.rr2 = 0
        self.nrot = 6

    def din(self, name, shape, dt=F32):
        self.dr[name] = self.nc.dram_tensor(name, list(shape), dt, kind="ExternalInput").ap()
        return self.dr[name]

    def dout(self, name, shape, dt=F32):
        self.dr[name] = self.nc.dram_tensor(name, list(shape), dt, kind="ExternalOutput").ap()
        return self.dr[name]

    def sb(self, name, shape, dt):
        return self.es.enter_context(self.nc.sbuf_tensor(name, list(shape), dt))

    def arena_reset(self):
        if getattr(self, "aoff", 0):
            print("arena used", self.aoff, flush=True)
        self.aoff = 0
        self.aviews = {}

    def al(self, name, shape, dt, alias=None):
        isz = 4 if dt == F32 else 2
        n = 1
        for d in shape[1:]:
            n *= d
        nbytes = ((n * isz + 63) // 64) * 64
        if alias is not None:
            off = self.aviews[alias][1]
        else:
            off = self.aoff
            self.aoff += nbytes
            assert self.aoff <= ARENA_BYTES, (name, self.aoff)
        v = self.arena[0:shape[0], off // 2:off // 2 + (n * isz) // 2]
        if dt == F32:
            v = v.bitcast(F32)
        if len(shape) == 3:
            v = v.rearrange("p (a b) -> p a b", a=shape[1])
        elif len(shape) == 4:
            v = v.rearrange("p (a b c) -> p a b c", a=shape[1], b=shape[2])
        self.aviews[name] = (v, off, nbytes)
        return v

    def view_at(self, off, shape, dt):
        isz = 4 if dt == F32 else 2
        n = 1
        for d in shape[1:]:
            n *= d
        v = self.arena[0:shape[0], off // 2:off // 2 + (n * isz) // 2]
        if dt == F32:
            v = v.bitcast(F32)
        if len(shape) == 3:
            v = v.rearrange("p (a b) -> p a b", a=shape[1])
        return v

    def dma(self, q, out, in_, is_out=False):
        isd = lambda a: type(a.tensor).__name__ == "DRamTensorHandle" and not a.tensor.name.startswith("scr_")
        rd = [] if isd(in_) else [in_]
        wr = [] if isd(out) else [out]
        return self.S.op(q, lambda e: e.dma_start(out=out, in_=in_), reads=rd, writes=wr, dma=True, is_out=is_out)

    def mm(self, out, lhsT, rhs, start=True, stop=True):
        self.S.op("pe", lambda e: e.matmul(out, lhsT=lhsT, rhs=rhs, start=start, stop=stop),
                  reads=[lhsT, rhs], writes=[out])

    def tr(self, out, in_, f32=False):
        n = in_.shape[0]
        idn = (self.identf if f32 else self.identb)[0:n, 0:n]
        self.S.op("pe", lambda e: e.transpose(out=out, in_=in_, identity=idn), reads=[in_, idn], writes=[out])

    def act(self, out, in_, func, bias=None, scale=None, accum=None):
        kw = {}
        rd = [in_]
        wr = [out]
        if bias is not None:
            kw["bias"] = bias
            rd.append(bias)
        if scale is not None:
            kw["scale"] = scale
            rd.append(scale)
        if accum is not None:
            kw["accum_out"] = accum
            wr.append(accum)
        self.S.op("act", lambda e: e.activation(out=out, in_=in_, func=func, **kw), reads=rd, writes=wr)

    def tt(self, eng, out, in0, in1, op):
        self.S.op(eng, lambda e: e.tensor_tensor(out=out, in0=in0, in1=in1, op=op), reads=[in0, in1], writes=[out])

    def ts(self, eng, out, in0, s1, op0, s2=None, op1=None):
        if op1 is None:
            self.S.op(eng, lambda e: e.tensor_scalar(out=out, in0=in0, scalar1=s1, scalar2=None, op0=op0),
                      reads=[in0, s1], writes=[out])
        else:
            self.S.op(eng, lambda e: e.tensor_scalar(out=out, in0=in0, scalar1=s1, scalar2=s2, op0=op0, op1=op1),
                      reads=[in0, s1, s2], writes=[out])

    def stt(self, eng, out, in0, scalar, in1, op0, op1):
        self.S.op(eng, lambda e: e.scalar_tensor_tensor(out=out, in0=in0, scalar=scalar, in1=in1, op0=op0, op1=op1),
                  reads=[in0, scalar, in1], writes=[out])

    def copy(self, eng, out, in_):
        if eng == "act":
            self.S.op("act", lambda e: e.copy(out=out, in_=in_), reads=[in_], writes=[out])
        else:
            self.S.op(eng, lambda e: e.tensor_copy(out=out, in_=in_), reads=[in_], writes=[out])

    def recip(self, out, in_):
        self.S.op("dve", lambda e: e.reciprocal(out=out, in_=in_), reads=[in_], writes=[out])

    def memset(self, eng, ap, val):
        self.S.op(eng, lambda e: e.memset(ap, val), writes=[ap])

    def bank(self):
        b = self.ps[self.rr % self.nrot]
        self.rr += 1
        return b

    def bank_bf(self):
        return self.bank()[:].bitcast(BF16)

    def accbank(self):
        b = self.ps[6 + self.rr2 % 2]
        self.rr2 += 1
        return b

    def build(self):
        nc, S, NS = self.nc, self.S, self.NS
        din, sb = self.din, self.sb
        din("x", [NS * T, D])
        for n, shp in (("gmix", [128, D]), ("gffn", [128, D]), ("gfin", [128, D]), ("wfm", [16, 128, 1024]),
                       ("wtm1", [D, 1032]), ("wtm2", [D, 280]), ("wmg", [16, 128, 1024]), ("bfm", [128, 16]),
                       ("btm1", [1, 1032]), ("btm2", [1, 280]), ("bmg", [128, 16]), ("convw", [128, 8, 4]),
                       ("bfg", [128, 4]), ("ghead", [128, 512]), ("ident", [128, 128]), ("negtri", [128, 128]),
                       ("negones", [128, 128]), ("maskT", [128, 128]), ("peT", [128, 2, 32]),
                       ("w1", [128, 2, 32, 128]), ("w2", [128, 2, 128]), ("overlap", [128, 32]),
                       ("BS", [8, 128, T]), ("BW", [8, 128, 640]), ("BC", [8, 128, T]), ("expand", [32, T]),
                       ("cand", [128, 8, 32]), ("negm", [128, 8, 32]), ("forced", [128, 8, 32]),
                       ("wbr", [16, 128, 512]), ("wout", [128, 8192]), ("wgate", [22, 128, 1024]), ("wup", [22, 128, 1024]),
                       ("wdown", [2 * 128 * 22, 512])):
            din(n, shp)
        dr = self.dr
        out = self.dout("out", [NS * T, D])
        dbg_t = {}
        for n in self.dbg:
            shp = {"ymT": [NS, 128, 4, T], "ynT": [NS, 128, 4, T], "qk": [NS, 128, 8, T], "uT": [NS, 128, 8, T]}[n]
            dbg_t[n] = self.dout("dbg_" + n, shp, BF16)
        self.dbg_t = dbg_t

        self.ps = [self.es.enter_context(nc.psum_tensor("ps%d" % i, [128, 512], F32)) for i in range(8)]

        self.identb = sb("identb", [128, 128], BF16)
        self.dma("pool", self.identb[:], dr["ident"][:, :])
        cst = {}
        late = []
        for n, shp in (("gmix", [128, D]), ("gffn", [128, D]), ("gfin", [128, D]), ("bfm", [128, 16]),
                       ("bmg", [128, 16]), ("convw", [128, 8, 4]), ("bfg", [128, 4]), ("ghead", [128, 512]),
                       ("negtri", [128, 128]), ("negones", [128, 128]), ("maskT", [128, 128]),
                       ("ident", [128, 128]), ("overlap", [128, 32])):
            cst[n] = sb("c_" + n, shp, F32)
            if n in ("gmix", "bfm"):
                self.dma("sp", cst[n][:], dr[n])
            else:
                late.append(n)
        self.late_consts = late
        self.identf = cst["ident"]
        cst["bfm8"] = sb("c_bfm8", [128, 16], F32)
        self.ts("dve", cst["bfm8"][:], cst["bfm"][:], 0.125, ALU.mult)
        self.cst = cst
        ones_row = sb("ones_row", [1, 128], BF16)
        self.memset("dve", ones_row[:], 1.0)
        self.ones_row = ones_row
        self.btm1 = sb("s_btm1", [1, 1032], BF16)
        self.dma("pool", self.btm1[:], dr["btm1"][:, :])
        self.btm2 = sb("s_btm2", [1, 280], BF16)
        self.dma("pool", self.btm2[:], dr["btm2"][:, :])
        self.st = sb("stats", [128, 64], F32)

        self.uT = sb("s_uT", [128, 8, T], BF16)
        self.ymT = sb("s_ymT", [128, 4, T], BF16)
        self.ynT = sb("s_ynT", [128, 4, T], BF16)
        self.arena = sb("arena", [128, ARENA_BYTES // 2], BF16)

        self.scr = {}
        for n, shp in (("wmg", [16, 128, 1024]), ("wbr", [16, 128, 512]), ("wout", [128, 8192]),
                       ("wgate", [22, 128, 1024]), ("wup", [22, 128, 1024]), ("wdown", [2 * 128 * 22, 512])):
            self.scr[n] = nc.dram_tensor("scr_" + n, shp, BF16, kind="Internal").ap()
        stages = ("a", "b1", "b2", "c")
        last = stages.index(self.stop_after) if self.stop_after else 3
        for s in range(NS):
            self.phase_a(s)
            if "uT" in dbg_t:
                self.dma("sp", dbg_t["uT"][s], self.uT[:], is_out=True)
            if last >= 1:
                self.phase_b1(s)
                if "ymT" in dbg_t:
                    self.dma("sp", dbg_t["ymT"][s], self.ymT[:], is_out=True)
            if last >= 2:
                self.phase_b2(s)
                if "ynT" in dbg_t:
                    self.dma("sp", dbg_t["ynT"][s], self.ynT[:], is_out=True)
            if last >= 3:
                self.phase_c(s)

        if last < 3:
            self.arena_reset()
            zt = self.al("zt", [128, D], F32)
            self.memset("dve", zt[:], 0.0)
            for i in range(NS * NT):
                self.dma("sp", out[i * 128:(i + 1) * 128, :], zt[:], is_out=True)
        print("arena used (last phase)", self.aoff, "sbuf remaining", nc.sbuf_bytes_remaining, flush=True)
        print(S.analyze(), flush=True)
        S.emit()
        self.es.close()
        return nc

    def prologue_list(self):
        dr, scr = self.dr, self.scr
        lst = []
        for c in range(0, 16, 2):
            lst.append((scr["wmg"][c:c + 2].rearrange("c p n -> (c p) n"), dr["wmg"][c:c + 2].rearrange("c p n -> (c p) n")))
        for c in range(0, 16, 4):
            lst.append((scr["wbr"][c:c + 4].rearrange("c p n -> (c p) n"), dr["wbr"][c:c + 4].rearrange("c p n -> (c p) n")))
        for k in range(0, 8, 2):
            lst.append((scr["wout"][:, k * 1024:(k + 2) * 1024].rearrange("p (k n) -> p k n", k=2),
                        dr["wout"][:, k * 1024:(k + 2) * 1024].rearrange("p (k n) -> p k n", k=2)))
        for c in range(0, 22, 2):
            lst.append((scr["wgate"][c:c + 2].rearrange("c p n -> (c p) n"), dr["wgate"][c:c + 2].rearrange("c p n -> (c p) n")))
            lst.append((scr["wup"][c:c + 2].rearrange("c p n -> (c p) n"), dr["wup"][c:c + 2].rearrange("c p n -> (c p) n")))
        for i in range(0, 5632, 704):
            lst.append((scr["wdown"][i:i + 704, :], dr["wdown"][i:i + 704, :]))
        return lst

    def prologue_some(self, n):
        for _ in range(n):
            if self.pl:
                o, i = self.pl.pop(0)
                self.dma("pool", o, i)

    def phase_a(self, s):
        self.arena_reset()
        x = self.dr["x"]
        xbuf = [self.al("xbuf%d" % i, [128, D], F32) for i in range(3)]
        xnb = [self.al("xnb%d" % i, [128, D], BF16) for i in range(2)]
        junkb = self.al("junkb", [128, D], BF16)
        st = self.st
        def stA(i):
            xt = xbuf[i % 3]
            xn = xnb[i % 2]
            r0 = s * T + i * 128
            self.dma("sp", xt[:], x[r0:r0 + 128, :])
            c = i % 2
            self.act(junkb[:], xt[:], AF.Square, accum=st[:, c:c + 1])
            self.act(st[:, 2 + c:3 + c], st[:, c:c + 1], AF.Sqrt, bias=1e-6, scale=1.0 / D)
            self.recip(st[:, 4 + c:5 + c], st[:, 2 + c:3 + c])
            self.stt("dve", xn[:], xt[:], st[:, 4 + c:5 + c], self.cst["gmix"][:], ALU.mult, ALU.mult)

        def stB(i):
            xn = xnb[i % 2]
            pb = self.bank_bf()
            for k in range(8):
                self.tr(pb[:, k * 128:(k + 1) * 128], xn[:, k * 128:(k + 1) * 128])
            self.copy("act" if i % 2 else "dve", self.uT[:, :, i * 128:(i + 1) * 128],
                      pb.rearrange("p (k n) -> p k n", k=8))

        stA(0)
        for i in range(NT):
            if i + 1 < NT:
                stA(i + 1)
            if i == 2 and self.late_consts:
                for n in self.late_consts:
                    self.dma("sp", self.cst[n][:], self.dr[n])
                self.late_consts = []
            stB(i)

    def fm_proj(self, c, wb, evac, spans=range(4), load=True):
        if load:
            self.dma("pool", wb[:].rearrange("p k n -> p (k n)"), self.dr["wfm"][c])
        for sp in spans:
            ps = self.bank()
            for k in range(8):
                self.mm(ps[:], wb[:, k, :], self.uT[:, k, sp * 512:(sp + 1) * 512], start=(k == 0), stop=(k == 7))
            evac(sp, ps)

    def phase_b1(self, s):
        self.arena_reset()
        al = self.al
        dr, cst, uT = self.dr, self.cst, self.uT
        wfmb = [al("wfmb%d" % i, [128, 8, 128], BF16) for i in range(3)]
        pre = [al("pre%d" % i, [128, 3 + T], F32) for i in range(2)]
        cacc = al("cacc", [128, T], F32)
        qk = al("qk", [128, 8, T], BF16)
        og_all = al("og_all", [128, NT, 512], BF16)
        ogtmp = [al("ogtmp%d" % i, [128, 512], F32) for i in range(2)]
        wtm1b = al("wtm1b", [128, 8, 1032], BF16)
        gsb = al("m_g", [128, 2, 64], F32)
        vexts = [al("m_vext%d" % i, [128, 4, 129], BF16) for i in range(2)]
        ATs = [al("m_AT%d" % i, [128, 4, 128], BF16) for i in range(2)]
        kTs = [al("m_kT%d" % i, [128, 4, 128], BF16) for i in range(2)]
        C32 = al("m_C32", [128, 4, 129], F32)
        Cb = al("m_Cb", [128, 4, 129], BF16)
        hhs = [al("m_hh%d" % i, [128, 4, 128], F32) for i in range(2)]
        junk = al("m_junk", [128, 128], F32)
        yms = [al("m_ym%d" % i, [128, 512], BF16) for i in range(2)]
        btm1 = self.btm1
        for p in pre:
            self.memset("dve", p[:, 0:3], 0.0)
        wtm1 = dr["wtm1"].rearrange("(k p) n -> p k n", p=128)
        for k in range(8):
            self.dma("pool", wtm1b[:, k, :], wtm1[:, k, :])
        for c in range(8):
            prb = pre[c % 2]

            def evac(sp, ps, prb=prb, c=c):
                self.act(prb[:, 3 + sp * 512:3 + (sp + 1) * 512], ps[:], AF.Identity, bias=cst["bfm"][:, c:c + 1])
            self.fm_proj(c, wfmb[c % 3], evac)
            cw = cst["convw"]
            self.ts("dve", cacc[:], prb[:, 0:T], cw[:, c, 0:1], ALU.mult)
            for j in range(1, 4):
                self.stt("dve", cacc[:], prb[:, j:j + T], cw[:, c, j:j + 1], cacc[:], ALU.mult, ALU.add)
            self.act(qk[:, c, :], cacc[:], AF.Silu)
        if "qk" in self.dbg_t:
            self.dma("sp", self.dbg_t["qk"][s], qk[:], is_out=True)
        for i in range(NT):
            tsl = slice(i * 128, (i + 1) * 128)
            ps = self.bank()
            for k in range(8):
                self.mm(ps[:], uT[:, k, tsl], wtm1b[:, k, 512:1024], start=(k == 0), stop=False)
            self.mm(ps[:], self.ones_row[0:1, :], btm1[0:1, 512:1024], start=False, stop=True)
            ogt = ogtmp[i % 2]
            self.act(ogt[:], ps[:], AF.Sigmoid)
            self.tt("dve", og_all[:, i, :], ogt[:], cst["ghead"][:], ALU.mult)
        live = {}

        def P1(c):
            tsl = slice(c * 128, c * 128 + 128)
            par = c % 2
            g = gsb[:, par, :]
            ps_v, ps_g = self.ps[6 + par], self.bank()
            for (ps, n0, nn) in ((ps_v, 0, 512), (ps_g, 1024, 8)):
                for k in range(8):
                    self.mm(ps[:, 0:nn], uT[:, k, tsl], wtm1b[:, k, n0:n0 + nn], start=(k == 0), stop=False)
                self.mm(ps[:, 0:nn], self.ones_row[0:1, :], btm1[0:1, n0:n0 + nn], start=False, stop=True)
            self.tt("dve", g[:, 0:4], ps_g[:, 4:8], cst["bfg"][:], ALU.add)
            self.copy("dve", g[:, 4:8], ps_g[:, 0:4])
            self.act(g[:, 44:48], g[:, 0:4], AF.Exp, scale=-1.0)
            self.act(g[:, 48:52], g[:, 44:48], AF.Ln, bias=1.0)
            ps_s = self.bank()
            for h in range(4):
                self.mm(ps_s[:, h * 128:(h + 1) * 128], qk[:, 4 + h, tsl], qk[:, h, tsl])
            self.tt("dve", ATs[par][:], ps_s[:].rearrange("p (h n) -> p h n", h=4),
                    cst["maskT"][:].unsqueeze(1).to_broadcast([128, 4, 128]), ALU.mult)
            if c < NT - 1:
                ps_t = self.bank_bf()
                for h in range(4):
                    self.tr(ps_t[:, h * 128:(h + 1) * 128], qk[:, 4 + h, tsl])
                self.copy("act", kTs[par][:], ps_t[:, 0:512].rearrange("p (h n) -> p h n", h=4))
            live[c] = ps_v

        def P2a(c):
            par = c % 2
            g = gsb[:, par, :]
            ps_v = live.pop(c)
            ps_c = self.bank()
            self.mm(ps_c[:, 0:4], cst["negtri"][:], g[:, 48:52])
            self.mm(ps_c[:, 4:8], cst["negones"][:], g[:, 48:52])
            self.tt("dve", g[:, 8:12], g[:, 4:8], ps_c[:, 0:4], ALU.subtract)
            self.copy("dve", g[:, 16:24], ps_c[:, 0:8])
            self.act(g[:, 12:16], g[:, 8:12], AF.Exp, bias=LN_C)
            self.act(g[:, 16:24], g[:, 16:24], AF.Exp)
            vext = vexts[par]
            self.tt("dve", vext[:, :, 0:128], ps_v[:].rearrange("p (h n) -> p h n", h=4),
                    g[:, 12:16].unsqueeze(2).to_broadcast([128, 4, 128]), ALU.mult)
            self.copy("dve", vext[:, :, 128:129], g[:, 12:16].unsqueeze(2))

        def P2b(c):
            tsl = slice(c * 128, c * 128 + 128)
            par = c % 2
            g = gsb[:, par, :]
            vext, AT, kT = vexts[par], ATs[par], kTs[par]
            ps_n = [self.bank(), self.bank()]
            for h in range(4):
                pn = ps_n[h // 2][:, (h % 2) * 129:(h % 2) * 129 + 129]
                self.mm(pn, AT[:, h, :], vext[:, h, :], start=(h % 2 == 0), stop=(c == 0))
                if c > 0:
                    self.mm(pn, qk[:, h, tsl], Cb[:, h, :], start=False, stop=True)
            if c < NT - 1:
                ps_u = [self.bank(), self.bank()]
                for h in range(4):
                    pu = ps_u[h // 2][:, (h % 2) * 129:(h % 2) * 129 + 129]
                    self.mm(pu, kT[:, h, :], vext[:, h, :], start=(h % 2 == 0), stop=True)
                for hp in range(2):
                    cs = C32[:, 2 * hp:2 * hp + 2, :]
                    pu = ps_u[hp][:, 0:258].rearrange("p (h n) -> p h n", h=2)
                    ebb = g[:, 20 + 2 * hp:22 + 2 * hp].unsqueeze(2).to_broadcast([128, 2, 129])
                    if c == 0:
                        self.tt("dve", cs, pu, ebb, ALU.mult)
                    else:
                        self.tt("dve", cs, cs, pu, ALU.add)
                        self.tt("dve", cs, cs, ebb, ALU.mult)
                self.copy("act", Cb[:], C32[:])
            for hp in range(2):
                pn3 = ps_n[hp][:, 0:258].rearrange("p (h n) -> p h n", h=2)
                self.tt("dve", g[:, 24 + 2 * hp:26 + 2 * hp].unsqueeze(2), pn3[:, :, 128:129],
                        g[:, 16 + 2 * hp:18 + 2 * hp].unsqueeze(2), ALU.mult)
            self.ts("dve", g[:, 56:60], g[:, 24:28], 1.0, ALU.max)
            self.stt("dve", g[:, 28:32], g[:, 24:28], -1.0, g[:, 56:60], ALU.mult, ALU.max)
            self.recip(g[:, 60:64], g[:, 28:32])
            self.tt("dve", g[:, 32:36], g[:, 16:20], g[:, 60:64], ALU.mult)
            hh = hhs[par]
            for hp in range(2):
                pn3 = ps_n[hp][:, 0:258].rearrange("p (h n) -> p h n", h=2)
                self.tt("dve", hh[:, 2 * hp:2 * hp + 2, :], pn3[:, :, 0:128],
                        g[:, 32 + 2 * hp:34 + 2 * hp].unsqueeze(2).to_broadcast([128, 2, 128]), ALU.mult)
            for h in range(4):
                self.act(junk[:], hh[:, h, :], AF.Square, accum=g[:, 36 + h:37 + h])
            self.act(g[:, 40:44], g[:, 36:40], AF.Ln, bias=1e-6, scale=1.0 / 128)
            self.act(g[:, 52:56], g[:, 40:44], AF.Exp, scale=-0.5)
            for h in range(4):
                self.stt("dve", yms[par][:, h * 128:(h + 1) * 128], hh[:, h, :], g[:, 52 + h:53 + h],
                         og_all[:, c, h * 128:(h + 1) * 128], ALU.mult, ALU.mult)

        def P3t(c):
            tsl = slice(c * 128, c * 128 + 128)
            ym = yms[c % 2]
            ps_y = self.bank_bf()
            for h in range(4):
                self.tr(ps_y[:, h * 128:(h + 1) * 128], ym[:, h * 128:(h + 1) * 128])
            self.copy("act", self.ymT[:, :, tsl], ps_y[:, 0:512].rearrange("p (h n) -> p h n", h=4))

        assert self.aviews["cacc"][1] + self.aviews["cacc"][2] >= NQ_OFF + 4 * T * 2
        assert self.aviews["qk"][1] >= NQ_OFF + 4 * T * 2
        nq_e = self.view_at(NQ_OFF, [128, 4, T], BF16)

        def early_nq(it):
            j, half = it // 2, it % 2

            def evac(sp, ps, j=j):
                self.act(nq_e[:, j, sp * 512:(sp + 1) * 512], ps[:], AF.Identity,
                         bias=cst["bfm8"][:, 8 + j:9 + j], scale=0.125)
            self.fm_proj(8 + j, wfmb[(8 + j) % 3], evac, spans=range(2 * half, 2 * half + 2), load=(half == 0))

        P1(0)
        for c in range(NT):
            P2a(c)
            if 2 <= c < 10:
                early_nq(c - 2)
            if c + 1 < NT:
                P1(c + 1)
            if c >= 1:
                P3t(c - 1)
            P2b(c)
        P3t(NT - 1)

    def phase_b2(self, s):
        self.arena_reset()
        al = self.al
        dr, cst, uT = self.dr, self.cst, self.uT
        wfmb = [al("wfmb%d" % i, [128, 8, 128], BF16) for i in range(3)]
        wtm2b = al("wtm2b", [128, 8, 280], BF16)
        nq = al("nq", [128, 4, T], BF16)
        kcT = al("kcT", [128, T], BF16)
        vcT = al("vcT", [128, T], BF16)
        ksa = [al("ksa%d" % i, [128, T], BF16) for i in range(2)]
        kwz = [al("kwz%d" % i, [128, T], BF16) for i in range(2)]
        qa = [al("qa%d" % i, [128, T], BF16) for i in range(2)]
        vsx = al("vsx", [128, NT, 2, 65], BF16)
        vwx = al("vwx", [128, NT, 2, 65], BF16)
        gat = al("gat", [128, NT, 24], F32)
        oc = al("oc", [128, NT, 4, 64], F32)
        w1b = al("w1b", [128, 2, 32, 128], BF16, alias="oc")
        impg = al("impg", [128, NT, 32], F32)
        w2b = al("w2b", [128, 2, 128], BF16)
        peTb = al("peTb", [128, 2, 32], BF16)
        c0 = al("c0", [128, 2], F32)
        shid = [al("shid%d" % i, [128, 128], BF16) for i in range(2)]
        kcmpz = [al("kcmpz%d" % i, [128, 128], BF16) for i in range(2)]
        vcx = al("vcx", [128, 2, 97], BF16)
        bland = al("bland", [128, T], F32)
        EB = [al("EB%d" % i, [128, T], BF16) for i in range(2)]
        bw16 = [al("bw16_%d" % i, [128, 640], BF16) for i in range(2)]
        e0b = [al("e0b%d" % i, [128, 512], BF16) for i in range(3)]
        eTb = [al("eTb%d" % i, [128, 512], BF16) for i in range(4)]
        tkc = al("tkc", [128, 3, 8, 32], F32)
        tk = al("tk", [128, 128], F32)
        selb = al("selb", [128, 128], BF16)
        sm = al("sm", [128, 32], F32)
        tmpo = [al("tmpo%d" % i, [128, 4, 64], F32) for i in range(2)]
        tmpi = al("tmpi", [128, 4, 32], F32)
        btm2 = self.btm2

        wtm2 = dr["wtm2"].rearrange("(k p) n -> p k n", p=128)
        for k in range(8):
            self.dma("pool", wtm2b[:, k, :], wtm2[:, k, :])
        if not (B2SUB & 2):
            for i, n in enumerate(("cand", "negm", "forced")):
                self.dma("sp", tkc[:, i, :, :], dr[n])
        if not (B2SUB & 4):
            self.memset("dve", vsx[:, :, :, 64:65], 1.0)
            self.memset("dve", vwx[:, :, :, 64:65], 1.0)
            self.memset("dve", vcx[:], 0.0)
            self.memset("dve", vcx[:, :, 96:97], 1.0)
            for g in range(2):
                self.copy("dve", vcx[:, g, 64:96], cst["overlap"][:])

        dests = {12: kcT, 13: vcT}
        dests2 = {14: ksa, 15: kwz}
        assert self.aviews["nq"][1] == NQ_OFF
        for c in range(12, 16):
            if B2SUB & 8:
                break
            if c < 12:
                def evac(sp, ps, c=c):
                    self.act(nq[:, c - 8, sp * 512:(sp + 1) * 512], ps[:], AF.Identity,
                             bias=cst["bfm8"][:, c:c + 1], scale=0.125)
            elif c < 14:
                def evac(sp, ps, c=c):
                    self.act(dests[c][:, sp * 512:(sp + 1) * 512], ps[:], AF.Identity, bias=cst["bfm"][:, c:c + 1])
            else:
                def evac(sp, ps, c=c):
                    for g_ in range(2):
                        gs_ = slice(g_ * 64, (g_ + 1) * 64)
                        self.act(dests2[c][g_][gs_, sp * 512:(sp + 1) * 512], ps[gs_, :], AF.Identity,
                                 bias=cst["bfm"][gs_, c:c + 1])
            self.fm_proj(c, wfmb[c % 3], evac)
        if not (B2SUB & 1):
            for j in range(2):
                for lh in range(4):
                    self.dma("pool", w1b[:, j, lh * 8:(lh + 1) * 8, :], dr["w1"][:, j, lh * 8:(lh + 1) * 8, :])
            self.dma("pool", w2b[:], dr["w2"])
            self.dma("pool", peTb[:], dr["peT"])
            self.memset("dve", ksa[0][64:128, :], 0.0)
            self.memset("dve", ksa[1][0:64, :], 0.0)
            self.memset("dve", kwz[0][64:128, :], 0.0)
            self.memset("dve", kwz[1][0:64, :], 0.0)
            self.memset("dve", kcmpz[0][:], 0.0)
            self.memset("dve", kcmpz[1][:], 0.0)
            self.memset("dve", selb[:], 0.0)
            for hh_ in range(2):
                hs_ = slice(hh_ * 1024, (hh_ + 1) * 1024)
                self.dma("pool", ksa[0][64:96, hs_], dr["expand"][:, hs_])
                self.dma("pool", ksa[1][0:32, hs_], dr["expand"][:, hs_])
        self.pl = self.prologue_list() if (s == 0 and self.stop_after in (None, "c")) else []
        for i in range(NT):
            if B2SUB & 16:
                break
            tsl = slice(i * 128, (i + 1) * 128)
            ps = self.bank()
            for k in range(8):
                self.mm(ps[:, 0:280], uT[:, k, tsl], wtm2b[:, k, :], start=(k == 0), stop=False)
            self.mm(ps[:, 0:280], self.ones_row[0:1, :], btm2[0:1, :], start=False, stop=True)
            if not (B2TM & 1):
                self.copy("act", vsx[:, i, :, 0:64], ps[:, 0:128].rearrange("p (g d) -> p g d", g=2))
            if not (B2TM & 2):
                self.copy("dve", vwx[:, i, :, 0:64], ps[:, 128:256].rearrange("p (g d) -> p g d", g=2))
            if not (B2TM & 4):
                self.act(gat[:, i, :], ps[:, 256:280], AF.Sigmoid)

        if B2STOP == 1:
            return
        for j in range(2):
            ps = self.bank()
            for l in range(32):
                self.mm(ps[:, 0:1], w1b[0:64, j, l, :], peTb[0:64, j, l:l + 1], start=(l == 0), stop=(l == 31))
            self.copy("dve", c0[:, j:j + 1], ps[:, 0:1])
        for j in range(2):
            src = kcT if j == 0 else vcT
            for g in range(2):
                ps = self.bank()
                for l in range(32):
                    self.mm(ps[:, 0:127], w1b[g * 64:(g + 1) * 64, j, l, :],
                            src[g * 64:(g + 1) * 64, l:l + 16 * 126 + 1:16], start=(l == 0), stop=(l == 31))
                sh = shid[(2 * j + g) % 2]
                self.act(sh[:, 0:127], ps[:, 0:127], AF.Silu, bias=c0[:, j:j + 1])
                ps2 = self.bank()
                if j == 0:
                    self.mm(ps2[:, 0:127], w2b[:, 0, :], sh[:, 0:127])
                    self.copy("dve", kcmpz[g][g * 64:(g + 1) * 64, 0:127], ps2[g * 64:(g + 1) * 64, 0:127])
                else:
                    self.mm(ps2[0:127, 0:64], sh[:, 0:127], w2b[:, 1, 0:64])
                    self.copy("dve", vcx[0:127, g, 0:64], ps2[0:127, 0:64])

        if B2STOP == 2:
            return
        cnt = {"ne": 0}
        jobs = []
        for g_ in range(2):
            jobs += [("c", g_, r_) for r_ in range(4)] + [("s", g_, r_) for r_ in range(4)]

        def pf_dma(ji):
            if ji >= len(jobs):
                return
            kind, g_, r_ = jobs[ji]
            h_ = g_ * 4 + r_
            if kind == "c":
                self.dma("sp", bland[:], dr["BC"][h_])
            else:
                self.dma("sp", bland[:], dr["BS"][h_])
                self.dma("pool", bw16[ji % 2][:], dr["BW"][h_])

        def pf_exp(ji):
            self.prologue_some(3)
            if ji >= len(jobs):
                return
            kind = jobs[ji][0]
            self.act(EB[ji % 2][:], bland[:], AF.Exp)

        pf_dma(0)
        pf_exp(0)
        self.prologue_some(2)

        def run_steps(steps, lag=3):
            n = len(steps)
            for i in range(n + lag):
                if i < n:
                    steps[i][0]()
                if i - lag >= 0:
                    steps[i - lag][1]()

        for g in range(2):
            gs = slice(g * 64, (g + 1) * 64)
            steps = []
            for r in range(4):
                h = g * 4 + r
                ji = jobs.index(("c", g, r))
                for sp in range(4):
                    box = {}

                    def fS(r=r, h=h, sp=sp, ji=ji, box=box):
                        if sp == 0:
                            pf_dma(ji + 1)
                        if sp == 2:
                            pf_exp(ji + 1)
                        ssl = slice(sp * 512, (sp + 1) * 512)
                        ps = self.bank()
                        self.mm(ps[0:127, :], kcmpz[g][:, 0:127], nq[:, r, ssl])
                        e0 = e0b[cnt["ne"] % 3]
                        eT = eTb[cnt["ne"] % 4]
                        cnt["ne"] += 1
                        self.act(e0[0:127, :], ps[0:127, :], AF.Exp)
                        self.tt("dve", eT[0:127, :], e0[0:127, :], EB[ji % 2][0:127, ssl], ALU.mult)
                        box["eT"] = eT

                    def fP(r=r, h=h, sp=sp, box=box):
                        eT = box["eT"]
                        ps2 = self.bank()
                        for q in range(4):
                            self.mm(ps2[:, q * 97:(q + 1) * 97], eT[0:127, q * 128:(q + 1) * 128], vcx[0:127, g, :])
                        p3 = ps2[:, 0:388].rearrange("p (q n) -> p q n", q=4)
                        tq = slice(sp * 4, sp * 4 + 4)
                        self.ts("dve", tk[:, 0:4].unsqueeze(2), p3[:, :, 96:97], 1e-30, ALU.max)
                        self.recip(tk[:, 4:8], tk[:, 0:4])
                        self.tt("dve", tk[:, 8:12].unsqueeze(2), tk[:, 4:8].unsqueeze(2), gat[:, tq, h * 3:h * 3 + 1], ALU.mult)
                        self.tt("dve", oc[:, tq, r, :], p3[:, :, 0:64],
                                tk[:, 8:12].unsqueeze(2).to_broadcast([128, 4, 64]), ALU.mult)
                        rb = tk[:, 4:8].unsqueeze(2).to_broadcast([128, 4, 32])
                        if r == 0:
                            self.tt("dve", impg[:, tq, :], p3[:, :, 64:96], rb, ALU.mult)
                        else:
                            self.tt("dve", tmpi[:], p3[:, :, 64:96], rb, ALU.mult)
                            self.tt("dve", impg[:, tq, :], impg[:, tq, :], tmpi[:], ALU.add)
                    steps.append((fS, fP))
            run_steps(steps)
            if B2STOP == 3:
                return
            mrows = slice(64, 96) if g == 0 else slice(0, 32)
            orows = slice(64, 128) if g == 0 else slice(0, 64)
            mc0 = 64 if g == 0 else 0
            for b_ in range(2):
                self.memset("dve", qa[b_][orows, :], 0.0)
            for qi in range(8):
                qt = 8 + qi
                wk = tk[:, 16:48]
                self.tt("dve", wk, impg[:, qt, :], tkc[:, 0, qi, :], ALU.mult)
                self.tt("dve", wk, wk, tkc[:, 1, qi, :], ALU.add)
                self.S.op("dve", lambda e, wk=wk: e.max(out=tk[:, 48:56], in_=wk), reads=[wk], writes=[tk[:, 48:56]])
                wk2 = tk[:, 64:96]
                self.S.op("dve", lambda e, wk=wk, wk2=wk2: e.match_replace(out=wk2, in_to_replace=tk[:, 48:56],
                                                                            in_values=wk, imm_value=-1.0),
                          reads=[wk, tk[:, 48:56]], writes=[wk2])
                self.S.op("dve", lambda e, wk2=wk2: e.max(out=tk[:, 56:64], in_=wk2), reads=[wk2], writes=[tk[:, 56:64]])
                self.ts("dve", sm[:], wk, tk[:, 60:61], ALU.is_ge)
                self.tt("dve", sm[:], sm[:], tkc[:, 2, qi, :], ALU.add)
                self.ts("dve", selb[:, mc0:mc0 + 32], sm[:], -1.0, ALU.add, -NEG, ALU.mult)
                pst = self.bank_bf()
                self.tr(pst[:, 0:128], selb[:])
                for b_ in range(2):
                    self.copy("act", qa[b_][mrows, qt * 128:(qt + 1) * 128], pst[mrows, 0:128])
            if B2STOP == 4:
                return
            steps = []
            self.copy("dve", qa[0][gs, :], nq[gs, 0, :])
            for r in range(4):
                h = g * 4 + r
                ji = jobs.index(("s", g, r))
                si = 0
                for branch in range(2):
                    kT_ = ksa[g] if branch == 0 else kwz[g]
                    vx = vsx if branch == 0 else vwx
                    bias = EB[ji % 2] if branch == 0 else bw16[ji % 2]
                    for qs in range(4):
                        q_lo, q_hi = 4 * qs, 4 * qs + 3
                        kt_lo = 0 if branch == 0 else max(0, q_lo - 4)
                        span = {}
                        for kt in range(kt_lo, q_hi + 1):
                            q0 = max(q_lo, kt)
                            q1 = q_hi if branch == 0 else min(q_hi, kt + 4)
                            box = {}

                            def fS(r=r, h=h, branch=branch, qs=qs, kt=kt, q0=q0, q1=q1, kT_=kT_, bias=bias,
                                   ji=ji, si=si, box=box):
                                if si == 0:
                                    pf_dma(ji + 1)
                                    if r + 1 < 4:
                                        self.copy("dve", qa[(r + 1) % 2][gs, :], nq[gs, r + 1, :])
                                if si == 6:
                                    pf_exp(ji + 1)
                                n = (q1 - q0 + 1) * 128
                                ksl = slice(kt * 128, (kt + 1) * 128)
                                qsl = slice(q0 * 128, (q1 + 1) * 128)
                                ps = self.bank()
                                qsrc = qa[r % 2][:, qsl] if branch == 0 else nq[:, r, qsl]
                                b0 = (q0 - kt) * 128
                                e0 = e0b[cnt["ne"] % 3]
                                eT = eTb[cnt["ne"] % 4]
                                cnt["ne"] += 1
                                if branch == 0:
                                    self.mm(ps[:, 0:n], kT_[:, ksl], qsrc)
                                    self.act(e0[:, 0:n], ps[:, 0:n], AF.Exp)
                                    self.tt("dve", eT[:, 0:n], e0[:, 0:n], bias[:, b0:b0 + n], ALU.mult)
                                else:
                                    self.mm(ps[:, 0:n], kT_[:, ksl], qsrc, start=True, stop=False)
                                    self.mm(ps[:, 0:n], self.identb[:], bias[:, b0:b0 + n], start=False, stop=True)
                                    self.act(eT[:, 0:n], ps[:, 0:n], AF.Exp)
                                box["eT"] = eT

                            def fP(r=r, h=h, branch=branch, qs=qs, kt=kt, q0=q0, q1=q1, vx=vx, box=box, span=span,
                                   q_lo=q_lo, q_hi=q_hi, kt_lo=kt_lo):
                                eT = box["eT"]
                                if kt == kt_lo:
                                    span["pacc"] = self.accbank()
                                pacc = span["pacc"]
                                for qt in range(q0, q1 + 1):
                                    self.mm(pacc[:, (qt - q_lo) * 65:(qt - q_lo) * 65 + 65],
                                            eT[:, (qt - q0) * 128:(qt - q0 + 1) * 128], vx[:, kt, g, :],
                                            start=(kt == kt_lo and qt == q0), stop=(kt == qt))
                                if kt == q_hi:
                                    p4 = pacc[:, 0:260].rearrange("p (q n) -> p q n", q=4)
                                    tq = slice(q_lo, q_lo + 4)
                                    o = 96 + branch * 16
                                    self.recip(tk[:, o + 4:o + 8].unsqueeze(2), p4[:, :, 64:65])
                                    self.tt("dve", tk[:, o + 8:o + 12].unsqueeze(2), tk[:, o + 4:o + 8].unsqueeze(2),
                                            gat[:, tq, h * 3 + 1 + branch:h * 3 + 2 + branch], ALU.mult)
                                    for q_ in range(4):
                                        self.stt("dve", oc[:, q_lo + q_, r, :], p4[:, q_, 0:64], tk[:, o + 8 + q_:o + 9 + q_],
                                                 oc[:, q_lo + q_, r, :], ALU.mult, ALU.add)
                            steps.append((fS, fP))
                            si += 1
            run_steps(steps)
            if B2STOP == 5:
                return
            for i in range(NT):
                if i % 2 == 0:
                    psy = self.bank()
                for p in range(2):
                    col = ((i % 2) * 2 + p) * 128
                    self.tr(psy[:, col:col + 128], oc[:, i, 2 * p:2 * p + 2, :].rearrange("p a b -> p (a b)"), f32=True)
                if i % 2 == 1:
                    i0 = (i - 1) * 128
                    self.copy("act" if (i // 2) % 2 else "dve",
                              self.ynT[:, 2 * g:2 * g + 2, i0:i0 + 256].rearrange("p c (i n) -> p i c n", i=2),
                              psy[:].rearrange("p (i c n) -> p i c n", i=2, c=2))

    def phase_c(self, s):
        self.prologue_some(1000)
        self.arena_reset()
        al = self.al
        dr, cst, uT = self.dr, self.cst, self.uT
        x, out = dr["x"], dr["out"]
        wdp = [al("wdp%d" % i, [128, 11, 512], BF16) for i in range(2)]
        self.aoff_save = self.aoff
        self.aoff = self.aviews["wdp0"][1]
        wmgb = [al("wmgb%d" % i, [128, 8, 128], BF16) for i in range(4)]
        wbrb = [al("wbrb%d" % i, [128, 4, 128], BF16) for i in range(4)]
        assert self.aoff <= self.aoff_save
        self.aoff = self.aoff_save
        mixT = al("mixT", [128, 8, 512], BF16)
        woutb = al("woutb", [128, 8, D], BF16)
        xt = [al("xt%d" % i, [128, D], F32) for i in range(2)]
        h2 = al("h2", [128, 4, D], F32)
        fT = al("fT", [128, 8, 512], BF16)
        aT = al("aT", [128, 22, 512], BF16)
        wgb = [al("wgb%d" % i, [128, 8, 128], BF16) for i in range(3)]
        wub = [al("wub%d" % i, [128, 8, 128], BF16) for i in range(3)]
        sgb = [al("sgb%d" % i, [128, 512], F32) for i in range(4)]
        fnb = [al("fnb%d" % i, [128, D], BF16) for i in range(2)]
        junkb = al("junkc", [128, D], BF16, alias="sgb3")
        st = self.st
        scr = self.scr
        wdown = scr["wdown"].rearrange("(h p c) n -> h p (c n)", h=2, p=128)
        yT = (self.ymT, self.ynT)
        self.dma("sp", woutb[:].rearrange("p k n -> p (k n)"), scr["wout"])
        cn = {"nsg": 0}

        def c1(sti, dcs):
            csl = slice(sti * 512, sti * 512 + 512)
            for dc in dcs:
                sgs = []
                for n in range(2):
                    wb = wmgb[(2 * dc + n) % 4]
                    self.dma("sp", wb[:].rearrange("p k n -> p (k n)"), scr["wmg"][n * 8 + dc])
                    ps = self.bank()
                    for k in range(8):
                        self.mm(ps[:], wb[:, k, :], uT[:, k, csl], start=(k == 0), stop=(k == 7))
                    sg = sgb[cn["nsg"] % 3]
                    cn["nsg"] += 1
                    self.act(sg[:], ps[:], AF.Sigmoid, bias=cst["bmg"][:, n * 8 + dc:n * 8 + dc + 1])
                    sgs.append(sg)
                for n in range(2):
                    wb = wbrb[(2 * dc + n) % 4]
                    self.dma("sp", wb[:].rearrange("p k n -> p (k n)"), scr["wbr"][n * 8 + dc])
                    ps = self.bank()
                    for k in range(4):
                        self.mm(ps[:], wb[:, k, :], yT[n][:, k, csl], start=(k == 0), stop=(k == 3))
                    self.tt("dve", sgs[n][:], sgs[n][:], ps[:], ALU.mult)
                self.tt("dve", mixT[:, dc, :], sgs[0][:], sgs[1][:], ALU.add)

        c1(0, range(8))
        for sti in range(4):
            c0 = sti * 512
            def c2(i):
                r0 = s * T + c0 + i * 128
                self.dma("sp", xt[i % 2][:], x[r0:r0 + 128, :])
                for half in range(2):
                    hsl = slice(half * 512, (half + 1) * 512)
                    ps = self.bank()
                    for k in range(8):
                        self.mm(ps[:], mixT[:, k, i * 128:(i + 1) * 128], woutb[:, k, hsl], start=(k == 0), stop=(k == 7))
                    self.tt("dve", h2[:, i, hsl], ps[:], xt[i % 2][:, hsl], ALU.add)

            def c3a(i):
                c = i % 2
                self.act(junkb[:], h2[:, i, :], AF.Square, accum=st[:, 8 + c:9 + c])
                self.act(st[:, 10 + c:11 + c], st[:, 8 + c:9 + c], AF.Sqrt, bias=1e-6, scale=1.0 / D)
                self.recip(st[:, 12 + c:13 + c], st[:, 10 + c:11 + c])
                self.stt("dve", fnb[c][:], h2[:, i, :], st[:, 12 + c:13 + c], cst["gffn"][:], ALU.mult, ALU.mult)

            def c3b(i):
                fn = fnb[i % 2]
                pb = self.bank_bf()
                for k in range(8):
                    self.tr(pb[:, k * 128:(k + 1) * 128], fn[:, k * 128:(k + 1) * 128])
                self.copy("act", fT[:, :, i * 128:(i + 1) * 128], pb.rearrange("p (k n) -> p k n", k=8))

            for i in range(4):
                c2(i)
                c3a(i)
                if i >= 1:
                    c3b(i - 1)
            if sti + 1 < 4:
                c1(sti + 1, range(0, 1))
            c3b(3)
            if sti + 1 < 4:
                c1(sti + 1, range(1, 8))
            for c in range(22):
                wg, wu = wgb[c % 3], wub[c % 3]
                self.dma("sp", wg[:].rearrange("p k n -> p (k n)"), scr["wgate"][c])
                self.dma("sp", wu[:].rearrange("p k n -> p (k n)"), scr["wup"][c])
                psg, psu = self.bank(), self.bank()
                for k in range(8):
                    self.mm(psg[:], wg[:, k, :], fT[:, k, :], start=(k == 0), stop=(k == 7))
                for k in range(8):
                    self.mm(psu[:], wu[:, k, :], fT[:, k, :], start=(k == 0), stop=(k == 7))
                sg = sgb[cn["nsg"] % 3]
                cn["nsg"] += 1
                self.act(sg[:], psg[:], AF.Silu)
                self.tt("dve", aT[:, c, :], sg[:], psu[:], ALU.mult)
            for half in range(2):
                accs = [self.bank() for _ in range(4)]
                for piece in range(2):
                    wp = wdp[piece]
                    self.dma("sp", wp[:].rearrange("p c n -> p (c n)"),
                             wdown[half][:, piece * 11 * 512:(piece + 1) * 11 * 512])
                    for i in range(4):
                        for cc in range(11):
                            c = piece * 11 + cc
                            self.mm(accs[i][:], aT[:, c, i * 128:(i + 1) * 128], wp[:, cc, :], start=(c == 0), stop=(c == 21))
                for i in range(4):
                    hs = h2[:, i, half * 512:(half + 1) * 512]
                    self.tt("dve", hs, accs[i][:], hs, ALU.add)
            for i in range(4):
                c = i % 2
                self.act(junkb[:], h2[:, i, :], AF.Square, accum=st[:, 16 + c:17 + c])
                self.act(st[:, 18 + c:19 + c], st[:, 16 + c:17 + c], AF.Sqrt, bias=1e-6, scale=1.0 / D)
                self.recip(st[:, 20 + c:21 + c], st[:, 18 + c:19 + c])
                self.stt("dve", h2[:, i, :], h2[:, i, :], st[:, 20 + c:21 + c], cst["gfin"][:], ALU.mult, ALU.mult)
                r0 = s * T + c0 + i * 128
                self.dma("sp", out[r0:r0 + 128, :], h2[:, i, :], is_out=True)


_CACHE = {}


def kernel(**inputs):
    NS = 2
    ncores = 8
    sh = prep_shared(inputs)
    if "nc" not in _CACHE:
        _CACHE["nc"] = Builder(NS).build()
    nc = _CACHE["nc"]
    x = np.asarray(inputs["x"], np.float32)
    in_maps = []
    for c in range(ncores):
        m = dict(sh)
        m["x"] = np.ascontiguousarray(x[c * NS:(c + 1) * NS].reshape(NS * T, D))
        in_maps.append(m)
    res = run_bass_kernel_spmd(nc, in_maps, core_ids=list(range(ncores)))
    outs = [np.asarray(r["out"], np.float32).reshape(NS, T, D) for r in res.results]
    return np.concatenate(outs, axis=0)
```

```python
import math
from contextlib import ExitStack
import numpy as np
import concourse.bass as bass
import concourse.mybir as mybir
from concourse.bass_utils import run_bass_kernel_spmd

F32 = mybir.dt.float32
BF16 = mybir.dt.bfloat16
AF = mybir.ActivationFunctionType
ALU = mybir.AluOpType
AX = mybir.AxisListType

T = 2048
D = 1024
NT = 16
DFF = 2816
NEG = -30000.0
LN_C = math.log(128 ** -0.5)

_DTSZ = {"dt.float32": 4, "dt.bfloat16": 2, "dt.int32": 4, "dt.uint32": 4, "dt.float16": 2,
         "dt.uint16": 2, "dt.int16": 2, "dt.uint8": 1, "dt.int8": 1}


def _box(ap):
    t = ap.tensor
    name = t.name
    tn = type(t).__name__
    if tn == "PSumTensorHandle":
        return (name, 0, 128, 0, 2048)
    a = ap.ap
    off = int(ap.offset)
    isz = _DTSZ[str(ap.dtype)]
    if tn == "DRamTensorHandle":
        ext = 1
        for st, cnt in a:
            ext += (cnt - 1) * abs(st)
        return (name, 0, 1, off * isz, (off + ext) * isz)
    pstep, pcnt = a[0]
    if pstep == 0:
        pstep = 1 << 40
    p0 = off // pstep
    f0 = off % pstep
    ext = 1
    for st, cnt in a[1:]:
        ext += (cnt - 1) * abs(st)
    return (name, p0, p0 + pcnt, f0 * isz, (f0 + ext) * isz)


def _ovl(a, b):
    return a[1] < b[2] and b[1] < a[2] and a[3] < b[4] and b[3] < a[4]


def _contains(a, b):
    return a[1] <= b[1] and b[2] <= a[2] and a[3] <= b[3] and b[4] <= a[4]


class Op:
    __slots__ = ("eng", "fn", "rb", "wb", "dma", "deps", "sig", "clock", "idx", "waits")


class Sched:
    ENGS = ("pe", "act", "dve", "pool", "sp")

    def __init__(self, nc, n_dma_sems=32):
        self.nc = nc
        self.ops = []
        self.n_dma_sems = n_dma_sems
        self.out_ops = []

    def op(self, eng, fn, reads=(), writes=(), dma=False, is_out=False):
        o = Op()
        o.eng = eng
        o.fn = fn
        o.rb = [_box(r) for r in reads if r is not None and not isinstance(r, (int, float))]
        o.wb = [_box(w) for w in writes]
        o.dma = dma
        o.idx = len(self.ops)
        self.ops.append(o)
        if is_out:
            self.out_ops.append(o)
        return o

    def analyze(self, skip=()):
        hist = {}
        dma_ops = []
        ops = self.ops
        self.dead = []
        for o in ops:
            deps = set()
            for r in o.rb:
                if r[0] in skip:
                    continue
                psum = r[0].startswith("ps")
                for rec in hist.get(r[0], ()):
                    if _ovl(rec[0], r) and (rec[2] or (psum and rec[1].eng != o.eng)):
                        deps.add(rec[1].idx)
                        if rec[2]:
                            rec[3] += 1
            for w in o.wb:
                if w[0] in skip:
                    continue
                for rec in hist.get(w[0], ()):
                    if _ovl(rec[0], w):
                        deps.add(rec[1].idx)
            for w in o.wb:
                if w[0] in skip:
                    continue
                lst = hist.setdefault(w[0], [])
                keep = []
                for rec in lst:
                    if _contains(w, rec[0]):
                        if rec[2] and rec[3] == 0 and not (rec[1].eng == "pe" and o.eng == "pe" and not o.dma):
                            self.dead.append((w[0], rec[1].idx, o.idx))
                    else:
                        keep.append(rec)
                lst[:] = keep
                lst.append([w, o, True, 0])
            for r in o.rb:
                if r[0] in skip:
                    continue
                lst = hist.setdefault(r[0], [])
                if not o.dma:
                    lst[:] = [rec for rec in lst if not ((not rec[2]) and rec[0] == r
                                                         and rec[1].eng == o.eng and not rec[1].dma)]
                lst.append([r, o, False, 0])
            if o.dma:
                k = len(dma_ops)
                if k >= self.n_dma_sems:
                    deps.add(dma_ops[k - self.n_dma_sems].idx)
                dma_ops.append(o)
            deps.discard(o.idx)
            if o.eng == "pe" and not o.dma:
                deps = {d for d in deps if ops[d].eng != "pe" or ops[d].dma}
            o.deps = sorted(deps)
        need = set()
        for o in ops:
            need.update(o.deps)
        for o in self.out_ops:
            need.add(o.idx)
        cnt = {e: 0 for e in self.ENGS}
        ndma = 0
        for o in ops:
            o.sig = None
            if o.dma:
                s = ndma % self.n_dma_sems
                o.sig = (("dma", s), 16 * (ndma // self.n_dma_sems + 1))
                ndma += 1
            elif o.idx in need:
                cnt[o.eng] += 1
                o.sig = ((o.eng,), cnt[o.eng])
        seen = {e: {} for e in self.ENGS}
        nw = 0
        for o in ops:
            sn = seen[o.eng]
            wm = {}
            for d in o.deps:
                dop = ops[d]
                s, v = dop.sig
                if sn.get(s, 0) >= v:
                    continue
                if wm.get(s, 0) < v:
                    wm[s] = v
                for k2, v2 in dop.clock.items():
                    if sn.get(k2, 0) < v2:
                        sn[k2] = v2
            o.waits = list(wm.items())
            nw += len(o.waits)
            if o.sig is not None:
                c = dict(sn)
                c[o.sig[0]] = o.sig[1]
                o.clock = c
            else:
                o.clock = None
        self.stats = dict(n_ops=len(ops), n_waits=nw, sig=dict(cnt), n_dma=ndma, dead=self.dead[:8], n_dead=len(self.dead))
        return self.stats

    def emit(self):
        nc = self.nc
        with ExitStack() as es:
            sems = {}
            for e in self.ENGS:
                sems[(e,)] = es.enter_context(nc.semaphore("s_" + e))
            for i in range(self.n_dma_sems):
                sems[("dma", i)] = es.enter_context(nc.semaphore("s_dma%d" % i))
            block = es.enter_context(nc.Block())
            per = {e: [o for o in self.ops if o.eng == e] for e in self.ENGS}
            finals = [o.sig for o in self.out_ops]

            def run(engh, lst, final=False):
                for o in lst:
                    for s, v in o.waits:
                        engh.wait_ge(sems[s], v)
                    if o.fn is None:
                        continue
                    ins = o.fn(engh)
                    if o.sig is not None:
                        ins.then_inc(sems[o.sig[0]], 16 if o.dma else 1)
                if final:
                    fm = {}
                    for s, v in finals:
                        if fm.get(s, 0) < v:
                            fm[s] = v
                    for s, v in fm.items():
                        engh.wait_ge(sems[s], v)

            @block.tensor
            def _(e):
                run(e, per["pe"])

            @block.scalar
            def _(e):
                run(e, per["act"])

            @block.vector
            def _(e):
                run(e, per["dve"])

            @block.gpsimd
            def _(e):
                run(e, per["pool"])

            @block.sync
            def _(e):
                run(e, per["sp"], final=True)


def _rel_bucket_np():
    n = np.arange(0, T + 1)
    nf = np.maximum(n, 1).astype(np.float32)
    large = 16 + (np.log(nf / np.float32(16)) / np.float32(math.log(1024 / 16)) * np.float32(16)).astype(np.int32)
    large = np.minimum(large, 31)
    return np.where(n < 16, n, large).astype(np.int64)


_OFF = {}
_o = 0
for _n, _s in (("mq", 512), ("mk", 512), ("mv", 512), ("mo", 512), ("mi", 4), ("mf", 4),
               ("nq", 512), ("nkv", 768), ("ngate", 24), ("merge", 2048)):
    _OFF[_n] = _o
    _o += _s


def prep_shared(inp):
    w_in = np.asarray(inp["w_in"][0], np.float32)
    b_in = np.asarray(inp["b_in"][0], np.float32)
    cols = []
    cols += list(range(_OFF["mq"], _OFF["mq"] + 512))
    cols += list(range(_OFF["mk"], _OFF["mk"] + 512))
    for j in range(4):
        for g in range(2):
            h = g * 4 + j
            cols += list(range(_OFF["nq"] + h * 64, _OFF["nq"] + h * 64 + 64))
    for s in (0, 1, 2, 4):
        cols += list(range(_OFF["nkv"] + s * 128, _OFF["nkv"] + s * 128 + 128))
    cols = np.array(cols)
    tm1 = np.array(list(range(_OFF["mv"], _OFF["mv"] + 512)) + list(range(_OFF["mo"], _OFF["mo"] + 512))
                   + list(range(_OFF["mi"], _OFF["mi"] + 4)) + list(range(_OFF["mf"], _OFF["mf"] + 4)))
    tm2 = np.array(list(range(_OFF["nkv"] + 3 * 128, _OFF["nkv"] + 4 * 128))
                   + list(range(_OFF["nkv"] + 5 * 128, _OFF["nkv"] + 6 * 128))
                   + list(range(_OFF["ngate"], _OFF["ngate"] + 24)))
    mg = np.arange(_OFF["merge"], _OFF["merge"] + 2048)
    sh = {}
    def chunked(w, nch):
        return np.ascontiguousarray(w.reshape(8, 128, nch, 128).transpose(2, 1, 0, 3).reshape(nch, 128, 1024))
    sh["wfm"] = chunked(w_in[:, cols], 16)
    sh["wtm1"] = np.ascontiguousarray(w_in[:, tm1])
    sh["wtm2"] = np.ascontiguousarray(w_in[:, tm2])
    sh["wmg"] = chunked(w_in[:, mg], 16)
    sh["bfm"] = np.ascontiguousarray(b_in[cols].reshape(16, 128).T)
    sh["btm1"] = np.ascontiguousarray(b_in[tm1][None, :])
    sh["btm2"] = np.ascontiguousarray(b_in[tm2][None, :])
    sh["bmg"] = np.ascontiguousarray(b_in[mg].reshape(16, 128).T)
    bc = lambda v: np.ascontiguousarray(np.broadcast_to(np.asarray(v, np.float32)[None, :], (128, len(v))))
    sh["gmix"] = bc(inp["g_norm_mix"][0])
    sh["gffn"] = bc(inp["g_norm_ffn"][0])
    sh["gfin"] = bc(inp["g_final"])
    sh["bfg"] = bc(inp["b_fgate"][0])
    sh["ghead"] = bc(inp["g_mlstm_head"][0])
    cw = np.asarray(inp["conv_qk"][0], np.float32)
    sh["convw"] = np.ascontiguousarray(cw.T.reshape(8, 128, 4).transpose(1, 0, 2))
    sh["ident"] = np.eye(128, dtype=np.float32)
    jj, ii = np.meshgrid(np.arange(128), np.arange(128), indexing="ij")
    sh["negtri"] = np.where(jj <= ii, -1.0, 0.0).astype(np.float32)
    sh["negones"] = -np.ones((128, 128), np.float32)
    sh["maskT"] = np.where(ii >= jj, 1.0, 0.0).astype(np.float32)
    pe = np.asarray(inp["pe_cmp"][0], np.float32)
    peT = pe.transpose(2, 0, 1)
    sh["peT"] = np.ascontiguousarray(np.concatenate([peT, peT], 0))
    w1 = np.asarray(inp["w_cmp1"][0], np.float32).reshape(2, 32, 64, 128).transpose(2, 0, 1, 3)
    sh["w1"] = np.ascontiguousarray(np.concatenate([w1, w1], 0))
    w2 = np.asarray(inp["w_cmp2"][0], np.float32).transpose(1, 0, 2)
    sh["w2"] = np.ascontiguousarray(np.concatenate([w2, w2], 2))
    cs = np.arange(127) * 16
    ss = np.arange(32) * 64
    ov = np.clip(np.minimum(cs[:, None] + 32, ss[None, :] + 64) - np.maximum(cs[:, None], ss[None, :]), 0, None) / 16.0
    ovp = np.zeros((128, 32), np.float32)
    ovp[:127] = ov
    sh["overlap"] = ovp
    rb = np.asarray(inp["rel_bias"], np.float32)
    bucket = _rel_bucket_np()
    tbl = rb.T
    dist = np.arange(T)[None, :] - np.arange(128)[:, None]
    BS = np.full((8, 128, T), NEG, np.float32)
    ok = dist >= 0
    BS[:, ok] = tbl[:, bucket[dist[ok]]]
    sh["BS"] = BS
    d640 = dist[:, :640]
    BW = np.full((8, 128, 640), NEG, np.float32)
    okw = (d640 >= 0) & (d640 < 512)
    BW[:, okw] = tbl[:, bucket[d640[okw]]]
    sh["BW"] = BW
    dc = np.arange(T)[None, :] - (np.arange(128) * 16 + 31)[:, None]
    BC = np.full((8, 128, T), NEG, np.float32)
    okc = dc >= 0
    okc[127, :] = False
    BC[:, okc] = tbl[:, bucket[dc[okc]]]
    sh["BC"] = BC
    ex = np.zeros((32, T), np.float32)
    ex[np.arange(T) // 64, np.arange(T)] = 1.0
    sh["expand"] = ex
    cand = np.zeros((8, 128, 32), np.float32)
    forced = np.zeros((8, 128, 32), np.float32)
    for qi in range(8):
        t = (8 + qi) * 128 + np.arange(128)
        cur = t // 64
        jb = np.arange(32)[None, :]
        f = (jb == 0) | (jb == cur[:, None]) | (jb == cur[:, None] - 1)
        el = jb * 64 <= t[:, None]
        forced[qi] = f
        cand[qi] = el & ~f
    sh["cand"] = np.ascontiguousarray(cand.transpose(1, 0, 2))
    sh["negm"] = np.ascontiguousarray((cand - 1.0).transpose(1, 0, 2))
    sh["forced"] = np.ascontiguousarray(forced.transpose(1, 0, 2))
    wbr = np.asarray(inp["w_branch"][0], np.float32).reshape(2, 4, 128, 8, 128)
    sh["wbr"] = np.ascontiguousarray(wbr.transpose(0, 3, 2, 1, 4).reshape(16, 128, 512))
    sh["wout"] = np.ascontiguousarray(np.asarray(inp["w_out"][0], np.float32).reshape(8, 128, 1024).transpose(1, 0, 2).reshape(128, 8192))
    sh["wgate"] = chunked(np.asarray(inp["w_gate"][0], np.float32), 22)
    sh["wup"] = chunked(np.asarray(inp["w_up"][0], np.float32), 22)
    sh["wdown"] = np.ascontiguousarray(np.asarray(inp["w_down"][0], np.float32).reshape(22, 128, 2, 512).transpose(2, 1, 0, 3).reshape(2 * 128 * 22, 512))
    return sh


ARENA_BYTES = 124 * 1024
NQ_OFF = 3 * 2048 + 4480
import os as _os
B2STOP = int(_os.environ.get('B2STOP', '0'))
B2SUB = int(_os.environ.get('B2SUB', '0'))
B2TM = int(_os.environ.get('B2TM', '0'))


class Builder:
    def __init__(self, NS, dbg=None, stop_after=None):
        self.NS = NS
        self.dbg = dbg or ()
        self.stop_after = stop_after
        self.nc = bass.Bass("TRN2", target_bir_lowering=False)
        self.S = Sched(self.nc)
        self.es = ExitStack()
        self.dr = {}
        self.rr = 0
        self.rr2 = 0
        self.nrot = 6

    def din(self, name, shape, dt=F32):
        self.dr[name] = self.nc.dram_tensor(name, list(shape), dt, kind="ExternalInput").ap()
        return self.dr[name]

    def dout(self, name, shape, dt=F32):
        self.dr[name] = self.nc.dram_tensor(name, list(shape), dt, kind="ExternalOutput").ap()
        return self.dr[name]

    def sb(self, name, shape, dt):
        return self.es.enter_context(self.nc.sbuf_tensor(name, list(shape), dt))

    def arena_reset(self):
        if getattr(self, "aoff", 0):
            print("arena used", self.aoff, flush=True)
        self.aoff = 0
        self.aviews = {}

    def al(self, name, shape, dt, alias=None):
        isz = 4 if dt == F32 else 2
        n = 1
        for d in shape[1:]:
            n *= d
        nbytes = ((n * isz + 63) // 64) * 64
        if alias is not None:
            off = self.aviews[alias][1]
        else:
            off = self.aoff
            self.aoff += nbytes
            assert self.aoff <= ARENA_BYTES, (name, self.aoff)
        v = self.arena[0:shape[0], off // 2:off // 2 + (n * isz) // 2]
        if dt == F32:
            v = v.bitcast(F32)
        if len(shape) == 3:
            v = v.rearrange("p (a b) -> p a b", a=shape[1])
        elif len(shape) == 4:
            v = v.rearrange("p (a b c) -> p a b c", a=shape[1], b=shape[2])
        self.aviews[name] = (v, off, nbytes)
        return v

    def view_at(self, off, shape, dt):
        isz = 4 if dt == F32 else 2
        n = 1
        for d in shape[1:]:
            n *= d
        v = self.arena[0:shape[0], off // 2:off // 2 + (n * isz) // 2]
        if dt == F32:
            v = v.bitcast(F32)
        if len(shape) == 3:
            v = v.rearrange("p (a b) -> p a b", a=shape[1])
        return v

    def dma(self, q, out, in_, is_out=False):
        isd = lambda a: type(a.tensor).__name__ == "DRamTensorHandle" and not a.tensor.name.startswith("scr_")
        rd = [] if isd(in_) else [in_]
        wr = [] if isd(out) else [out]
        return self.S.op(q, lambda e: e.dma_start(out=out, in_=in_), reads=rd, writes=wr, dma=True, is_out=is_out)

    def mm(self, out, lhsT, rhs, start=True, stop=True):
        self.S.op("pe", lambda e: e.matmul(out, lhsT=lhsT, rhs=rhs, start=start, stop=stop),
                  reads=[lhsT, rhs], writes=[out])

    def tr(self, out, in_, f32=False):
        n = in_.shape[0]
        idn = (self.identf if f32 else self.identb)[0:n, 0:n]
        self.S.op("pe", lambda e: e.transpose(out=out, in_=in_, identity=idn), reads=[in_, idn], writes=[out])

    def act(self, out, in_, func, bias=None, scale=None, accum=None):
        kw = {}
        rd = [in_]
        wr = [out]
        if bias is not None:
            kw["bias"] = bias
            rd.append(bias)
        if scale is not None:
            kw["scale"] = scale
            rd.append(scale)
        if accum is not None:
            kw["accum_out"] = accum
            wr.append(accum)
        self.S.op("act", lambda e: e.activation(out=out, in_=in_, func=func, **kw), reads=rd, writes=wr)

    def tt(self, eng, out, in0, in1, op):
        self.S.op(eng, lambda e: e.tensor_tensor(out=out, in0=in0, in1=in1, op=op), reads=[in0, in1], writes=[out])

    def ts(self, eng, out, in0, s1, op0, s2=None, op1=None):
        if op1 is None:
            self.S.op(eng, lambda e: e.tensor_scalar(out=out, in0=in0, scalar1=s1, scalar2=None, op0=op0),
                      reads=[in0, s1], writes=[out])
        else:
            self.S.op(eng, lambda e: e.tensor_scalar(out=out, in0=in0, scalar1=s1, scalar2=s2, op0=op0, op1=op1),
                      reads=[in0, s1, s2], writes=[out])

    def stt(self, eng, out, in0, scalar, in1, op0, op1):
        self.S.op(eng, lambda e: e.scalar_tensor_tensor(out=out, in0=in0, scalar=scalar, in1=in1, op0=op0, op1=op1),
                  reads=[in0, scalar, in1], writes=[out])

    def copy(self, eng, out, in_):
        if eng == "act":
            self.S.op("act", lambda e: e.copy(out=out, in_=in_), reads=[in_], writes=[out])
        else:
            self.S.op(eng, lambda e: e.tensor_copy(out=out, in_=in_), reads=[in_], writes=[out])

    def recip(self, out, in_):
        self.S.op("dve", lambda e: e.reciprocal(out=out, in_=in_), reads=[in_], writes=[out])

    def memset(self, eng, ap, val):
        self.S.op(eng, lambda e: e.memset(ap, val), writes=[ap])

    def bank(self):
        b = self.ps[self.rr % self.nrot]
        self.rr += 1
        return b

    def bank_bf(self):
        return self.bank()[:].bitcast(BF16)

    def accbank(self):
        b = self.ps[6 + self.rr2 % 2]
        self.rr2 += 1
        return b

    def build(self):
        nc, S, NS = self.nc, self.S, self.NS
        din, sb = self.din, self.sb
        din("x", [NS * T, D])
        for n, shp in (("gmix", [128, D]), ("gffn", [128, D]), ("gfin", [128, D]), ("wfm", [16, 128, 1024]),
                       ("wtm1", [D, 1032]), ("wtm2", [D, 280]), ("wmg", [16, 128, 1024]), ("bfm", [128, 16]),
                       ("btm1", [1, 1032]), ("btm2", [1, 280]), ("bmg", [128, 16]), ("convw", [128, 8, 4]),
                       ("bfg", [128, 4]), ("ghead", [128, 512]), ("ident", [128, 128]), ("negtri", [128, 128]),
                       ("negones", [128, 128]), ("maskT", [128, 128]), ("peT", [128, 2, 32]),
                       ("w1", [128, 2, 32, 128]), ("w2", [128, 2, 128]), ("overlap", [128, 32]),
                       ("BS", [8, 128, T]), ("BW", [8, 128, 640]), ("BC", [8, 128, T]), ("expand", [32, T]),
                       ("cand", [128, 8, 32]), ("negm", [128, 8, 32]), ("forced", [128, 8, 32]),
                       ("wbr", [16, 128, 512]), ("wout", [128, 8192]), ("wgate", [22, 128, 1024]), ("wup", [22, 128, 1024]),
                       ("wdown", [2 * 128 * 22, 512])):
            din(n, shp)
        dr = self.dr
        out = self.dout("out", [NS * T, D])
        dbg_t = {}
        for n in self.dbg:
            shp = {"ymT": [NS, 128, 4, T], "ynT": [NS, 128, 4, T], "qk": [NS, 128, 8, T], "uT": [NS, 128, 8, T]}[n]
            dbg_t[n] = self.dout("dbg_" + n, shp, BF16)
        self.dbg_t = dbg_t

        self.ps = [self.es.enter_context(nc.psum_tensor("ps%d" % i, [128, 512], F32)) for i in range(8)]

        self.identb = sb("identb", [128, 128], BF16)
        self.dma("pool", self.identb[:], dr["ident"][:, :])
        cst = {}
        late = []
        for n, shp in (("gmix", [128, D]), ("gffn", [128, D]), ("gfin", [128, D]), ("bfm", [128, 16]),
                       ("bmg", [128, 16]), ("convw", [128, 8, 4]), ("bfg", [128, 4]), ("ghead", [128, 512]),
                       ("negtri", [128, 128]), ("negones", [128, 128]), ("maskT", [128, 128]),
                       ("ident", [128, 128]), ("overlap", [128, 32])):
            cst[n] = sb("c_" + n, shp, F32)
            if n in ("gmix", "bfm"):
                self.dma("sp", cst[n][:], dr[n])
            else:
                late.append(n)
        self.late_consts = late
        self.identf = cst["ident"]
        cst["bfm8"] = sb("c_bfm8", [128, 16], F32)
        self.ts("dve", cst["bfm8"][:], cst["bfm"][:], 0.125, ALU.mult)
        self.cst = cst
        ones_row = sb("ones_row", [1, 128], BF16)
        self.memset("dve", ones_row[:], 1.0)
        self.ones_row = ones_row
        self.btm1 = sb("s_btm1", [1, 1032], BF16)
        self.dma("pool", self.btm1[:], dr["btm1"][:, :])
        self.btm2 = sb("s_btm2", [1, 280], BF16)
        self.dma("pool", self.btm2[:], dr["btm2"][:, :])
        self.st = sb("stats", [128, 64], F32)

        self.uT = sb("s_uT", [128, 8, T], BF16)
        self.ymT = sb("s_ymT", [128, 4, T], BF16)
        self.ynT = sb("s_ynT", [128, 4, T], BF16)
        self.arena = sb("arena", [128, ARENA_BYTES // 2], BF16)

        self.scr = {}
        for n, shp in (("wmg", [16, 128, 1024]), ("wbr", [16, 128, 512]), ("wout", [128, 8192]),
                       ("wgate", [22, 128, 1024]), ("wup", [22, 128, 1024]), ("wdown", [2 * 128 * 22, 512])):
            self.scr[n] = nc.dram_tensor("scr_" + n, shp, BF16, kind="Internal").ap()
        stages = ("a", "b1", "b2", "c")
        last = stages.index(self.stop_after) if self.stop_after else 3
        for s in range(NS):
            self.phase_a(s)
            if "uT" in dbg_t:
                self.dma("sp", dbg_t["uT"][s], self.uT[:], is_out=True)
            if last >= 1:
                self.phase_b1(s)
                if "ymT" in dbg_t:
                    self.dma("sp", dbg_t["ymT"][s], self.ymT[:], is_out=True)
            if last >= 2:
                self.phase_b2(s)
                if "ynT" in dbg_t:
                    self.dma("sp", dbg_t["ynT"][s], self.ynT[:], is_out=True)
            if last >= 3:
                self.phase_c(s)

        if last < 3:
            self.arena_reset()
            zt = self.al("zt", [128, D], F32)
            self.memset("dve", zt[:], 0.0)
            for i in range(NS * NT):
                self.dma("sp", out[i * 128:(i + 1) * 128, :], zt[:], is_out=True)
        print("arena used (last phase)", self.aoff, "sbuf remaining", nc.sbuf_bytes_remaining, flush=True)
        print(S.analyze(), flush=True)
        S.emit()
        self.es.close()
        return nc

    def prologue_list(self):
        dr, scr = self.dr, self.scr
        lst = []
        for c in range(0, 16, 2):
            lst.append((scr["wmg"][c:c + 2].rearrange("c p n -> (c p) n"), dr["wmg"][c:c + 2].rearrange("c p n -> (c p) n")))
        for c in range(0, 16, 4):
            lst.append((scr["wbr"][c:c + 4].rearrange("c p n -> (c p) n"), dr["wbr"][c:c + 4].rearrange("c p n -> (c p) n")))
        for k in range(0, 8, 2):
            lst.append((scr["wout"][:, k * 1024:(k + 2) * 1024].rearrange("p (k n) -> p k n", k=2),
                        dr["wout"][:, k * 1024:(k + 2) * 1024].rearrange("p (k n) -> p k n", k=2)))
        for c in range(0, 22, 2):
            lst.append((scr["wgate"][c:c + 2].rearrange("c p n -> (c p) n"), dr["wgate"][c:c + 2].rearrange("c p n -> (c p) n")))
            lst.append((scr["wup"][c:c + 2].rearrange("c p n -> (c p) n"), dr["wup"][c:c + 2].rearrange("c p n -> (c p) n")))
        for i in range(0, 5632, 704):
            lst.append((scr["wdown"][i:i + 704, :], dr["wdown"][i:i + 704, :]))
        return lst

    def prologue_some(self, n):
        for _ in range(n):
            if self.pl:
                o, i = self.pl.pop(0)
                self.dma("pool", o, i)

    def phase_a(self, s):
        self.arena_reset()
        x = self.dr["x"]
        xbuf = [self.al("xbuf%d" % i, [128, D], F32) for i in range(3)]
        xnb = [self.al("xnb%d" % i, [128, D], BF16) for i in range(2)]
        junkb = self.al("junkb", [128, D], BF16)
        st = self.st
        def stA(i):
            xt = xbuf[i % 3]
            xn = xnb[i % 2]
            r0 = s * T + i * 128
            self.dma("sp", xt[:], x[r0:r0 + 128, :])
            c = i % 2
            self.act(junkb[:], xt[:], AF.Square, accum=st[:, c:c + 1])
            self.act(st[:, 2 + c:3 + c], st[:, c:c + 1], AF.Sqrt, bias=1e-6, scale=1.0 / D)
            self.recip(st[:, 4 + c:5 + c], st[:, 2 + c:3 + c])
            self.stt("dve", xn[:], xt[:], st[:, 4 + c:5 + c], self.cst["gmix"][:], ALU.mult, ALU.mult)

        def stB(i):
            xn = xnb[i % 2]
            pb = self.bank_bf()
            for k in range(8):
                self.tr(pb[:, k * 128:(k + 1) * 128], xn[:, k * 128:(k + 1) * 128])
            self.copy("act" if i % 2 else "dve", self.uT[:, :, i * 128:(i + 1) * 128],
                      pb.rearrange("p (k n) -> p k n", k=8))

        stA(0)
        for i in range(NT):
            if i + 1 < NT:
                stA(i + 1)
            if i == 2 and self.late_consts:
                for n in self.late_consts:
                    self.dma("sp", self.cst[n][:], self.dr[n])
                self.late_consts = []
            stB(i)

    def fm_proj(self, c, wb, evac, spans=range(4), load=True):
        if load:
            self.dma("pool", wb[:].rearrange("p k n -> p (k n)"), self.dr["wfm"][c])
        for sp in spans:
            ps = self.bank()
            for k in range(8):
                self.mm(ps[:], wb[:, k, :], self.uT[:, k, sp * 512:(sp + 1) * 512], start=(k == 0), stop=(k == 7))
            evac(sp, ps)

    def phase_b1(self, s):
        self.arena_reset()
        al = self.al
        dr, cst, uT = self.dr, self.cst, self.uT
        wfmb = [al("wfmb%d" % i, [128, 8, 128], BF16) for i in range(3)]
        pre = [al("pre%d" % i, [128, 3 + T], F32) for i in range(2)]
        cacc = al("cacc", [128, T], F32)
        qk = al("qk", [128, 8, T], BF16)
        og_all = al("og_all", [128, NT, 512], BF16)
        ogtmp = [al("ogtmp%d" % i, [128, 512], F32) for i in range(2)]
        wtm1b = al("wtm1b", [128, 8, 1032], BF16)
        gsb = al("m_g", [128, 2, 64], F32)
        vexts = [al("m_vext%d" % i, [128, 4, 129], BF16) for i in range(2)]
        ATs = [al("m_AT%d" % i, [128, 4, 128], BF16) for i in range(2)]
        kTs = [al("m_kT%d" % i, [128, 4, 128], BF16) for i in range(2)]
        C32 = al("m_C32", [128, 4, 129], F32)
        Cb = al("m_Cb", [128, 4, 129], BF16)
        hhs = [al("m_hh%d" % i, [128, 4, 128], F32) for i in range(2)]
        junk = al("m_junk", [128, 128], F32)
        yms = [al("m_ym%d" % i, [128, 512], BF16) for i in range(2)]
        btm1 = self.btm1
        for p in pre:
            self.memset("dve", p[:, 0:3], 0.0)
        wtm1 = dr["wtm1"].rearrange("(k p) n -> p k n", p=128)
        for k in range(8):
            self.dma("pool", wtm1b[:, k, :], wtm1[:, k, :])
        for c in range(8):
            prb = pre[c % 2]

            def evac(sp, ps, prb=prb, c=c):
                self.act(prb[:, 3 + sp * 512:3 + (sp + 1) * 512], ps[:], AF.Identity, bias=cst["bfm"][:, c:c + 1])
            self.fm_proj(c, wfmb[c % 3], evac)
            cw = cst["convw"]
            self.ts("dve", cacc[:], prb[:, 0:T], cw[:, c, 0:1], ALU.mult)
            for j in range(1, 4):
                self.stt("dve", cacc[:], prb[:, j:j + T], cw[:, c, j:j + 1], cacc[:], ALU.mult, ALU.add)
            self.act(qk[:, c, :], cacc[:], AF.Silu)
        if "qk" in self.dbg_t:
            self.dma("sp", self.dbg_t["qk"][s], qk[:], is_out=True)
        for i in range(NT):
            tsl = slice(i * 128, (i + 1) * 128)
            ps = self.bank()
            for k in range(8):
                self.mm(ps[:], uT[:, k, tsl], wtm1b[:, k, 512:1024], start=(k == 0), stop=False)
            self.mm(ps[:], self.ones_row[0:1, :], btm1[0:1, 512:1024], start=False, stop=True)
            ogt = ogtmp[i % 2]
            self.act(ogt[:], ps[:], AF.Sigmoid)
            self.tt("dve", og_all[:, i, :], ogt[:], cst["ghead"][:], ALU.mult)
        live = {}

        def P1(c):
            tsl = slice(c * 128, c * 128 + 128)
            par = c % 2
            g = gsb[:, par, :]
            ps_v, ps_g = self.ps[6 + par], self.bank()
            for (ps, n0, nn) in ((ps_v, 0, 512), (ps_g, 1024, 8)):
                for k in range(8):
                    self.mm(ps[:, 0:nn], uT[:, k, tsl], wtm1b[:, k, n0:n0 + nn], start=(k == 0), stop=False)
                self.mm(ps[:, 0:nn], self.ones_row[0:1, :], btm1[0:1, n0:n0 + nn], start=False, stop=True)
            self.tt("dve", g[:, 0:4], ps_g[:, 4:8], cst["bfg"][:], ALU.add)
            self.copy("dve", g[:, 4:8], ps_g[:, 0:4])
            self.act(g[:, 44:48], g[:, 0:4], AF.Exp, scale=-1.0)
            self.act(g[:, 48:52], g[:, 44:48], AF.Ln, bias=1.0)
            ps_s = self.bank()
            for h in range(4):
                self.mm(ps_s[:, h * 128:(h + 1) * 128], qk[:, 4 + h, tsl], qk[:, h, tsl])
            self.tt("dve", ATs[par][:], ps_s[:].rearrange("p (h n) -> p h n", h=4),
                    cst["maskT"][:].unsqueeze(1).to_broadcast([128, 4, 128]), ALU.mult)
            if c < NT - 1:
                ps_t = self.bank_bf()
                for h in range(4):
                    self.tr(ps_t[:, h * 128:(h + 1) * 128], qk[:, 4 + h, tsl])
                self.copy("act", kTs[par][:], ps_t[:, 0:512].rearrange("p (h n) -> p h n", h=4))
            live[c] = ps_v

        def P2a(c):
            par = c % 2
            g = gsb[:, par, :]
            ps_v = live.pop(c)
            ps_c = self.bank()
            self.mm(ps_c[:, 0:4], cst["negtri"][:], g[:, 48:52])
            self.mm(ps_c[:, 4:8], cst["negones"][:], g[:, 48:52])
            self.tt("dve", g[:, 8:12], g[:, 4:8], ps_c[:, 0:4], ALU.subtract)
            self.copy("dve", g[:, 16:24], ps_c[:, 0:8])
            self.act(g[:, 12:16], g[:, 8:12], AF.Exp, bias=LN_C)
            self.act(g[:, 16:24], g[:, 16:24], AF.Exp)
            vext = vexts[par]
            self.tt("dve", vext[:, :, 0:128], ps_v[:].rearrange("p (h n) -> p h n", h=4),
                    g[:, 12:16].unsqueeze(2).to_broadcast([128, 4, 128]), ALU.mult)
            self.copy("dve", vext[:, :, 128:129], g[:, 12:16].unsqueeze(2))

        def P2b(c):
            tsl = slice(c * 128, c * 128 + 128)
            par = c % 2
            g = gsb[:, par, :]
            vext, AT, kT = vexts[par], ATs[par], kTs[par]
            ps_n = [self.bank(), self.bank()]
            for h in range(4):
                pn = ps_n[h // 2][:, (h % 2) * 129:(h % 2) * 129 + 129]
                self.mm(pn, AT[:, h, :], vext[:, h, :], start=(h % 2 == 0), stop=(c == 0))
                if c > 0:
                    self.mm(pn, qk[:, h, tsl], Cb[:, h, :], start=False, stop=True)
            if c < NT - 1:
                ps_u = [self.bank(), self.bank()]
                for h in range(4):
                    pu = ps_u[h // 2][:, (h % 2) * 129:(h % 2) * 129 + 129]
                    self.mm(pu, kT[:, h, :], vext[:, h, :], start=(h % 2 == 0), stop=True)
                for hp in range(2):
                    cs = C32[:, 2 * hp:2 * hp + 2, :]
                    pu = ps_u[hp][:, 0:258].rearrange("p (h n) -> p h n", h=2)
                    ebb = g[:, 20 + 2 * hp:22 + 2 * hp].unsqueeze(2).to_broadcast([128, 2, 129])
                    if c == 0:
                        self.tt("dve", cs, pu, ebb, ALU.mult)
                    else:
                        self.tt("dve", cs, cs, pu, ALU.add)
                        self.tt("dve", cs, cs, ebb, ALU.mult)
                self.copy("act", Cb[:], C32[:])
            for hp in range(2):
                pn3 = ps_n[hp][:, 0:258].rearrange("p (h n) -> p h n", h=2)
                self.tt("dve", g[:, 24 + 2 * hp:26 + 2 * hp].unsqueeze(2), pn3[:, :, 128:129],
                        g[:, 16 + 2 * hp:18 + 2 * hp].unsqueeze(2), ALU.mult)
            self.ts("dve", g[:, 56:60], g[:, 24:28], 1.0, ALU.max)
            self.stt("dve", g[:, 28:32], g[:, 24:28], -1.0, g[:, 56:60], ALU.mult, ALU.max)
            self.recip(g[:, 60:64], g[:, 28:32])
            self.tt("dve", g[:, 32:36], g[:, 16:20], g[:, 60:64], ALU.mult)
            hh = hhs[par]
            for hp in range(2):
                pn3 = ps_n[hp][:, 0:258].rearrange("p (h n) -> p h n", h=2)
                self.tt("dve", hh[:, 2 * hp:2 * hp + 2, :], pn3[:, :, 0:128],
                        g[:, 32 + 2 * hp:34 + 2 * hp].unsqueeze(2).to_broadcast([128, 2, 128]), ALU.mult)
            for h in range(4):
                self.act(junk[:], hh[:, h, :], AF.Square, accum=g[:, 36 + h:37 + h])
            self.act(g[:, 40:44], g[:, 36:40], AF.Ln, bias=1e-6, scale=1.0 / 128)
            self.act(g[:, 52:56], g[:, 40:44], AF.Exp, scale=-0.5)
            for h in range(4):
                self.stt("dve", yms[par][:, h * 128:(h + 1) * 128], hh[:, h, :], g[:, 52 + h:53 + h],
                         og_all[:, c, h * 128:(h + 1) * 128], ALU.mult, ALU.mult)

        def P3t(c):
            tsl = slice(c * 128, c * 128 + 128)
            ym = yms[c % 2]
            ps_y = self.bank_bf()
            for h in range(4):
                self.tr(ps_y[:, h * 128:(h + 1) * 128], ym[:, h * 128:(h + 1) * 128])
            self.copy("act", self.ymT[:, :, tsl], ps_y[:, 0:512].rearrange("p (h n) -> p h n", h=4))

        assert self.aviews["cacc"][1] + self.aviews["cacc"][2] >= NQ_OFF + 4 * T * 2
        assert self.aviews["qk"][1] >= NQ_OFF + 4 * T * 2
        nq_e = self.view_at(NQ_OFF, [128, 4, T], BF16)

        def early_nq(it):
            j, half = it // 2, it % 2

            def evac(sp, ps, j=j):
                self.act(nq_e[:, j, sp * 512:(sp + 1) * 512], ps[:], AF.Identity,
                         bias=cst["bfm8"][:, 8 + j:9 + j], scale=0.125)
            self.fm_proj(8 + j, wfmb[(8 + j) % 3], evac, spans=range(2 * half, 2 * half + 2), load=(half == 0))

        P1(0)
        for c in range(NT):
            P2a(c)
            if 2 <= c < 10:
                early_nq(c - 2)
            if c + 1 < NT:
                P1(c + 1)
            if c >= 1:
                P3t(c - 1)
            P2b(c)
        P3t(NT - 1)

    def phase_b2(self, s):
        self.arena_reset()
        al = self.al
        dr, cst, uT = self.dr, self.cst, self.uT
        wfmb = [al("wfmb%d" % i, [128, 8, 128], BF16) for i in range(3)]
        wtm2b = al("wtm2b", [128, 8, 280], BF16)
        nq = al("nq", [128, 4, T], BF16)
        kcT = al("kcT", [128, T], BF16)
        vcT = al("vcT", [128, T], BF16)
        ksa = [al("ksa%d" % i, [128, T], BF16) for i in range(2)]
        kwz = [al("kwz%d" % i, [128, T], BF16) for i in range(2)]
        qa = [al("qa%d" % i, [128, T], BF16) for i in range(2)]
        vsx = al("vsx", [128, NT, 2, 65], BF16)
        vwx = al("vwx", [128, NT, 2, 65], BF16)
        gat = al("gat", [128, NT, 24], F32)
        oc = al("oc", [128, NT, 4, 64], F32)
        w1b = al("w1b", [128, 2, 32, 128], BF16, alias="oc")
        impg = al("impg", [128, NT, 32], F32)
        w2b = al("w2b", [128, 2, 128], BF16)
        peTb = al("peTb", [128, 2, 32], BF16)
        c0 = al("c0", [128, 2], F32)
        shid = [al("shid%d" % i, [128, 128], BF16) for i in range(2)]
        kcmpz = [al("kcmpz%d" % i, [128, 128], BF16) for i in range(2)]
        vcx = al("vcx", [128, 2, 97], BF16)
        bland = al("bland", [128, T], F32)
        EB = [al("EB%d" % i, [128, T], BF16) for i in range(2)]
        bw16 = [al("bw16_%d" % i, [128, 640], BF16) for i in range(2)]
        e0b = [al("e0b%d" % i, [128, 512], BF16) for i in range(4)]
        eTb = [al("eTb%d" % i, [128, 512], BF16) for i in range(5)]
        tkc = al("tkc", [128, 3, 8, 32], F32)
        tk = al("tk", [128, 128], F32)
        selb = al("selb", [128, 128], BF16)
        sm = al("sm", [128, 32], F32)
        tmpo = [al("tmpo%d" % i, [128, 4, 64], F32) for i in range(2)]
        tmpi = al("tmpi", [128, 4, 32], F32)
        btm2 = self.btm2

        wtm2 = dr["wtm2"].rearrange("(k p) n -> p k n", p=128)
        for k in range(8):
            self.dma("pool", wtm2b[:, k, :], wtm2[:, k, :])
        if not (B2SUB & 2):
            for i, n in enumerate(("cand", "negm", "forced")):
                self.dma("sp", tkc[:, i, :, :], dr[n])
        if not (B2SUB & 4):
            self.memset("dve", vsx[:, :, :, 64:65], 1.0)
            self.memset("dve", vwx[:, :, :, 64:65], 1.0)
            self.memset("dve", vcx[:], 0.0)
            self.memset("dve", vcx[:, :, 96:97], 1.0)
            for g in range(2):
                self.copy("dve", vcx[:, g, 64:96], cst["overlap"][:])

        dests = {12: kcT, 13: vcT}
        dests2 = {14: ksa, 15: kwz}
        assert self.aviews["nq"][1] == NQ_OFF
        for c in range(12, 16):
            if B2SUB & 8:
                break
            if c < 12:
                def evac(sp, ps, c=c):
                    self.act(nq[:, c - 8, sp * 512:(sp + 1) * 512], ps[:], AF.Identity,
                             bias=cst["bfm8"][:, c:c + 1], scale=0.125)
            elif c < 14:
                def evac(sp, ps, c=c):
                    self.act(dests[c][:, sp * 512:(sp + 1) * 512], ps[:], AF.Identity, bias=cst["bfm"][:, c:c + 1])
            else:
                def evac(sp, ps, c=c):
                    for g_ in range(2):
                        gs_ = slice(g_ * 64, (g_ + 1) * 64)
                        self.act(dests2[c][g_][gs_, sp * 512:(sp + 1) * 512], ps[gs_, :], AF.Identity,
                                 bias=cst["bfm"][gs_, c:c + 1])
            self.fm_proj(c, wfmb[c % 3], evac)
        if not (B2SUB & 1):
            for j in range(2):
                for lh in range(4):
                    self.dma("pool", w1b[:, j, lh * 8:(lh + 1) * 8, :], dr["w1"][:, j, lh * 8:(lh + 1) * 8, :])
            self.dma("pool", w2b[:], dr["w2"])
            self.dma("pool", peTb[:], dr["peT"])
            self.memset("dve", ksa[0][64:128, :], 0.0)
            self.memset("dve", ksa[1][0:64, :], 0.0)
            self.memset("dve", kwz[0][64:128, :], 0.0)
            self.memset("dve", kwz[1][0:64, :], 0.0)
            self.memset("dve", kcmpz[0][:], 0.0)
            self.memset("dve", kcmpz[1][:], 0.0)
            self.memset("dve", selb[:], 0.0)
            for hh_ in range(2):
                hs_ = slice(hh_ * 1024, (hh_ + 1) * 1024)
                self.dma("pool", ksa[0][64:96, hs_], dr["expand"][:, hs_])
                self.dma("pool", ksa[1][0:32, hs_], dr["expand"][:, hs_])
        self.pl = self.prologue_list() if (s == 0 and self.stop_after in (None, "c")) else []
        for i in range(NT):
            if B2SUB & 16:
                break
            tsl = slice(i * 128, (i + 1) * 128)
            ps = self.bank()
            for k in range(8):
                self.mm(ps[:, 0:280], uT[:, k, tsl], wtm2b[:, k, :], start=(k == 0), stop=False)
            self.mm(ps[:, 0:280], self.ones_row[0:1, :], btm2[0:1, :], start=False, stop=True)
            if not (B2TM & 1):
                self.copy("act", vsx[:, i, :, 0:64], ps[:, 0:128].rearrange("p (g d) -> p g d", g=2))
            if not (B2TM & 2):
                self.copy("dve", vwx[:, i, :, 0:64], ps[:, 128:256].rearrange("p (g d) -> p g d", g=2))
            if not (B2TM & 4):
                self.act(gat[:, i, :], ps[:, 256:280], AF.Sigmoid)

        if B2STOP == 1:
            return
        for j in range(2):
            ps = self.bank()
            for l in range(32):
                self.mm(ps[:, 0:1], w1b[0:64, j, l, :], peTb[0:64, j, l:l + 1], start=(l == 0), stop=(l == 31))
            self.copy("dve", c0[:, j:j + 1], ps[:, 0:1])
        for j in range(2):
            src = kcT if j == 0 else vcT
            for g in range(2):
                ps = self.bank()
                for l in range(32):
                    self.mm(ps[:, 0:127], w1b[g * 64:(g + 1) * 64, j, l, :],
                            src[g * 64:(g + 1) * 64, l:l + 16 * 126 + 1:16], start=(l == 0), stop=(l == 31))
                sh = shid[(2 * j + g) % 2]
                self.act(sh[:, 0:127], ps[:, 0:127], AF.Silu, bias=c0[:, j:j + 1])
                ps2 = self.bank()
                if j == 0:
                    self.mm(ps2[:, 0:127], w2b[:, 0, :], sh[:, 0:127])
                    self.copy("dve", kcmpz[g][g * 64:(g + 1) * 64, 0:127], ps2[g * 64:(g + 1) * 64, 0:127])
                else:
                    self.mm(ps2[0:127, 0:64], sh[:, 0:127], w2b[:, 1, 0:64])
                    self.copy("dve", vcx[0:127, g, 0:64], ps2[0:127, 0:64])

        if B2STOP == 2:
            return
        cnt = {"ne": 0}
        jobs = []
        for g_ in range(2):
            jobs += [("c", g_, r_) for r_ in range(4)] + [("s", g_, r_) for r_ in range(4)]

        def pf_dma(ji):
            if ji >= len(jobs):
                return
            kind, g_, r_ = jobs[ji]
            h_ = g_ * 4 + r_
            if kind == "c":
                self.dma("sp", bland[:], dr["BC"][h_])
            else:
                self.dma("sp", bland[:], dr["BS"][h_])
                self.dma("pool", bw16[ji % 2][:], dr["BW"][h_])

        def pf_exp(ji):
            self.prologue_some(3)
            if ji >= len(jobs):
                return
            kind = jobs[ji][0]
            self.act(EB[ji % 2][:], bland[:], AF.Exp)

        pf_dma(0)
        pf_exp(0)
        self.prologue_some(2)

        def run_steps(steps, lag=4):
            n = len(steps)
            for i in range(n + lag):
                if i < n:
                    steps[i][0]()
                if i - lag >= 0:
                    steps[i - lag][1]()

        for g in range(2):
            gs = slice(g * 64, (g + 1) * 64)
            steps = []
            for r in range(4):
                h = g * 4 + r
                ji = jobs.index(("c", g, r))
                for sp in range(4):
                    box = {}

                    def fS(r=r, h=h, sp=sp, ji=ji, box=box):
                        if sp == 0:
                            pf_dma(ji + 1)
                        if sp == 2:
                            pf_exp(ji + 1)
                        ssl = slice(sp * 512, (sp + 1) * 512)
                        ps = self.bank()
                        self.mm(ps[0:127, :], kcmpz[g][:, 0:127], nq[:, r, ssl])
                        e0 = e0b[cnt["ne"] % 4]
                        eT = eTb[cnt["ne"] % 5]
                        cnt["ne"] += 1
                        self.act(e0[0:127, :], ps[0:127, :], AF.Exp)
                        self.tt("dve", eT[0:127, :], e0[0:127, :], EB[ji % 2][0:127, ssl], ALU.mult)
                        box["eT"] = eT

                    def fP(r=r, h=h, sp=sp, box=box):
                        eT = box["eT"]
                        ps2 = self.bank()
                        for q in range(4):
                            self.mm(ps2[:, q * 97:(q + 1) * 97], eT[0:127, q * 128:(q + 1) * 128], vcx[0:127, g, :])
                        p3 = ps2[:, 0:388].rearrange("p (q n) -> p q n", q=4)
                        tq = slice(sp * 4, sp * 4 + 4)
                        self.ts("dve", tk[:, 0:4].unsqueeze(2), p3[:, :, 96:97], 1e-30, ALU.max)
                        self.recip(tk[:, 4:8], tk[:, 0:4])
                        self.tt("dve", tk[:, 8:12].unsqueeze(2), tk[:, 4:8].unsqueeze(2), gat[:, tq, h * 3:h * 3 + 1], ALU.mult)
                        self.tt("dve", oc[:, tq, r, :], p3[:, :, 0:64],
                                tk[:, 8:12].unsqueeze(2).to_broadcast([128, 4, 64]), ALU.mult)
                        rb = tk[:, 4:8].unsqueeze(2).to_broadcast([128, 4, 32])
                        if r == 0:
                            self.tt("dve", impg[:, tq, :], p3[:, :, 64:96], rb, ALU.mult)
                        else:
                            self.tt("dve", tmpi[:], p3[:, :, 64:96], rb, ALU.mult)
                            self.tt("dve", impg[:, tq, :], impg[:, tq, :], tmpi[:], ALU.add)
                    steps.append((fS, fP))
            run_steps(steps)
            if B2STOP == 3:
                return
            mrows = slice(64, 96) if g == 0 else slice(0, 32)
            orows = slice(64, 128) if g == 0 else slice(0, 64)
            mc0 = 64 if g == 0 else 0
            for b_ in range(2):
                self.memset("dve", qa[b_][orows, :], 0.0)
            for qi in range(8):
                qt = 8 + qi
                wk = tk[:, 16:48]
                self.tt("dve", wk, impg[:, qt, :], tkc[:, 0, qi, :], ALU.mult)
                self.tt("dve", wk, wk, tkc[:, 1, qi, :], ALU.add)
                self.S.op("dve", lambda e, wk=wk: e.max(out=tk[:, 48:56], in_=wk), reads=[wk], writes=[tk[:, 48:56]])
                wk2 = tk[:, 64:96]
                self.S.op("dve", lambda e, wk=wk, wk2=wk2: e.match_replace(out=wk2, in_to_replace=tk[:, 48:56],
                                                                            in_values=wk, imm_value=-1.0),
                          reads=[wk, tk[:, 48:56]], writes=[wk2])
                self.S.op("dve", lambda e, wk2=wk2: e.max(out=tk[:, 56:64], in_=wk2), reads=[wk2], writes=[tk[:, 56:64]])
                self.ts("dve", sm[:], wk, tk[:, 60:61], ALU.is_ge)
                self.tt("dve", sm[:], sm[:], tkc[:, 2, qi, :], ALU.add)
                self.ts("dve", selb[:, mc0:mc0 + 32], sm[:], -1.0, ALU.add, -NEG, ALU.mult)
                pst = self.bank_bf()
                self.tr(pst[:, 0:128], selb[:])
                for b_ in range(2):
                    self.copy("act", qa[b_][mrows, qt * 128:(qt + 1) * 128], pst[mrows, 0:128])
            if B2STOP == 4:
                return
            steps = []
            self.copy("dve", qa[0][gs, :], nq[gs, 0, :])
            for r in range(4):
                h = g * 4 + r
                ji = jobs.index(("s", g, r))
                si = 0
                for branch in range(2):
                    kT_ = ksa[g] if branch == 0 else kwz[g]
                    vx = vsx if branch == 0 else vwx
                    bias = EB[ji % 2] if branch == 0 else bw16[ji % 2]
                    for qs in range(4):
                        q_lo, q_hi = 4 * qs, 4 * qs + 3
                        kt_lo = 0 if branch == 0 else max(0, q_lo - 4)
                        span = {}
                        for kt in range(kt_lo, q_hi + 1):
                            q0 = max(q_lo, kt)
                            q1 = q_hi if branch == 0 else min(q_hi, kt + 4)
                            box = {}

                            def fS(r=r, h=h, branch=branch, qs=qs, kt=kt, q0=q0, q1=q1, kT_=kT_, bias=bias,
                                   ji=ji, si=si, box=box):
                                if si == 0:
                                    pf_dma(ji + 1)
                                    if r + 1 < 4:
                                        self.copy("dve", qa[(r + 1) % 2][gs, :], nq[gs, r + 1, :])
                                if si == 6:
                                    pf_exp(ji + 1)
                                n = (q1 - q0 + 1) * 128
                                ksl = slice(kt * 128, (kt + 1) * 128)
                                qsl = slice(q0 * 128, (q1 + 1) * 128)
                                ps = self.bank()
                                qsrc = qa[r % 2][:, qsl] if branch == 0 else nq[:, r, qsl]
                                b0 = (q0 - kt) * 128
                                e0 = e0b[cnt["ne"] % 4]
                                eT = eTb[cnt["ne"] % 5]
                                cnt["ne"] += 1
                                if branch == 0:
                                    self.mm(ps[:, 0:n], kT_[:, ksl], qsrc)
                                    self.act(e0[:, 0:n], ps[:, 0:n], AF.Exp)
                                    self.tt("dve", eT[:, 0:n], e0[:, 0:n], bias[:, b0:b0 + n], ALU.mult)
                                else:
                                    self.mm(ps[:, 0:n], kT_[:, ksl], qsrc, start=True, stop=False)
                                    self.mm(ps[:, 0:n], self.identb[:], bias[:, b0:b0 + n], start=False, stop=True)
                                    self.act(eT[:, 0:n], ps[:, 0:n], AF.Exp)
                                box["eT"] = eT

                            def fP(r=r, h=h, branch=branch, qs=qs, kt=kt, q0=q0, q1=q1, vx=vx, box=box, span=span,
                                   q_lo=q_lo, q_hi=q_hi, kt_lo=kt_lo):
                                eT = box["eT"]
                                if kt == kt_lo:
                                    span["pacc"] = self.accbank()
                                pacc = span["pacc"]
                                for qt in range(q0, q1 + 1):
                                    self.mm(pacc[:, (qt - q_lo) * 65:(qt - q_lo) * 65 + 65],
                                            eT[:, (qt - q0) * 128:(qt - q0 + 1) * 128], vx[:, kt, g, :],
                                            start=(kt == kt_lo and qt == q0), stop=(kt == qt))
                                if kt == q_hi:
                                    p4 = pacc[:, 0:260].rearrange("p (q n) -> p q n", q=4)
                                    tq = slice(q_lo, q_lo + 4)
                                    o = 96 + branch * 16
                                    self.recip(tk[:, o + 4:o + 8].unsqueeze(2), p4[:, :, 64:65])
                                    self.tt("dve", tk[:, o + 8:o + 12].unsqueeze(2), tk[:, o + 4:o + 8].unsqueeze(2),
                                            gat[:, tq, h * 3 + 1 + branch:h * 3 + 2 + branch], ALU.mult)
                                    for q_ in range(4):
                                        self.stt("dve", oc[:, q_lo + q_, r, :], p4[:, q_, 0:64], tk[:, o + 8 + q_:o + 9 + q_],
                                                 oc[:, q_lo + q_, r, :], ALU.mult, ALU.add)
                            steps.append((fS, fP))
                            si += 1
            run_steps(steps)
            if B2STOP == 5:
                return
            for i in range(NT):
                if i % 2 == 0:
                    psy = self.bank()
                for p in range(2):
                    col = ((i % 2) * 2 + p) * 128
                    self.tr(psy[:, col:col + 128], oc[:, i, 2 * p:2 * p + 2, :].rearrange("p a b -> p (a b)"), f32=True)
                if i % 2 == 1:
                    i0 = (i - 1) * 128
                    self.copy("act" if (i // 2) % 2 else "dve",
                              self.ynT[:, 2 * g:2 * g + 2, i0:i0 + 256].rearrange("p c (i n) -> p i c n", i=2),
                              psy[:].rearrange("p (i c n) -> p i c n", i=2, c=2))

    def phase_c(self, s):
        self.prologue_some(1000)
        self.arena_reset()
        al = self.al
        dr, cst, uT = self.dr, self.cst, self.uT
        x, out = dr["x"], dr["out"]
        wdp = [al("wdp%d" % i, [128, 11, 512], BF16) for i in range(2)]
        self.aoff_save = self.aoff
        self.aoff = self.aviews["wdp0"][1]
        wmgb = [al("wmgb%d" % i, [128, 8, 128], BF16) for i in range(4)]
        wbrb = [al("wbrb%d" % i, [128, 4, 128], BF16) for i in range(4)]
        assert self.aoff <= self.aoff_save
        self.aoff = self.aoff_save
        mixT = al("mixT", [128, 8, 512], BF16)
        woutb = al("woutb", [128, 8, D], BF16)
        xt = [al("xt%d" % i, [128, D], F32) for i in range(2)]
        h2 = al("h2", [128, 4, D], F32)
        fT = al("fT", [128, 8, 512], BF16)
        aT = al("aT", [128, 22, 512], BF16)
        wgb = [al("wgb%d" % i, [128, 8, 128], BF16) for i in range(3)]
        wub = [al("wub%d" % i, [128, 8, 128], BF16) for i in range(3)]
        sgb = [al("sgb%d" % i, [128, 512], F32) for i in range(4)]
        fnb = [al("fnb%d" % i, [128, D], BF16) for i in range(2)]
        junkb = al("junkc", [128, D], BF16, alias="sgb3")
        st = self.st
        scr = self.scr
        wdown = scr["wdown"].rearrange("(h p c) n -> h p (c n)", h=2, p=128)
        yT = (self.ymT, self.ynT)
        self.dma("sp", woutb[:].rearrange("p k n -> p (k n)"), scr["wout"])
        cn = {"nsg": 0}

        def c1(sti, dcs):
            csl = slice(sti * 512, sti * 512 + 512)
            for dc in dcs:
                sgs = []
                for n in range(2):
                    wb = wmgb[(2 * dc + n) % 4]
                    self.dma("sp", wb[:].rearrange("p k n -> p (k n)"), scr["wmg"][n * 8 + dc])
                    ps = self.bank()
                    for k in range(8):
                        self.mm(ps[:], wb[:, k, :], uT[:, k, csl], start=(k == 0), stop=(k == 7))
                    sg = sgb[cn["nsg"] % 3]
                    cn["nsg"] += 1
                    self.act(sg[:], ps[:], AF.Sigmoid, bias=cst["bmg"][:, n * 8 + dc:n * 8 + dc + 1])
                    sgs.append(sg)
                for n in range(2):
                    wb = wbrb[(2 * dc + n) % 4]
                    self.dma("sp", wb[:].rearrange("p k n -> p (k n)"), scr["wbr"][n * 8 + dc])
                    ps = self.bank()
                    for k in range(4):
                        self.mm(ps[:], wb[:, k, :], yT[n][:, k, csl], start=(k == 0), stop=(k == 3))
                    self.tt("dve", sgs[n][:], sgs[n][:], ps[:], ALU.mult)
                self.tt("dve", mixT[:, dc, :], sgs[0][:], sgs[1][:], ALU.add)

        c1(0, range(8))
        for sti in range(4):
            c0 = sti * 512
            def c2(i):
                r0 = s * T + c0 + i * 128
                self.dma("sp", xt[i % 2][:], x[r0:r0 + 128, :])
                for half in range(2):
                    hsl = slice(half * 512, (half + 1) * 512)
                    ps = self.bank()
                    for k in range(8):
                        self.mm(ps[:], mixT[:, k, i * 128:(i + 1) * 128], woutb[:, k, hsl], start=(k == 0), stop=(k == 7))
                    self.tt("dve", h2[:, i, hsl], ps[:], xt[i % 2][:, hsl], ALU.add)

            def c3a(i):
                c = i % 2
                self.act(junkb[:], h2[:, i, :], AF.Square, accum=st[:, 8 + c:9 + c])
                self.act(st[:, 10 + c:11 + c], st[:, 8 + c:9 + c], AF.Sqrt, bias=1e-6, scale=1.0 / D)
                self.recip(st[:, 12 + c:13 + c], st[:, 10 + c:11 + c])
                self.stt("dve", fnb[c][:], h2[:, i, :], st[:, 12 + c:13 + c], cst["gffn"][:], ALU.mult, ALU.mult)

            def c3b(i):
                fn = fnb[i % 2]
                pb = self.bank_bf()
                for k in range(8):
                    self.tr(pb[:, k * 128:(k + 1) * 128], fn[:, k * 128:(k + 1) * 128])
                self.copy("act", fT[:, :, i * 128:(i + 1) * 128], pb.rearrange("p (k n) -> p k n", k=8))

            for i in range(4):
                c2(i)
                c3a(i)
                if i >= 1:
                    c3b(i - 1)
            if sti + 1 < 4:
                c1(sti + 1, range(0, 1))
            c3b(3)
            if sti + 1 < 4:
                c1(sti + 1, range(1, 8))
            for c in range(22):
                wg, wu = wgb[c % 3], wub[c % 3]
                self.dma("sp", wg[:].rearrange("p k n -> p (k n)"), scr["wgate"][c])
                self.dma("sp", wu[:].rearrange("p k n -> p (k n)"), scr["wup"][c])
                psg, psu = self.bank(), self.bank()
                for k in range(8):
                    self.mm(psg[:], wg[:, k, :], fT[:, k, :], start=(k == 0), stop=(k == 7))
                for k in range(8):
                    self.mm(psu[:], wu[:, k, :], fT[:, k, :], start=(k == 0), stop=(k == 7))
                sg = sgb[cn["nsg"] % 3]
                cn["nsg"] += 1
                self.act(sg[:], psg[:], AF.Silu)
                self.tt("dve", aT[:, c, :], sg[:], psu[:], ALU.mult)
            for half in range(2):
                accs = [self.bank() for _ in range(4)]
                for piece in range(2):
                    wp = wdp[piece]
                    self.dma("sp", wp[:].rearrange("p c n -> p (c n)"),
                             wdown[half][:, piece * 11 * 512:(piece + 1) * 11 * 512])
                    for i in range(4):
                        for cc in range(11):
                            c = piece * 11 + cc
                            self.mm(accs[i][:], aT[:, c, i * 128:(i + 1) * 128], wp[:, cc, :], start=(c == 0), stop=(c == 21))
                for i in range(4):
                    hs = h2[:, i, half * 512:(half + 1) * 512]
                    self.tt("dve", hs, accs[i][:], hs, ALU.add)
            for i in range(4):
                c = i % 2
                self.act(junkb[:], h2[:, i, :], AF.Square, accum=st[:, 16 + c:17 + c])
                self.act(st[:, 18 + c:19 + c], st[:, 16 + c:17 + c], AF.Sqrt, bias=1e-6, scale=1.0 / D)
                self.recip(st[:, 20 + c:21 + c], st[:, 18 + c:19 + c])
                self.stt("dve", h2[:, i, :], h2[:, i, :], st[:, 20 + c:21 + c], cst["gfin"][:], ALU.mult, ALU.mult)
                r0 = s * T + c0 + i * 128
                self.dma("sp", out[r0:r0 + 128, :], h2[:, i, :], is_out=True)


_CACHE = {}


def kernel(**inputs):
    NS = 2
    ncores = 8
    sh = prep_shared(inputs)
    if "nc" not in _CACHE:
        _CACHE["nc"] = Builder(NS).build()
    nc = _CACHE["nc"]
    x = np.asarray(inputs["x"], np.float32)
    in_maps = []
    for c in range(ncores):
        m = dict(sh)
        m["x"] = np.ascontiguousarray(x[c * NS:(c + 1) * NS].reshape(NS * T, D))
        in_maps.append(m)
    res = run_bass_kernel_spmd(nc, in_maps, core_ids=list(range(ncores)))
    outs = [np.asarray(r["out"], np.float32).reshape(NS, T, D) for r in res.results]
    return np.concatenate(outs, axis=0)
```

```python
import math
from contextlib import ExitStack
import numpy as np
import concourse.bass as bass
import concourse.mybir as mybir
from concourse.bass_utils import run_bass_kernel_spmd

F32 = mybir.dt.float32
BF16 = mybir.dt.bfloat16
AF = mybir.ActivationFunctionType
ALU = mybir.AluOpType
AX = mybir.AxisListType

T = 2048
D = 1024
NT = 16
DFF = 2816
NEG = -30000.0
LN_C = math.log(128 ** -0.5)

_DTSZ = {"dt.float32": 4, "dt.bfloat16": 2, "dt.int32": 4, "dt.uint32": 4, "dt.float16": 2,
         "dt.uint16": 2, "dt.int16": 2, "dt.uint8": 1, "dt.int8": 1}


def _box(ap):
    t = ap.tensor
    name = t.name
    tn = type(t).__name__
    if tn == "PSumTensorHandle":
        return (name, 0, 128, 0, 2048)
    a = ap.ap
    off = int(ap.offset)
    isz = _DTSZ[str(ap.dtype)]
    if tn == "DRamTensorHandle":
        ext = 1
        for st, cnt in a:
            ext += (cnt - 1) * abs(st)
        return (name, 0, 1, off * isz, (off + ext) * isz)
    pstep, pcnt = a[0]
    if pstep == 0:
        pstep = 1 << 40
    p0 = off // pstep
    f0 = off % pstep
    ext = 1
    for st, cnt in a[1:]:
        ext += (cnt - 1) * abs(st)
    return (name, p0, p0 + pcnt, f0 * isz, (f0 + ext) * isz)


def _ovl(a, b):
    return a[1] < b[2] and b[1] < a[2] and a[3] < b[4] and b[3] < a[4]


def _contains(a, b):
    return a[1] <= b[1] and b[2] <= a[2] and a[3] <= b[3] and b[4] <= a[4]


class Op:
    __slots__ = ("eng", "fn", "rb", "wb", "dma", "deps", "sig", "clock", "idx", "waits")


class Sched:
    ENGS = ("pe", "act", "dve", "pool", "sp")

    def __init__(self, nc, n_dma_sems=32):
        self.nc = nc
        self.ops = []
        self.n_dma_sems = n_dma_sems
        self.out_ops = []

    def op(self, eng, fn, reads=(), writes=(), dma=False, is_out=False):
        o = Op()
        o.eng = eng
        o.fn = fn
        o.rb = [_box(r) for r in reads if r is not None and not isinstance(r, (int, float))]
        o.wb = [_box(w) for w in writes]
        o.dma = dma
        o.idx = len(self.ops)
        self.ops.append(o)
        if is_out:
            self.out_ops.append(o)
        return o

    def analyze(self, skip=()):
        hist = {}
        dma_ops = []
        ops = self.ops
        self.dead = []
        for o in ops:
            deps = set()
            for r in o.rb:
                if r[0] in skip:
                    continue
                psum = r[0].startswith("ps")
                for rec in hist.get(r[0], ()):
                    if _ovl(rec[0], r) and (rec[2] or (psum and rec[1].eng != o.eng)):
                        deps.add(rec[1].idx)
                        if rec[2]:
                            rec[3] += 1
            for w in o.wb:
                if w[0] in skip:
                    continue
                for rec in hist.get(w[0], ()):
                    if _ovl(rec[0], w):
                        deps.add(rec[1].idx)
            for w in o.wb:
                if w[0] in skip:
                    continue
                lst = hist.setdefault(w[0], [])
                keep = []
                for rec in lst:
                    if _contains(w, rec[0]):
                        if rec[2] and rec[3] == 0 and not (rec[1].eng == "pe" and o.eng == "pe" and not o.dma):
                            self.dead.append((w[0], rec[1].idx, o.idx))
                    else:
                        keep.append(rec)
                lst[:] = keep
                lst.append([w, o, True, 0])
            for r in o.rb:
                if r[0] in skip:
                    continue
                lst = hist.setdefault(r[0], [])
                if not o.dma:
                    lst[:] = [rec for rec in lst if not ((not rec[2]) and rec[0] == r
                                                         and rec[1].eng == o.eng and not rec[1].dma)]
                lst.append([r, o, False, 0])
            if o.dma:
                k = len(dma_ops)
                if k >= self.n_dma_sems:
                    deps.add(dma_ops[k - self.n_dma_sems].idx)
                dma_ops.append(o)
            deps.discard(o.idx)
            if o.eng == "pe" and not o.dma:
                deps = {d for d in deps if ops[d].eng != "pe" or ops[d].dma}
            o.deps = sorted(deps)
        need = set()
        for o in ops:
            need.update(o.deps)
        for o in self.out_ops:
            need.add(o.idx)
        cnt = {e: 0 for e in self.ENGS}
        ndma = 0
        for o in ops:
            o.sig = None
            if o.dma:
                s = ndma % self.n_dma_sems
                o.sig = (("dma", s), 16 * (ndma // self.n_dma_sems + 1))
                ndma += 1
            elif o.idx in need:
                cnt[o.eng] += 1
                o.sig = ((o.eng,), cnt[o.eng])
        seen = {e: {} for e in self.ENGS}
        nw = 0
        for o in ops:
            sn = seen[o.eng]
            wm = {}
            for d in o.deps:
                dop = ops[d]
                s, v = dop.sig
                if sn.get(s, 0) >= v:
                    continue
                if wm.get(s, 0) < v:
                    wm[s] = v
                for k2, v2 in dop.clock.items():
                    if sn.get(k2, 0) < v2:
                        sn[k2] = v2
            o.waits = list(wm.items())
            nw += len(o.waits)
            if o.sig is not None:
                c = dict(sn)
                c[o.sig[0]] = o.sig[1]
                o.clock = c
            else:
                o.clock = None
        self.stats = dict(n_ops=len(ops), n_waits=nw, sig=dict(cnt), n_dma=ndma, dead=self.dead[:8], n_dead=len(self.dead))
        return self.stats

    def emit(self):
        nc = self.nc
        with ExitStack() as es:
            sems = {}
            for e in self.ENGS:
                sems[(e,)] = es.enter_context(nc.semaphore("s_" + e))
            for i in range(self.n_dma_sems):
                sems[("dma", i)] = es.enter_context(nc.semaphore("s_dma%d" % i))
            block = es.enter_context(nc.Block())
            per = {e: [o for o in self.ops if o.eng == e] for e in self.ENGS}
            finals = [o.sig for o in self.out_ops]

            def run(engh, lst, final=False):
                for o in lst:
                    for s, v in o.waits:
                        engh.wait_ge(sems[s], v)
                    if o.fn is None:
                        continue
                    ins = o.fn(engh)
                    if o.sig is not None:
                        ins.then_inc(sems[o.sig[0]], 16 if o.dma else 1)
                if final:
                    fm = {}
                    for s, v in finals:
                        if fm.get(s, 0) < v:
                            fm[s] = v
                    for s, v in fm.items():
                        engh.wait_ge(sems[s], v)

            @block.tensor
            def _(e):
                run(e, per["pe"])

            @block.scalar
            def _(e):
                run(e, per["act"])

            @block.vector
            def _(e):
                run(e, per["dve"])

            @block.gpsimd
            def _(e):
                run(e, per["pool"])

            @block.sync
            def _(e):
                run(e, per["sp"], final=True)


def _rel_bucket_np():
    n = np.arange(0, T + 1)
    nf = np.maximum(n, 1).astype(np.float32)
    large = 16 + (np.log(nf / np.float32(16)) / np.float32(math.log(1024 / 16)) * np.float32(16)).astype(np.int32)
    large = np.minimum(large, 31)
    return np.where(n < 16, n, large).astype(np.int64)


_OFF = {}
_o = 0
for _n, _s in (("mq", 512), ("mk", 512), ("mv", 512), ("mo", 512), ("mi", 4), ("mf", 4),
               ("nq", 512), ("nkv", 768), ("ngate", 24), ("merge", 2048)):
    _OFF[_n] = _o
    _o += _s


def prep_shared(inp):
    w_in = np.asarray(inp["w_in"][0], np.float32)
    b_in = np.asarray(inp["b_in"][0], np.float32)
    cols = []
    cols += list(range(_OFF["mq"], _OFF["mq"] + 512))
    cols += list(range(_OFF["mk"], _OFF["mk"] + 512))
    for j in range(4):
        for g in range(2):
            h = g * 4 + j
            cols += list(range(_OFF["nq"] + h * 64, _OFF["nq"] + h * 64 + 64))
    for s in (0, 1, 2, 4):
        cols += list(range(_OFF["nkv"] + s * 128, _OFF["nkv"] + s * 128 + 128))
    cols = np.array(cols)
    tm1 = np.array(list(range(_OFF["mv"], _OFF["mv"] + 512)) + list(range(_OFF["mo"], _OFF["mo"] + 512))
                   + list(range(_OFF["mi"], _OFF["mi"] + 4)) + list(range(_OFF["mf"], _OFF["mf"] + 4)))
    tm2 = np.array(list(range(_OFF["nkv"] + 3 * 128, _OFF["nkv"] + 4 * 128))
                   + list(range(_OFF["nkv"] + 5 * 128, _OFF["nkv"] + 6 * 128))
                   + list(range(_OFF["ngate"], _OFF["ngate"] + 24)))
    mg = np.arange(_OFF["merge"], _OFF["merge"] + 2048)
    sh = {}
    def chunked(w, nch):
        return np.ascontiguousarray(w.reshape(8, 128, nch, 128).transpose(2, 1, 0, 3).reshape(nch, 128, 1024))
    sh["wfm"] = chunked(w_in[:, cols], 16)
    sh["wtm1"] = np.ascontiguousarray(w_in[:, tm1])
    sh["wtm2"] = np.ascontiguousarray(w_in[:, tm2])
    sh["wmg"] = chunked(w_in[:, mg], 16)
    sh["bfm"] = np.ascontiguousarray(b_in[cols].reshape(16, 128).T)
    sh["btm1"] = np.ascontiguousarray(b_in[tm1][None, :])
    sh["btm2"] = np.ascontiguousarray(b_in[tm2][None, :])
    sh["bmg"] = np.ascontiguousarray(b_in[mg].reshape(16, 128).T)
    bc = lambda v: np.ascontiguousarray(np.broadcast_to(np.asarray(v, np.float32)[None, :], (128, len(v))))
    sh["gmix"] = bc(inp["g_norm_mix"][0])
    sh["gffn"] = bc(inp["g_norm_ffn"][0])
    sh["gfin"] = bc(inp["g_final"])
    sh["bfg"] = bc(inp["b_fgate"][0])
    sh["ghead"] = bc(inp["g_mlstm_head"][0])
    cw = np.asarray(inp["conv_qk"][0], np.float32)
    sh["convw"] = np.ascontiguousarray(cw.T.reshape(8, 128, 4).transpose(1, 0, 2))
    sh["ident"] = np.eye(128, dtype=np.float32)
    jj, ii = np.meshgrid(np.arange(128), np.arange(128), indexing="ij")
    sh["negtri"] = np.where(jj <= ii, -1.0, 0.0).astype(np.float32)
    sh["negones"] = -np.ones((128, 128), np.float32)
    sh["maskT"] = np.where(ii >= jj, 1.0, 0.0).astype(np.float32)
    pe = np.asarray(inp["pe_cmp"][0], np.float32)
    peT = pe.transpose(2, 0, 1)
    sh["peT"] = np.ascontiguousarray(np.concatenate([peT, peT], 0))
    w1 = np.asarray(inp["w_cmp1"][0], np.float32).reshape(2, 32, 64, 128).transpose(2, 0, 1, 3)
    sh["w1"] = np.ascontiguousarray(np.concatenate([w1, w1], 0))
    w2 = np.asarray(inp["w_cmp2"][0], np.float32).transpose(1, 0, 2)
    sh["w2"] = np.ascontiguousarray(np.concatenate([w2, w2], 2))
    cs = np.arange(127) * 16
    ss = np.arange(32) * 64
    ov = np.clip(np.minimum(cs[:, None] + 32, ss[None, :] + 64) - np.maximum(cs[:, None], ss[None, :]), 0, None) / 16.0
    ovp = np.zeros((128, 32), np.float32)
    ovp[:127] = ov
    sh["overlap"] = ovp
    rb = np.asarray(inp["rel_bias"], np.float32)
    bucket = _rel_bucket_np()
    tbl = rb.T
    dist = np.arange(T)[None, :] - np.arange(128)[:, None]
    BS = np.full((8, 128, T), NEG, np.float32)
    ok = dist >= 0
    BS[:, ok] = tbl[:, bucket[dist[ok]]]
    sh["BS"] = BS
    d640 = dist[:, :640]
    BW = np.full((8, 128, 640), NEG, np.float32)
    okw = (d640 >= 0) & (d640 < 512)
    BW[:, okw] = tbl[:, bucket[d640[okw]]]
    sh["BW"] = BW
    dc = np.arange(T)[None, :] - (np.arange(128) * 16 + 31)[:, None]
    BC = np.full((8, 128, T), NEG, np.float32)
    okc = dc >= 0
    okc[127, :] = False
    BC[:, okc] = tbl[:, bucket[dc[okc]]]
    sh["BC"] = BC
    ex = np.zeros((32, T), np.float32)
    ex[np.arange(T) // 64, np.arange(T)] = 1.0
    sh["expand"] = ex
    cand = np.zeros((8, 128, 32), np.float32)
    forced = np.zeros((8, 128, 32), np.float32)
    for qi in range(8):
        t = (8 + qi) * 128 + np.arange(128)
        cur = t // 64
        jb = np.arange(32)[None, :]
        f = (jb == 0) | (jb == cur[:, None]) | (jb == cur[:, None] - 1)
        el = jb * 64 <= t[:, None]
        forced[qi] = f
        cand[qi] = el & ~f
    sh["cand"] = np.ascontiguousarray(cand.transpose(1, 0, 2))
    sh["negm"] = np.ascontiguousarray((cand - 1.0).transpose(1, 0, 2))
    sh["forced"] = np.ascontiguousarray(forced.transpose(1, 0, 2))
    wbr = np.asarray(inp["w_branch"][0], np.float32).reshape(2, 4, 128, 8, 128)
    sh["wbr"] = np.ascontiguousarray(wbr.transpose(0, 3, 2, 1, 4).reshape(16, 128, 512))
    sh["wout"] = np.ascontiguousarray(np.asarray(inp["w_out"][0], np.float32).reshape(8, 128, 1024).transpose(1, 0, 2).reshape(128, 8192))
    sh["wgate"] = chunked(np.asarray(inp["w_gate"][0], np.float32), 22)
    sh["wup"] = chunked(np.asarray(inp["w_up"][0], np.float32), 22)
    sh["wdown"] = np.ascontiguousarray(np.asarray(inp["w_down"][0], np.float32).reshape(22, 128, 2, 512).transpose(2, 1, 0, 3).reshape(2 * 128 * 22, 512))
    return sh


ARENA_BYTES = 124 * 1024
NQ_OFF = 3 * 2048 + 4480
import os as _os
B2STOP = int(_os.environ.get('B2STOP', '0'))
B2SUB = int(_os.environ.get('B2SUB', '0'))
B2TM = int(_os.environ.get('B2TM', '0'))


class Builder:
    def __init__(self, NS, dbg=None, stop_after=None):
        self.NS = NS
        self.dbg = dbg or ()
        self.stop_after = stop_after
        self.nc = bass.Bass("TRN2", target_bir_lowering=False)
        self.S = Sched(self.nc)
        self.es = ExitStack()
        self.dr = {}
        self.rr = 0
        self.rr2 = 0
        self.nrot = 6

    def din(self, name, shape, dt=F32):
        self.dr[name] = self.nc.dram_tensor(name, list(shape), dt, kind="ExternalInput").ap()
        return self.dr[name]

    def dout(self, name, shape, dt=F32):
        self.dr[name] = self.nc.dram_tensor(name, list(shape), dt, kind="ExternalOutput").ap()
        return self.dr[name]

    def sb(self, name, shape, dt):
        return self.es.enter_context(self.nc.sbuf_tensor(name, list(shape), dt))

    def arena_reset(self):
        if getattr(self, "aoff", 0):
            print("arena used", self.aoff, flush=True)
        self.aoff = 0
        self.aviews = {}

    def al(self, name, shape, dt, alias=None):
        isz = 4 if dt == F32 else 2
        n = 1
        for d in shape[1:]:
            n *= d
        nbytes = ((n * isz + 63) // 64) * 64
        if alias is not None:
            off = self.aviews[alias][1]
        else:
            off = self.aoff
            self.aoff += nbytes
            assert self.aoff <= ARENA_BYTES, (name, self.aoff)
        v = self.arena[0:shape[0], off // 2:off // 2 + (n * isz) // 2]
        if dt == F32:
            v = v.bitcast(F32)
        if len(shape) == 3:
            v = v.rearrange("p (a b) -> p a b", a=shape[1])
        elif len(shape) == 4:
            v = v.rearrange("p (a b c) -> p a b c", a=shape[1], b=shape[2])
        self.aviews[name] = (v, off, nbytes)
        return v

    def view_at(self, off, shape, dt):
        isz = 4 if dt == F32 else 2
        n = 1
        for d in shape[1:]:
            n *= d
        v = self.arena[0:shape[0], off // 2:off // 2 + (n * isz) // 2]
        if dt == F32:
            v = v.bitcast(F32)
        if len(shape) == 3:
            v = v.rearrange("p (a b) -> p a b", a=shape[1])
        return v

    def dma(self, q, out, in_, is_out=False):
        isd = lambda a: type(a.tensor).__name__ == "DRamTensorHandle" and not a.tensor.name.startswith("scr_")
        rd = [] if isd(in_) else [in_]
        wr = [] if isd(out) else [out]
        return self.S.op(q, lambda e: e.dma_start(out=out, in_=in_), reads=rd, writes=wr, dma=True, is_out=is_out)

    def mm(self, out, lhsT, rhs, start=True, stop=True):
        self.S.op("pe", lambda e: e.matmul(out, lhsT=lhsT, rhs=rhs, start=start, stop=stop),
                  reads=[lhsT, rhs], writes=[out])

    def tr(self, out, in_, f32=False):
        n = in_.shape[0]
        idn = (self.identf if f32 else self.identb)[0:n, 0:n]
        self.S.op("pe", lambda e: e.transpose(out=out, in_=in_, identity=idn), reads=[in_, idn], writes=[out])

    def act(self, out, in_, func, bias=None, scale=None, accum=None):
        kw = {}
        rd = [in_]
        wr = [out]
        if bias is not None:
            kw["bias"] = bias
            rd.append(bias)
        if scale is not None:
            kw["scale"] = scale
            rd.append(scale)
        if accum is not None:
            kw["accum_out"] = accum
            wr.append(accum)
        self.S.op("act", lambda e: e.activation(out=out, in_=in_, func=func, **kw), reads=rd, writes=wr)

    def tt(self, eng, out, in0, in1, op):
        self.S.op(eng, lambda e: e.tensor_tensor(out=out, in0=in0, in1=in1, op=op), reads=[in0, in1], writes=[out])

    def ts(self, eng, out, in0, s1, op0, s2=None, op1=None):
        if op1 is None:
            self.S.op(eng, lambda e: e.tensor_scalar(out=out, in0=in0, scalar1=s1, scalar2=None, op0=op0),
                      reads=[in0, s1], writes=[out])
        else:
            self.S.op(eng, lambda e: e.tensor_scalar(out=out, in0=in0, scalar1=s1, scalar2=s2, op0=op0, op1=op1),
                      reads=[in0, s1, s2], writes=[out])

    def stt(self, eng, out, in0, scalar, in1, op0, op1):
        self.S.op(eng, lambda e: e.scalar_tensor_tensor(out=out, in0=in0, scalar=scalar, in1=in1, op0=op0, op1=op1),
                  reads=[in0, scalar, in1], writes=[out])

    def copy(self, eng, out, in_):
        if eng == "act":
            self.S.op("act", lambda e: e.copy(out=out, in_=in_), reads=[in_], writes=[out])
        else:
            self.S.op(eng, lambda e: e.tensor_copy(out=out, in_=in_), reads=[in_], writes=[out])

    def recip(self, out, in_):
        self.S.op("dve", lambda e: e.reciprocal(out=out, in_=in_), reads=[in_], writes=[out])

    def memset(self, eng, ap, val):
        self.S.op(eng, lambda e: e.memset(ap, val), writes=[ap])

    def bank(self):
        b = self.ps[self.rr % self.nrot]
        self.rr += 1
        return b

    def bank_bf(self):
        return self.bank()[:].bitcast(BF16)

    def accbank(self):
        b = self.ps[6 + self.rr2 % 2]
        self.rr2 += 1
        return b

    def build(self):
        nc, S, NS = self.nc, self.S, self.NS
        din, sb = self.din, self.sb
        din("x", [NS * T, D])
        for n, shp in (("gmix", [128, D]), ("gffn", [128, D]), ("gfin", [128, D]), ("wfm", [16, 128, 1024]),
                       ("wtm1", [D, 1032]), ("wtm2", [D, 280]), ("wmg", [16, 128, 1024]), ("bfm", [128, 16]),
                       ("btm1", [1, 1032]), ("btm2", [1, 280]), ("bmg", [128, 16]), ("convw", [128, 8, 4]),
                       ("bfg", [128, 4]), ("ghead", [128, 512]), ("ident", [128, 128]), ("negtri", [128, 128]),
                       ("negones", [128, 128]), ("maskT", [128, 128]), ("peT", [128, 2, 32]),
                       ("w1", [128, 2, 32, 128]), ("w2", [128, 2, 128]), ("overlap", [128, 32]),
                       ("BS", [8, 128, T]), ("BW", [8, 128, 640]), ("BC", [8, 128, T]), ("expand", [32, T]),
                       ("cand", [128, 8, 32]), ("negm", [128, 8, 32]), ("forced", [128, 8, 32]),
                       ("wbr", [16, 128, 512]), ("wout", [128, 8192]), ("wgate", [22, 128, 1024]), ("wup", [22, 128, 1024]),
                       ("wdown", [2 * 128 * 22, 512])):
            din(n, shp)
        dr = self.dr
        out = self.dout("out", [NS * T, D])
        dbg_t = {}
        for n in self.dbg:
            shp = {"ymT": [NS, 128, 4, T], "ynT": [NS, 128, 4, T], "qk": [NS, 128, 8, T], "uT": [NS, 128, 8, T]}[n]
            dbg_t[n] = self.dout("dbg_" + n, shp, BF16)
        self.dbg_t = dbg_t

        self.ps = [self.es.enter_context(nc.psum_tensor("ps%d" % i, [128, 512], F32)) for i in range(8)]

        self.identb = sb("identb", [128, 128], BF16)
        self.dma("pool", self.identb[:], dr["ident"][:, :])
        cst = {}
        late = []
        for n, shp in (("gmix", [128, D]), ("gffn", [128, D]), ("gfin", [128, D]), ("bfm", [128, 16]),
                       ("bmg", [128, 16]), ("convw", [128, 8, 4]), ("bfg", [128, 4]), ("ghead", [128, 512]),
                       ("negtri", [128, 128]), ("negones", [128, 128]), ("maskT", [128, 128]),
                       ("ident", [128, 128]), ("overlap", [128, 32])):
            cst[n] = sb("c_" + n, shp, F32)
            if n in ("gmix", "bfm"):
                self.dma("sp", cst[n][:], dr[n])
            else:
                late.append(n)
        self.late_consts = late
        self.identf = cst["ident"]
        cst["bfm8"] = sb("c_bfm8", [128, 16], F32)
        self.ts("dve", cst["bfm8"][:], cst["bfm"][:], 0.125, ALU.mult)
        self.cst = cst
        ones_row = sb("ones_row", [1, 128], BF16)
        self.memset("dve", ones_row[:], 1.0)
        self.ones_row = ones_row
        self.btm1 = sb("s_btm1", [1, 1032], BF16)
        self.dma("pool", self.btm1[:], dr["btm1"][:, :])
        self.btm2 = sb("s_btm2", [1, 280], BF16)
        self.dma("pool", self.btm2[:], dr["btm2"][:, :])
        self.st = sb("stats", [128, 64], F32)

        self.uT = sb("s_uT", [128, 8, T], BF16)
        self.ymT = sb("s_ymT", [128, 4, T], BF16)
        self.ynT = sb("s_ynT", [128, 4, T], BF16)
        self.arena = sb("arena", [128, ARENA_BYTES // 2], BF16)

        self.scr = {}
        for n, shp in (("wmg", [16, 128, 1024]), ("wbr", [16, 128, 512]), ("wout", [128, 8192]),
                       ("wgate", [22, 128, 1024]), ("wup", [22, 128, 1024]), ("wdown", [2 * 128 * 22, 512])):
            self.scr[n] = nc.dram_tensor("scr_" + n, shp, BF16, kind="Internal").ap()
        stages = ("a", "b1", "b2", "c")
        last = stages.index(self.stop_after) if self.stop_after else 3
        for s in range(NS):
            self.phase_a(s)
            if "uT" in dbg_t:
                self.dma("sp", dbg_t["uT"][s], self.uT[:], is_out=True)
            if last >= 1:
                self.phase_b1(s)
                if "ymT" in dbg_t:
                    self.dma("sp", dbg_t["ymT"][s], self.ymT[:], is_out=True)
            if last >= 2:
                self.phase_b2(s)
                if "ynT" in dbg_t:
                    self.dma("sp", dbg_t["ynT"][s], self.ynT[:], is_out=True)
            if last >= 3:
                self.phase_c(s)

        if last < 3:
            self.arena_reset()
            zt = self.al("zt", [128, D], F32)
            self.memset("dve", zt[:], 0.0)
            for i in range(NS * NT):
                self.dma("sp", out[i * 128:(i + 1) * 128, :], zt[:], is_out=True)
        print("arena used (last phase)", self.aoff, "sbuf remaining", nc.sbuf_bytes_remaining, flush=True)
        print(S.analyze(), flush=True)
        S.emit()
        self.es.close()
        return nc

    def prologue_list(self):
        dr, scr = self.dr, self.scr
        lst = []
        for c in range(0, 16, 2):
            lst.append((scr["wmg"][c:c + 2].rearrange("c p n -> (c p) n"), dr["wmg"][c:c + 2].rearrange("c p n -> (c p) n")))
        for c in range(0, 16, 4):
            lst.append((scr["wbr"][c:c + 4].rearrange("c p n -> (c p) n"), dr["wbr"][c:c + 4].rearrange("c p n -> (c p) n")))
        for k in range(0, 8, 2):
            lst.append((scr["wout"][:, k * 1024:(k + 2) * 1024].rearrange("p (k n) -> p k n", k=2),
                        dr["wout"][:, k * 1024:(k + 2) * 1024].rearrange("p (k n) -> p k n", k=2)))
        for c in range(0, 22, 2):
            lst.append((scr["wgate"][c:c + 2].rearrange("c p n -> (c p) n"), dr["wgate"][c:c + 2].rearrange("c p n -> (c p) n")))
            lst.append((scr["wup"][c:c + 2].rearrange("c p n -> (c p) n"), dr["wup"][c:c + 2].rearrange("c p n -> (c p) n")))
        for i in range(0, 5632, 704):
            lst.append((scr["wdown"][i:i + 704, :], dr["wdown"][i:i + 704, :]))
        return lst

    def prologue_some(self, n):
        for _ in range(n):
            if self.pl:
                o, i = self.pl.pop(0)
                self.dma("pool", o, i)

    def phase_a(self, s):
        self.arena_reset()
        x = self.dr["x"]
        xbuf = [self.al("xbuf%d" % i, [128, D], F32) for i in range(3)]
        xnb = [self.al("xnb%d" % i, [128, D], BF16) for i in range(2)]
        junkb = self.al("junkb", [128, D], BF16)
        st = self.st
        def stA(i):
            xt = xbuf[i % 3]
            xn = xnb[i % 2]
            r0 = s * T + i * 128
            self.dma("sp", xt[:], x[r0:r0 + 128, :])
            c = i % 2
            self.act(junkb[:], xt[:], AF.Square, accum=st[:, c:c + 1])
            self.act(st[:, 2 + c:3 + c], st[:, c:c + 1], AF.Sqrt, bias=1e-6, scale=1.0 / D)
            self.recip(st[:, 4 + c:5 + c], st[:, 2 + c:3 + c])
            self.stt("dve", xn[:], xt[:], st[:, 4 + c:5 + c], self.cst["gmix"][:], ALU.mult, ALU.mult)

        def stB(i):
            xn = xnb[i % 2]
            pb = self.bank_bf()
            for k in range(8):
                self.tr(pb[:, k * 128:(k + 1) * 128], xn[:, k * 128:(k + 1) * 128])
            self.copy("act" if i % 2 else "dve", self.uT[:, :, i * 128:(i + 1) * 128],
                      pb.rearrange("p (k n) -> p k n", k=8))

        stA(0)
        for i in range(NT):
            if i + 1 < NT:
                stA(i + 1)
            if i == 2 and self.late_consts:
                for n in self.late_consts:
                    self.dma("sp", self.cst[n][:], self.dr[n])
                self.late_consts = []
            stB(i)

    def fm_proj(self, c, wb, evac, spans=range(4), load=True):
        if load:
            self.dma("pool", wb[:].rearrange("p k n -> p (k n)"), self.dr["wfm"][c])
        for sp in spans:
            ps = self.bank()
            for k in range(8):
                self.mm(ps[:], wb[:, k, :], self.uT[:, k, sp * 512:(sp + 1) * 512], start=(k == 0), stop=(k == 7))
            evac(sp, ps)

    def phase_b1(self, s):
        self.arena_reset()
        al = self.al
        dr, cst, uT = self.dr, self.cst, self.uT
        wfmb = [al("wfmb%d" % i, [128, 8, 128], BF16) for i in range(3)]
        pre = [al("pre%d" % i, [128, 3 + T], F32) for i in range(2)]
        cacc = al("cacc", [128, T], F32)
        qk = al("qk", [128, 8, T], BF16)
        og_all = al("og_all", [128, NT, 512], BF16)
        ogtmp = [al("ogtmp%d" % i, [128, 512], F32) for i in range(2)]
        wtm1b = al("wtm1b", [128, 8, 1032], BF16)
        gsb = al("m_g", [128, 2, 64], F32)
        vexts = [al("m_vext%d" % i, [128, 4, 129], BF16) for i in range(2)]
        ATs = [al("m_AT%d" % i, [128, 4, 128], BF16) for i in range(2)]
        kTs = [al("m_kT%d" % i, [128, 4, 128], BF16) for i in range(2)]
        C32 = al("m_C32", [128, 4, 129], F32)
        Cb = al("m_Cb", [128, 4, 129], BF16)
        hhs = [al("m_hh%d" % i, [128, 4, 128], F32) for i in range(2)]
        junk = al("m_junk", [128, 128], F32)
        yms = [al("m_ym%d" % i, [128, 512], BF16) for i in range(2)]
        btm1 = self.btm1
        for p in pre:
            self.memset("dve", p[:, 0:3], 0.0)
        wtm1 = dr["wtm1"].rearrange("(k p) n -> p k n", p=128)
        for k in range(8):
            self.dma("pool", wtm1b[:, k, :], wtm1[:, k, :])
        for c in range(8):
            prb = pre[c % 2]

            def evac(sp, ps, prb=prb, c=c):
                self.act(prb[:, 3 + sp * 512:3 + (sp + 1) * 512], ps[:], AF.Identity, bias=cst["bfm"][:, c:c + 1])
            self.fm_proj(c, wfmb[c % 3], evac)
            cw = cst["convw"]
            self.ts("dve", cacc[:], prb[:, 0:T], cw[:, c, 0:1], ALU.mult)
            for j in range(1, 4):
                self.stt("dve", cacc[:], prb[:, j:j + T], cw[:, c, j:j + 1], cacc[:], ALU.mult, ALU.add)
            self.act(qk[:, c, :], cacc[:], AF.Silu)
        if "qk" in self.dbg_t:
            self.dma("sp", self.dbg_t["qk"][s], qk[:], is_out=True)
        for i in range(NT):
            tsl = slice(i * 128, (i + 1) * 128)
            ps = self.bank()
            for k in range(8):
                self.mm(ps[:], uT[:, k, tsl], wtm1b[:, k, 512:1024], start=(k == 0), stop=False)
            self.mm(ps[:], self.ones_row[0:1, :], btm1[0:1, 512:1024], start=False, stop=True)
            ogt = ogtmp[i % 2]
            self.act(ogt[:], ps[:], AF.Sigmoid)
            self.tt("dve", og_all[:, i, :], ogt[:], cst["ghead"][:], ALU.mult)
        live = {}

        def P1(c):
            tsl = slice(c * 128, c * 128 + 128)
            par = c % 2
            g = gsb[:, par, :]
            ps_v, ps_g = self.ps[6 + par], self.bank()
            for (ps, n0, nn) in ((ps_v, 0, 512), (ps_g, 1024, 8)):
                for k in range(8):
                    self.mm(ps[:, 0:nn], uT[:, k, tsl], wtm1b[:, k, n0:n0 + nn], start=(k == 0), stop=False)
                self.mm(ps[:, 0:nn], self.ones_row[0:1, :], btm1[0:1, n0:n0 + nn], start=False, stop=True)
            self.tt("dve", g[:, 0:4], ps_g[:, 4:8], cst["bfg"][:], ALU.add)
            self.copy("dve", g[:, 4:8], ps_g[:, 0:4])
            self.act(g[:, 44:48], g[:, 0:4], AF.Exp, scale=-1.0)
            self.act(g[:, 48:52], g[:, 44:48], AF.Ln, bias=1.0)
            ps_s = self.bank()
            for h in range(4):
                self.mm(ps_s[:, h * 128:(h + 1) * 128], qk[:, 4 + h, tsl], qk[:, h, tsl])
            self.tt("dve", ATs[par][:], ps_s[:].rearrange("p (h n) -> p h n", h=4),
                    cst["maskT"][:].unsqueeze(1).to_broadcast([128, 4, 128]), ALU.mult)
            if c < NT - 1:
                ps_t = self.bank_bf()
                for h in range(4):
                    self.tr(ps_t[:, h * 128:(h + 1) * 128], qk[:, 4 + h, tsl])
                self.copy("act", kTs[par][:], ps_t[:, 0:512].rearrange("p (h n) -> p h n", h=4))
            live[c] = ps_v

        def P2a(c):
            par = c % 2
            g = gsb[:, par, :]
            ps_v = live.pop(c)
            ps_c = self.bank()
            self.mm(ps_c[:, 0:4], cst["negtri"][:], g[:, 48:52])
            self.mm(ps_c[:, 4:8], cst["negones"][:], g[:, 48:52])
            self.tt("dve", g[:, 8:12], g[:, 4:8], ps_c[:, 0:4], ALU.subtract)
            self.copy("dve", g[:, 16:24], ps_c[:, 0:8])
            self.act(g[:, 12:16], g[:, 8:12], AF.Exp, bias=LN_C)
            self.act(g[:, 16:24], g[:, 16:24], AF.Exp)
            vext = vexts[par]
            self.tt("dve", vext[:, :, 0:128], ps_v[:].rearrange("p (h n) -> p h n", h=4),
                    g[:, 12:16].unsqueeze(2).to_broadcast([128, 4, 128]), ALU.mult)
            self.copy("dve", vext[:, :, 128:129], g[:, 12:16].unsqueeze(2))

        def P2b(c):
            tsl = slice(c * 128, c * 128 + 128)
            par = c % 2
            g = gsb[:, par, :]
            vext, AT, kT = vexts[par], ATs[par], kTs[par]
            ps_n = [self.bank(), self.bank()]
            for h in range(4):
                pn = ps_n[h // 2][:, (h % 2) * 129:(h % 2) * 129 + 129]
                self.mm(pn, AT[:, h, :], vext[:, h, :], start=(h % 2 == 0), stop=(c == 0))
                if c > 0:
                    self.mm(pn, qk[:, h, tsl], Cb[:, h, :], start=False, stop=True)
            if c < NT - 1:
                ps_u = [self.bank(), self.bank()]
                for h in range(4):
                    pu = ps_u[h // 2][:, (h % 2) * 129:(h % 2) * 129 + 129]
                    self.mm(pu, kT[:, h, :], vext[:, h, :], start=(h % 2 == 0), stop=True)
                for hp in range(2):
                    cs = C32[:, 2 * hp:2 * hp + 2, :]
                    pu = ps_u[hp][:, 0:258].rearrange("p (h n) -> p h n", h=2)
                    ebb = g[:, 20 + 2 * hp:22 + 2 * hp].unsqueeze(2).to_broadcast([128, 2, 129])
                    if c == 0:
                        self.tt("dve", cs, pu, ebb, ALU.mult)
                    else:
                        self.tt("dve", cs, cs, pu, ALU.add)
                        self.tt("dve", cs, cs, ebb, ALU.mult)
                self.copy("act", Cb[:], C32[:])
            for hp in range(2):
                pn3 = ps_n[hp][:, 0:258].rearrange("p (h n) -> p h n", h=2)
                self.tt("dve", g[:, 24 + 2 * hp:26 + 2 * hp].unsqueeze(2), pn3[:, :, 128:129],
                        g[:, 16 + 2 * hp:18 + 2 * hp].unsqueeze(2), ALU.mult)
            self.ts("dve", g[:, 56:60], g[:, 24:28], 1.0, ALU.max)
            self.stt("dve", g[:, 28:32], g[:, 24:28], -1.0, g[:, 56:60], ALU.mult, ALU.max)
            self.recip(g[:, 60:64], g[:, 28:32])
            self.tt("dve", g[:, 32:36], g[:, 16:20], g[:, 60:64], ALU.mult)
            hh = hhs[par]
            for hp in range(2):
                pn3 = ps_n[hp][:, 0:258].rearrange("p (h n) -> p h n", h=2)
                self.tt("dve", hh[:, 2 * hp:2 * hp + 2, :], pn3[:, :, 0:128],
                        g[:, 32 + 2 * hp:34 + 2 * hp].unsqueeze(2).to_broadcast([128, 2, 128]), ALU.mult)
            for h in range(4):
                self.act(junk[:], hh[:, h, :], AF.Square, accum=g[:, 36 + h:37 + h])
            self.act(g[:, 40:44], g[:, 36:40], AF.Ln, bias=1e-6, scale=1.0 / 128)
            self.act(g[:, 52:56], g[:, 40:44], AF.Exp, scale=-0.5)
            for h in range(4):
                self.stt("dve", yms[par][:, h * 128:(h + 1) * 128], hh[:, h, :], g[:, 52 + h:53 + h],
                         og_all[:, c, h * 128:(h + 1) * 128], ALU.mult, ALU.mult)

        def P3t(c):
            tsl = slice(c * 128, c * 128 + 128)
            ym = yms[c % 2]
            ps_y = self.bank_bf()
            for h in range(4):
                self.tr(ps_y[:, h * 128:(h + 1) * 128], ym[:, h * 128:(h + 1) * 128])
            self.copy("act", self.ymT[:, :, tsl], ps_y[:, 0:512].rearrange("p (h n) -> p h n", h=4))

        assert self.aviews["cacc"][1] + self.aviews["cacc"][2] >= NQ_OFF + 4 * T * 2
        assert self.aviews["qk"][1] >= NQ_OFF + 4 * T * 2
        nq_e = self.view_at(NQ_OFF, [128, 4, T], BF16)

        def early_nq(it):
            j, half = it // 2, it % 2

            def evac(sp, ps, j=j):
                self.act(nq_e[:, j, sp * 512:(sp + 1) * 512], ps[:], AF.Identity,
                         bias=cst["bfm8"][:, 8 + j:9 + j], scale=0.125)
            self.fm_proj(8 + j, wfmb[(8 + j) % 3], evac, spans=range(2 * half, 2 * half + 2), load=(half == 0))

        P1(0)
        for c in range(NT):
            P2a(c)
            if 2 <= c < 10:
                early_nq(c - 2)
            if c + 1 < NT:
                P1(c + 1)
            if c >= 1:
                P3t(c - 1)
            P2b(c)
        P3t(NT - 1)

    def phase_b2(self, s):
        self.arena_reset()
        al = self.al
        dr, cst, uT = self.dr, self.cst, self.uT
        wfmb = [al("wfmb%d" % i, [128, 8, 128], BF16) for i in range(3)]
        wtm2b = al("wtm2b", [128, 8, 280], BF16)
        nq = al("nq", [128, 4, T], BF16)
        kcT = al("kcT", [128, T], BF16)
        vcT = al("vcT", [128, T], BF16)
        ksa = [al("ksa%d" % i, [128, T], BF16) for i in range(2)]
        kwz = [al("kwz%d" % i, [128, T], BF16) for i in range(2)]
        qa = [al("qa%d" % i, [128, T], BF16) for i in range(2)]
        vsx = al("vsx", [128, NT, 2, 65], BF16)
        vwx = al("vwx", [128, NT, 2, 65], BF16)
        gat = al("gat", [128, NT, 24], F32)
        oc = al("oc", [128, NT, 4, 64], F32)
        w1b = al("w1b", [128, 2, 32, 128], BF16, alias="oc")
        impg = al("impg", [128, NT, 32], F32)
        w2b = al("w2b", [128, 2, 128], BF16)
        peTb = al("peTb", [128, 2, 32], BF16)
        c0 = al("c0", [128, 2], F32)
        shid = [al("shid%d" % i, [128, 128], BF16) for i in range(2)]
        kcmpz = [al("kcmpz%d" % i, [128, 128], BF16) for i in range(2)]
        vcx = al("vcx", [128, 2, 97], BF16)
        bland = al("bland", [128, T], F32)
        EB = [al("EB%d" % i, [128, T], BF16) for i in range(2)]
        bw16 = [al("bw16_%d" % i, [128, 640], BF16) for i in range(2)]
        e0b = [al("e0b%d" % i, [128, 512], BF16) for i in range(5)]
        eTb = [al("eTb%d" % i, [128, 512], BF16) for i in range(6)]
        tkc = al("tkc", [128, 3, 8, 32], F32)
        tk = al("tk", [128, 128], F32)
        selb = al("selb", [128, 128], BF16)
        sm = al("sm", [128, 32], F32)
        tmpo = [al("tmpo%d" % i, [128, 4, 64], F32) for i in range(2)]
        tmpi = al("tmpi", [128, 4, 32], F32)
        btm2 = self.btm2

        wtm2 = dr["wtm2"].rearrange("(k p) n -> p k n", p=128)
        for k in range(8):
            self.dma("pool", wtm2b[:, k, :], wtm2[:, k, :])
        if not (B2SUB & 2):
            for i, n in enumerate(("cand", "negm", "forced")):
                self.dma("sp", tkc[:, i, :, :], dr[n])
        if not (B2SUB & 4):
            self.memset("dve", vsx[:, :, :, 64:65], 1.0)
            self.memset("dve", vwx[:, :, :, 64:65], 1.0)
            self.memset("dve", vcx[:], 0.0)
            self.memset("dve", vcx[:, :, 96:97], 1.0)
            for g in range(2):
                self.copy("dve", vcx[:, g, 64:96], cst["overlap"][:])

        dests = {12: kcT, 13: vcT}
        dests2 = {14: ksa, 15: kwz}
        assert self.aviews["nq"][1] == NQ_OFF
        for c in range(12, 16):
            if B2SUB & 8:
                break
            if c < 12:
                def evac(sp, ps, c=c):
                    self.act(nq[:, c - 8, sp * 512:(sp + 1) * 512], ps[:], AF.Identity,
                             bias=cst["bfm8"][:, c:c + 1], scale=0.125)
            elif c < 14:
                def evac(sp, ps, c=c):
                    self.act(dests[c][:, sp * 512:(sp + 1) * 512], ps[:], AF.Identity, bias=cst["bfm"][:, c:c + 1])
            else:
                def evac(sp, ps, c=c):
                    for g_ in range(2):
                        gs_ = slice(g_ * 64, (g_ + 1) * 64)
                        self.act(dests2[c][g_][gs_, sp * 512:(sp + 1) * 512], ps[gs_, :], AF.Identity,
                                 bias=cst["bfm"][gs_, c:c + 1])
            self.fm_proj(c, wfmb[c % 3], evac)
        if not (B2SUB & 1):
            for j in range(2):
                for lh in range(4):
                    self.dma("pool", w1b[:, j, lh * 8:(lh + 1) * 8, :], dr["w1"][:, j, lh * 8:(lh + 1) * 8, :])
            self.dma("pool", w2b[:], dr["w2"])
            self.dma("pool", peTb[:], dr["peT"])
            self.memset("dve", ksa[0][64:128, :], 0.0)
            self.memset("dve", ksa[1][0:64, :], 0.0)
            self.memset("dve", kwz[0][64:128, :], 0.0)
            self.memset("dve", kwz[1][0:64, :], 0.0)
            self.memset("dve", kcmpz[0][:], 0.0)
            self.memset("dve", kcmpz[1][:], 0.0)
            self.memset("dve", selb[:], 0.0)
            for hh_ in range(2):
                hs_ = slice(hh_ * 1024, (hh_ + 1) * 1024)
                self.dma("pool", ksa[0][64:96, hs_], dr["expand"][:, hs_])
                self.dma("pool", ksa[1][0:32, hs_], dr["expand"][:, hs_])
        self.pl = self.prologue_list() if (s == 0 and self.stop_after in (None, "c")) else []
        for i in range(NT):
            if B2SUB & 16:
                break
            tsl = slice(i * 128, (i + 1) * 128)
            ps = self.bank()
            for k in range(8):
                self.mm(ps[:, 0:280], uT[:, k, tsl], wtm2b[:, k, :], start=(k == 0), stop=False)
            self.mm(ps[:, 0:280], self.ones_row[0:1, :], btm2[0:1, :], start=False, stop=True)
            if not (B2TM & 1):
                self.copy("act", vsx[:, i, :, 0:64], ps[:, 0:128].rearrange("p (g d) -> p g d", g=2))
            if not (B2TM & 2):
                self.copy("dve", vwx[:, i, :, 0:64], ps[:, 128:256].rearrange("p (g d) -> p g d", g=2))
            if not (B2TM & 4):
                self.act(gat[:, i, :], ps[:, 256:280], AF.Sigmoid)

        if B2STOP == 1:
            return
        for j in range(2):
            ps = self.bank()
            for l in range(32):
                self.mm(ps[:, 0:1], w1b[0:64, j, l, :], peTb[0:64, j, l:l + 1], start=(l == 0), stop=(l == 31))
            self.copy("dve", c0[:, j:j + 1], ps[:, 0:1])
        for j in range(2):
            src = kcT if j == 0 else vcT
            for g in range(2):
                ps = self.bank()
                for l in range(32):
                    self.mm(ps[:, 0:127], w1b[g * 64:(g + 1) * 64, j, l, :],
                            src[g * 64:(g + 1) * 64, l:l + 16 * 126 + 1:16], start=(l == 0), stop=(l == 31))
                sh = shid[(2 * j + g) % 2]
                self.act(sh[:, 0:127], ps[:, 0:127], AF.Silu, bias=c0[:, j:j + 1])
                ps2 = self.bank()
                if j == 0:
                    self.mm(ps2[:, 0:127], w2b[:, 0, :], sh[:, 0:127])
                    self.copy("dve", kcmpz[g][g * 64:(g + 1) * 64, 0:127], ps2[g * 64:(g + 1) * 64, 0:127])
                else:
                    self.mm(ps2[0:127, 0:64], sh[:, 0:127], w2b[:, 1, 0:64])
                    self.copy("dve", vcx[0:127, g, 0:64], ps2[0:127, 0:64])

        if B2STOP == 2:
            return
        cnt = {"ne": 0}
        jobs = []
        for g_ in range(2):
            jobs += [("c", g_, r_) for r_ in range(4)] + [("s", g_, r_) for r_ in range(4)]

        def pf_dma(ji):
            if ji >= len(jobs):
                return
            kind, g_, r_ = jobs[ji]
            h_ = g_ * 4 + r_
            if kind == "c":
                self.dma("sp", bland[:], dr["BC"][h_])
            else:
                self.dma("sp", bland[:], dr["BS"][h_])
                self.dma("pool", bw16[ji % 2][:], dr["BW"][h_])

        def pf_exp(ji):
            self.prologue_some(3)
            if ji >= len(jobs):
                return
            kind = jobs[ji][0]
            self.act(EB[ji % 2][:], bland[:], AF.Exp)

        pf_dma(0)
        pf_exp(0)
        self.prologue_some(2)

        def run_steps(steps, lag=5):
            n = len(steps)
            for i in range(n + lag):
                if i < n:
                    steps[i][0]()
                if i - lag >= 0:
                    steps[i - lag][1]()

        for g in range(2):
            gs = slice(g * 64, (g + 1) * 64)
            steps = []
            for r in range(4):
                h = g * 4 + r
                ji = jobs.index(("c", g, r))
                for sp in range(4):
                    box = {}

                    def fS(r=r, h=h, sp=sp, ji=ji, box=box):
                        if sp == 0:
                            pf_dma(ji + 1)
                        if sp == 2:
                            pf_exp(ji + 1)
                        ssl = slice(sp * 512, (sp + 1) * 512)
                        ps = self.bank()
                        self.mm(ps[0:127, :], kcmpz[g][:, 0:127], nq[:, r, ssl])
                        e0 = e0b[cnt["ne"] % 5]
                        eT = eTb[cnt["ne"] % 6]
                        cnt["ne"] += 1
                        self.act(e0[0:127, :], ps[0:127, :], AF.Exp)
                        self.tt("dve", eT[0:127, :], e0[0:127, :], EB[ji % 2][0:127, ssl], ALU.mult)
                        box["eT"] = eT

                    def fP(r=r, h=h, sp=sp, box=box):
                        eT = box["eT"]
                        ps2 = self.bank()
                        for q in range(4):
                            self.mm(ps2[:, q * 97:(q + 1) * 97], eT[0:127, q * 128:(q + 1) * 128], vcx[0:127, g, :])
                        p3 = ps2[:, 0:388].rearrange("p (q n) -> p q n", q=4)
                        tq = slice(sp * 4, sp * 4 + 4)
                        self.ts("dve", tk[:, 0:4].unsqueeze(2), p3[:, :, 96:97], 1e-30, ALU.max)
                        self.recip(tk[:, 4:8], tk[:, 0:4])
                        self.tt("dve", tk[:, 8:12].unsqueeze(2), tk[:, 4:8].unsqueeze(2), gat[:, tq, h * 3:h * 3 + 1], ALU.mult)
                        self.tt("dve", oc[:, tq, r, :], p3[:, :, 0:64],
                                tk[:, 8:12].unsqueeze(2).to_broadcast([128, 4, 64]), ALU.mult)
                        rb = tk[:, 4:8].unsqueeze(2).to_broadcast([128, 4, 32])
                        if r == 0:
                            self.tt("dve", impg[:, tq, :], p3[:, :, 64:96], rb, ALU.mult)
                        else:
                            self.tt("dve", tmpi[:], p3[:, :, 64:96], rb, ALU.mult)
                            self.tt("dve", impg[:, tq, :], impg[:, tq, :], tmpi[:], ALU.add)
                    steps.append((fS, fP))
            run_steps(steps)
            if B2STOP == 3:
                return
            mrows = slice(64, 96) if g == 0 else slice(0, 32)
            orows = slice(64, 128) if g == 0 else slice(0, 64)
            mc0 = 64 if g == 0 else 0
            for b_ in range(2):
                self.memset("dve", qa[b_][orows, :], 0.0)
            for qi in range(8):
                qt = 8 + qi
                wk = tk[:, 16:48]
                self.tt("dve", wk, impg[:, qt, :], tkc[:, 0, qi, :], ALU.mult)
                self.tt("dve", wk, wk, tkc[:, 1, qi, :], ALU.add)
                self.S.op("dve", lambda e, wk=wk: e.max(out=tk[:, 48:56], in_=wk), reads=[wk], writes=[tk[:, 48:56]])
                wk2 = tk[:, 64:96]
                self.S.op("dve", lambda e, wk=wk, wk2=wk2: e.match_replace(out=wk2, in_to_replace=tk[:, 48:56],
                                                                            in_values=wk, imm_value=-1.0),
                          reads=[wk, tk[:, 48:56]], writes=[wk2])
                self.S.op("dve", lambda e, wk2=wk2: e.max(out=tk[:, 56:64], in_=wk2), reads=[wk2], writes=[tk[:, 56:64]])
                self.ts("dve", sm[:], wk, tk[:, 60:61], ALU.is_ge)
                self.tt("dve", sm[:], sm[:], tkc[:, 2, qi, :], ALU.add)
                self.ts("dve", selb[:, mc0:mc0 + 32], sm[:], -1.0, ALU.add, -NEG, ALU.mult)
                pst = self.bank_bf()
                self.tr(pst[:, 0:128], selb[:])
                for b_ in range(2):
                    self.copy("act", qa[b_][mrows, qt * 128:(qt + 1) * 128], pst[mrows, 0:128])
            if B2STOP == 4:
                return
            steps = []
            self.copy("dve", qa[0][gs, :], nq[gs, 0, :])
            for r in range(4):
                h = g * 4 + r
                ji = jobs.index(("s", g, r))
                si = 0
                for branch in range(2):
                    kT_ = ksa[g] if branch == 0 else kwz[g]
                    vx = vsx if branch == 0 else vwx
                    bias = EB[ji % 2] if branch == 0 else bw16[ji % 2]
                    for qs in range(4):
                        q_lo, q_hi = 4 * qs, 4 * qs + 3
                        kt_lo = 0 if branch == 0 else max(0, q_lo - 4)
                        span = {}
                        for kt in range(kt_lo, q_hi + 1):
                            q0 = max(q_lo, kt)
                            q1 = q_hi if branch == 0 else min(q_hi, kt + 4)
                            box = {}

                            def fS(r=r, h=h, branch=branch, qs=qs, kt=kt, q0=q0, q1=q1, kT_=kT_, bias=bias,
                                   ji=ji, si=si, box=box):
                                if si == 0:
                                    pf_dma(ji + 1)
                                    if r + 1 < 4:
                                        self.copy("dve", qa[(r + 1) % 2][gs, :], nq[gs, r + 1, :])
                                if si == 6:
                                    pf_exp(ji + 1)
                                n = (q1 - q0 + 1) * 128
                                ksl = slice(kt * 128, (kt + 1) * 128)
                                qsl = slice(q0 * 128, (q1 + 1) * 128)
                                ps = self.bank()
                                qsrc = qa[r % 2][:, qsl] if branch == 0 else nq[:, r, qsl]
                                b0 = (q0 - kt) * 128
                                e0 = e0b[cnt["ne"] % 5]
                                eT = eTb[cnt["ne"] % 6]
                                cnt["ne"] += 1
                                if branch == 0:
                                    self.mm(ps[:, 0:n], kT_[:, ksl], qsrc)
                                    self.act(e0[:, 0:n], ps[:, 0:n], AF.Exp)
                                    self.tt("dve", eT[:, 0:n], e0[:, 0:n], bias[:, b0:b0 + n], ALU.mult)
                                else:
                                    self.mm(ps[:, 0:n], kT_[:, ksl], qsrc, start=True, stop=False)
                                    self.mm(ps[:, 0:n], self.identb[:], bias[:, b0:b0 + n], start=False, stop=True)
                                    self.act(eT[:, 0:n], ps[:, 0:n], AF.Exp)
                                box["eT"] = eT

                            def fP(r=r, h=h, branch=branch, qs=qs, kt=kt, q0=q0, q1=q1, vx=vx, box=box, span=span,
                                   q_lo=q_lo, q_hi=q_hi, kt_lo=kt_lo):
                                eT = box["eT"]
                                if kt == kt_lo:
                                    span["pacc"] = self.accbank()
                                pacc = span["pacc"]
                                for qt in range(q0, q1 + 1):
                                    self.mm(pacc[:, (qt - q_lo) * 65:(qt - q_lo) * 65 + 65],
                                            eT[:, (qt - q0) * 128:(qt - q0 + 1) * 128], vx[:, kt, g, :],
                                            start=(kt == kt_lo and qt == q0), stop=(kt == qt))
                                if kt == q_hi:
                                    p4 = pacc[:, 0:260].rearrange("p (q n) -> p q n", q=4)
                                    tq = slice(q_lo, q_lo + 4)
                                    o = 96 + branch * 16
                                    self.recip(tk[:, o + 4:o + 8].unsqueeze(2), p4[:, :, 64:65])
                                    self.tt("dve", tk[:, o + 8:o + 12].unsqueeze(2), tk[:, o + 4:o + 8].unsqueeze(2),
                                            gat[:, tq, h * 3 + 1 + branch:h * 3 + 2 + branch], ALU.mult)
                                    for q_ in range(4):
                                        self.stt("dve", oc[:, q_lo + q_, r, :], p4[:, q_, 0:64], tk[:, o + 8 + q_:o + 9 + q_],
                                                 oc[:, q_lo + q_, r, :], ALU.mult, ALU.add)
                            steps.append((fS, fP))
                            si += 1
            run_steps(steps)
            if B2STOP == 5:
                return
            for i in range(NT):
                if i % 2 == 0:
                    psy = self.bank()
                for p in range(2):
                    col = ((i % 2) * 2 + p) * 128
                    self.tr(psy[:, col:col + 128], oc[:, i, 2 * p:2 * p + 2, :].rearrange("p a b -> p (a b)"), f32=True)
                if i % 2 == 1:
                    i0 = (i - 1) * 128
                    self.copy("act" if (i // 2) % 2 else "dve",
                              self.ynT[:, 2 * g:2 * g + 2, i0:i0 + 256].rearrange("p c (i n) -> p i c n", i=2),
                              psy[:].rearrange("p (i c n) -> p i c n", i=2, c=2))

    def phase_c(self, s):
        self.prologue_some(1000)
        self.arena_reset()
        al = self.al
        dr, cst, uT = self.dr, self.cst, self.uT
        x, out = dr["x"], dr["out"]
        wdp = [al("wdp%d" % i, [128, 11, 512], BF16) for i in range(2)]
        self.aoff_save = self.aoff
        self.aoff = self.aviews["wdp0"][1]
        wmgb = [al("wmgb%d" % i, [128, 8, 128], BF16) for i in range(4)]
        wbrb = [al("wbrb%d" % i, [128, 4, 128], BF16) for i in range(4)]
        assert self.aoff <= self.aoff_save
        self.aoff = self.aoff_save
        mixT = al("mixT", [128, 8, 512], BF16)
        woutb = al("woutb", [128, 8, D], BF16)
        xt = [al("xt%d" % i, [128, D], F32) for i in range(2)]
        h2 = al("h2", [128, 4, D], F32)
        fT = al("fT", [128, 8, 512], BF16)
        aT = al("aT", [128, 22, 512], BF16)
        wgb = [al("wgb%d" % i, [128, 8, 128], BF16) for i in range(3)]
        wub = [al("wub%d" % i, [128, 8, 128], BF16) for i in range(3)]
        sgb = [al("sgb%d" % i, [128, 512], F32) for i in range(4)]
        fnb = [al("fnb%d" % i, [128, D], BF16) for i in range(2)]
        junkb = al("junkc", [128, D], BF16, alias="sgb3")
        st = self.st
        scr = self.scr
        wdown = scr["wdown"].rearrange("(h p c) n -> h p (c n)", h=2, p=128)
        yT = (self.ymT, self.ynT)
        self.dma("sp", woutb[:].rearrange("p k n -> p (k n)"), scr["wout"])
        cn = {"nsg": 0}

        def c1(sti, dcs):
            csl = slice(sti * 512, sti * 512 + 512)
            for dc in dcs:
                sgs = []
                for n in range(2):
                    wb = wmgb[(2 * dc + n) % 4]
                    self.dma("sp", wb[:].rearrange("p k n -> p (k n)"), scr["wmg"][n * 8 + dc])
                    ps = self.bank()
                    for k in range(8):
                        self.mm(ps[:], wb[:, k, :], uT[:, k, csl], start=(k == 0), stop=(k == 7))
                    sg = sgb[cn["nsg"] % 3]
                    cn["nsg"] += 1
                    self.act(sg[:], ps[:], AF.Sigmoid, bias=cst["bmg"][:, n * 8 + dc:n * 8 + dc + 1])
                    sgs.append(sg)
                for n in range(2):
                    wb = wbrb[(2 * dc + n) % 4]
                    self.dma("sp", wb[:].rearrange("p k n -> p (k n)"), scr["wbr"][n * 8 + dc])
                    ps = self.bank()
                    for k in range(4):
                        self.mm(ps[:], wb[:, k, :], yT[n][:, k, csl], start=(k == 0), stop=(k == 3))
                    self.tt("dve", sgs[n][:], sgs[n][:], ps[:], ALU.mult)
                self.tt("dve", mixT[:, dc, :], sgs[0][:], sgs[1][:], ALU.add)

        c1(0, range(8))
        for sti in range(4):
            c0 = sti * 512
            def c2(i):
                r0 = s * T + c0 + i * 128
                self.dma("sp", xt[i % 2][:], x[r0:r0 + 128, :])
                for half in range(2):
                    hsl = slice(half * 512, (half + 1) * 512)
                    ps = self.bank()
                    for k in range(8):
                        self.mm(ps[:], mixT[:, k, i * 128:(i + 1) * 128], woutb[:, k, hsl], start=(k == 0), stop=(k == 7))
                    self.tt("dve", h2[:, i, hsl], ps[:], xt[i % 2][:, hsl], ALU.add)

            def c3a(i):
                c = i % 2
                self.act(junkb[:], h2[:, i, :], AF.Square, accum=st[:, 8 + c:9 + c])
                self.act(st[:, 10 + c:11 + c], st[:, 8 + c:9 + c], AF.Sqrt, bias=1e-6, scale=1.0 / D)
                self.recip(st[:, 12 + c:13 + c], st[:, 10 + c:11 + c])
                self.stt("dve", fnb[c][:], h2[:, i, :], st[:, 12 + c:13 + c], cst["gffn"][:], ALU.mult, ALU.mult)

            def c3b(i):
                fn = fnb[i % 2]
                pb = self.bank_bf()
                for k in range(8):
                    self.tr(pb[:, k * 128:(k + 1) * 128], fn[:, k * 128:(k + 1) * 128])
                self.copy("act", fT[:, :, i * 128:(i + 1) * 128], pb.rearrange("p (k n) -> p k n", k=8))

            for i in range(4):
                c2(i)
                c3a(i)
                if i >= 1:
                    c3b(i - 1)
            if sti + 1 < 4:
                c1(sti + 1, range(0, 1))
            c3b(3)
            if sti + 1 < 4:
                c1(sti + 1, range(1, 8))
            for c in range(22):
                wg, wu = wgb[c % 3], wub[c % 3]
                self.dma("sp", wg[:].rearrange("p k n -> p (k n)"), scr["wgate"][c])
                self.dma("sp", wu[:].rearrange("p k n -> p (k n)"), scr["wup"][c])
                psg, psu = self.bank(), self.bank()
                for k in range(8):
                    self.mm(psg[:], wg[:, k, :], fT[:, k, :], start=(k == 0), stop=(k == 7))
                for k in range(8):
                    self.mm(psu[:], wu[:, k, :], fT[:, k, :], start=(k == 0), stop=(k == 7))
                sg = sgb[cn["nsg"] % 3]
                cn["nsg"] += 1
                self.act(sg[:], psg[:], AF.Silu)
                self.tt("dve", aT[:, c, :], sg[:], psu[:], ALU.mult)
            for half in range(2):
                accs = [self.bank() for _ in range(4)]
                for piece in range(2):
                    wp = wdp[piece]
                    self.dma("sp", wp[:].rearrange("p c n -> p (c n)"),
                             wdown[half][:, piece * 11 * 512:(piece + 1) * 11 * 512])
                    for i in range(4):
                        for cc in range(11):
                            c = piece * 11 + cc
                            self.mm(accs[i][:], aT[:, c, i * 128:(i + 1) * 128], wp[:, cc, :], start=(c == 0), stop=(c == 21))
                for i in range(4):
                    hs = h2[:, i, half * 512:(half + 1) * 512]
                    self.tt("dve", hs, accs[i][:], hs, ALU.add)
            for i in range(4):
                c = i % 2
                self.act(junkb[:], h2[:, i, :], AF.Square, accum=st[:, 16 + c:17 + c])
                self.act(st[:, 18 + c:19 + c], st[:, 16 + c:17 + c], AF.Sqrt, bias=1e-6, scale=1.0 / D)
                self.recip(st[:, 20 + c:21 + c], st[:, 18 + c:19 + c])
                self.stt("dve", h2[:, i, :], h2[:, i, :], st[:, 20 + c:21 + c], cst["gfin"][:], ALU.mult, ALU.mult)
                r0 = s * T + c0 + i * 128
                self.dma("sp", out[r0:r0 + 128, :], h2[:, i, :], is_out=True)


_CACHE = {}


def kernel(**inputs):
    NS = 2
    ncores = 8
    sh = prep_shared(inputs)
    if "nc" not in _CACHE:
        _CACHE["nc"] = Builder(NS).build()
    nc = _CACHE["nc"]
    x = np.asarray(inputs["x"], np.float32)
    in_maps = []
    for c in range(ncores):
        m = dict(sh)
        m["x"] = np.ascontiguousarray(x[c * NS:(c + 1) * NS].reshape(NS * T, D))
        in_maps.append(m)
    res = run_bass_kernel_spmd(nc, in_maps, core_ids=list(range(ncores)))
    outs = [np.asarray(r["out"], np.float32).reshape(NS, T, D) for r in res.results]
    return np.concatenate(outs, axis=0)
```
